# Optimizing a Trainium2 kernel written in Bass

```python
import jax, jax.numpy as jnp
from jax import lax
import numpy as np

D_MODEL = 1024
BATCH = 2
SEQ = 8192
DEPTH = 2

BRANCH_WIDTH = D_MODEL // 4
N_BRANCHES = 3
SB_HEADS = 4
SB_HEAD_DIM = BRANCH_WIDTH // SB_HEADS
SB_BLOCK = 128
HGRN_HEADS = 4
HGRN_HEAD_DIM = BRANCH_WIDTH // HGRN_HEADS
HGRN_EXP_CLIP = 60.0
GLA_HEADS = 4
GLA_VALUE_DIM = BRANCH_WIDTH // GLA_HEADS
GLA_KEY_DIM = GLA_VALUE_DIM // 2
GLA_KEY_WIDTH = GLA_HEADS * GLA_KEY_DIM
GLA_GATE_RANK = 16
GLA_TAU = 16.0
CHUNK = 64
NORM_EPS = 1e-5
DEEPNORM_ALPHA = (2 * DEPTH) ** 0.25
DEEPNORM_BETA = (8 * DEPTH) ** -0.25

IN_WIDTHS = (
    BRANCH_WIDTH, BRANCH_WIDTH, BRANCH_WIDTH, BRANCH_WIDTH,
    BRANCH_WIDTH, BRANCH_WIDTH, BRANCH_WIDTH, BRANCH_WIDTH,
    GLA_KEY_WIDTH, GLA_KEY_WIDTH, BRANCH_WIDTH, BRANCH_WIDTH,
    GLA_GATE_RANK,
    D_MODEL, D_MODEL, D_MODEL,
)
IN_SPLITS = tuple(int(s) for s in np.cumsum(IN_WIDTHS)[:-1])
IN_TOTAL = int(sum(IN_WIDTHS))

kernel_name = "hybrid_stickbreak_hgrn2_gla_deepnorm"


def split_heads(a, n_heads):
    b, t, w = a.shape
    return a.reshape(b, t, n_heads, w // n_heads).transpose(0, 2, 1, 3)


def merge_heads(a):
    b, h, t, d = a.shape
    return a.transpose(0, 2, 1, 3).reshape(b, t, h * d)


def masked_exp(mask, log_val):
    return jnp.where(mask, jnp.exp(jnp.where(mask, log_val, 0.0)), 0.0)


def head_rmsnorm(o, gain):
    o = o * lax.rsqrt(jnp.mean(o * o, axis=-1, keepdims=True) + NORM_EPS)
    return merge_heads(o) * gain.astype(jnp.float32)


def layer_norm(x, g, b):
    xf = x.astype(jnp.float32)
    mu = jnp.mean(xf, axis=-1, keepdims=True)
    var = jnp.mean(jnp.square(xf - mu), axis=-1, keepdims=True)
    y = (xf - mu) * lax.rsqrt(var + NORM_EPS) * g.astype(jnp.float32) + b.astype(jnp.float32)
    return y.astype(x.dtype)


def stick_breaking_attention(q, k, v):
    t_len, d = q.shape[2], q.shape[3]
    scale = d ** -0.5
    outs = []
    for blk in range(t_len // SB_BLOCK):
        t0 = blk * SB_BLOCK
        t1 = t0 + SB_BLOCK
        qb, kb, vb = q[:, :, t0:t1], k[:, :, :t1], v[:, :, :t1]
        z = jnp.einsum('bhtd,bhsd->bhts', qb, kb).astype(jnp.float32) * scale
        t_idx = t0 + jnp.arange(SB_BLOCK)[:, None]
        s_idx = jnp.arange(t1)[None, :]
        causal = s_idx < t_idx
        log_beta = jax.nn.log_sigmoid(z)
        log_one_minus = jnp.where(causal, jax.nn.log_sigmoid(-z), 0.0)
        tail = lax.cumsum(log_one_minus, axis=3, reverse=True) - log_one_minus
        weights = masked_exp(causal, log_beta + tail)
        outs.append(jnp.einsum('bhts,bhsd->bhtd', weights, vb.astype(jnp.float32)))
    return jnp.concatenate(outs, axis=2)


def chunked_gated_linear_recurrence(q, k, v, log_f):
    b, h, t_len, dk = q.shape
    dv = v.shape[-1]
    n_chunks = t_len // CHUNK

    def to_chunks(a):
        a = a.astype(jnp.float32)
        return a.reshape(b, h, n_chunks, CHUNK, a.shape[-1]).transpose(2, 0, 1, 3, 4)

    qc, kc, vc, gc = to_chunks(q), to_chunks(k), to_chunks(v), to_chunks(log_f)
    causal = jnp.tril(jnp.ones((CHUNK, CHUNK), dtype=bool))[:, :, None]

    def step(state, inp):
        qi, ki, vi, gi = inp
        cum = jnp.cumsum(gi, axis=2)
        o_inter = jnp.einsum('bhck,bhkv->bhcv', qi * jnp.exp(cum), state)
        diff = cum[:, :, :, None, :] - cum[:, :, None, :, :]
        decay = masked_exp(causal, diff)
        scores = jnp.einsum('bhtsk,bhsk->bhts', qi[:, :, :, None, :] * decay, ki)
        o = o_inter + jnp.einsum('bhts,bhsv->bhtv', scores, vi)
        last = cum[:, :, -1:, :]
        state = jnp.exp(last[:, :, 0, :])[..., None] * state + jnp.einsum(
            'bhsk,bhsv->bhkv', ki * jnp.exp(last - cum), vi)
        return state, o

    state0 = jnp.zeros((b, h, dk, dv), jnp.float32)
    _, o = lax.scan(step, state0, (qc, kc, vc, gc))
    return o.transpose(1, 2, 0, 3, 4).reshape(b, h, t_len, dv)


def hybrid_layer(x, w_in, gla_gate_w2, gla_gate_b, hgrn_lb, hgrn_norm_g, gla_norm_g,
                 w_up, w_out, ln_g, ln_b):
    proj = jnp.einsum('btd,dc->btc', x, w_in)
    (a_q, a_k, a_v, a_g,
     h_f, h_i, h_q, h_g,
     c_q, c_k, c_v, c_g, c_r,
     gate_logits_a, gate_logits_b, gate_logits_c) = jnp.split(proj, IN_SPLITS, axis=-1)

    a_out = stick_breaking_attention(split_heads(a_q, SB_HEADS), split_heads(a_k, SB_HEADS),
                                     split_heads(a_v, SB_HEADS))
    y_a = merge_heads(a_out) * jax.nn.silu(a_g.astype(jnp.float32))

    lb = hgrn_lb.astype(jnp.float32).reshape(HGRN_HEADS, 1, HGRN_HEAD_DIM)
    zf = split_heads(h_f, HGRN_HEADS).astype(jnp.float32)
    log_f = jax.nn.log_sigmoid(zf) + jnp.log1p(lb * jnp.exp(jnp.minimum(-zf, HGRN_EXP_CLIP)))
    h_key = (1.0 - lb) * jax.nn.sigmoid(-zf)
    h_out = chunked_gated_linear_recurrence(split_heads(h_q, HGRN_HEADS), h_key,
                                            split_heads(h_i, HGRN_HEADS), log_f)
    y_b = head_rmsnorm(h_out, hgrn_norm_g) * jax.nn.silu(h_g.astype(jnp.float32))

    gate_pre = jnp.einsum('btr,rk->btk', c_r, gla_gate_w2) + gla_gate_b
    c_log_f = jax.nn.log_sigmoid(gate_pre.astype(jnp.float32)) / GLA_TAU
    c_out = chunked_gated_linear_recurrence(
        split_heads(c_q, GLA_HEADS).astype(jnp.float32) * (GLA_KEY_DIM ** -0.5),
        split_heads(c_k, GLA_HEADS), split_heads(c_v, GLA_HEADS),
        split_heads(c_log_f, GLA_HEADS))
    y_c = head_rmsnorm(c_out, gla_norm_g) * jax.nn.silu(c_g.astype(jnp.float32))

    branches = jnp.stack([y_a, y_b, y_c], axis=0).astype(x.dtype)
    up = jnp.einsum('nbtw,nwd->nbtd', branches, w_up)
    gates = jax.nn.sigmoid(jnp.stack([gate_logits_a, gate_logits_b, gate_logits_c], axis=0))
    merged = jnp.sum(gates * up, axis=0)
    out = jnp.einsum('btd,de->bte', merged, w_out)

    return layer_norm(DEEPNORM_ALPHA * x + out, ln_g, ln_b)


def setup_inputs(seed: int = 0) -> dict:
    key = jax.random.key(seed)
    ks = jax.random.split(key, 11)
    f32 = jnp.float32
    x = jax.random.normal(ks[0], (BATCH, SEQ, D_MODEL), f32)
    w_in = jax.random.normal(ks[1], (DEPTH, D_MODEL, IN_TOTAL), f32) * D_MODEL ** -0.5
    gla_gate_w2 = jax.random.normal(ks[2], (DEPTH, GLA_GATE_RANK, GLA_KEY_WIDTH), f32) * GLA_GATE_RANK ** -0.5
    gla_gate_b = 0.1 * jax.random.normal(ks[3], (DEPTH, GLA_KEY_WIDTH), f32)
    hgrn_lb_logits = 0.5 * jax.random.normal(ks[4], (DEPTH, BRANCH_WIDTH), f32)
    hgrn_norm_g = 1.0 + 0.02 * jax.random.normal(ks[5], (DEPTH, BRANCH_WIDTH), f32)
    gla_norm_g = 1.0 + 0.02 * jax.random.normal(ks[6], (DEPTH, BRANCH_WIDTH), f32)
    w_up = jax.random.normal(ks[7], (DEPTH, N_BRANCHES, BRANCH_WIDTH, D_MODEL), f32) * (
        BRANCH_WIDTH ** -0.5 * DEEPNORM_BETA)
    w_out = jax.random.normal(ks[8], (DEPTH, D_MODEL, D_MODEL), f32) * (D_MODEL ** -0.5 * DEEPNORM_BETA)
    ln_g = 1.0 + 0.02 * jax.random.normal(ks[9], (DEPTH, D_MODEL), f32)
    ln_b = 0.02 * jax.random.normal(ks[10], (DEPTH, D_MODEL), f32)
    return {"x": x, "w_in": w_in, "gla_gate_w2": gla_gate_w2, "gla_gate_b": gla_gate_b,
            "hgrn_lb_logits": hgrn_lb_logits, "hgrn_norm_g": hgrn_norm_g, "gla_norm_g": gla_norm_g,
            "w_up": w_up, "w_out": w_out, "ln_g": ln_g, "ln_b": ln_b}


def reference(x, w_in, gla_gate_w2, gla_gate_b, hgrn_lb_logits, hgrn_norm_g, gla_norm_g,
              w_up, w_out, ln_g, ln_b):
    lb_soft = jax.nn.softmax(hgrn_lb_logits.astype(jnp.float32), axis=0)
    lower_bounds = jnp.cumsum(lb_soft, axis=0) - lb_soft[0:1]
    for layer in range(DEPTH):
        x = hybrid_layer(x, w_in[layer], gla_gate_w2[layer], gla_gate_b[layer],
                         lower_bounds[layer], hgrn_norm_g[layer], gla_norm_g[layer],
                         w_up[layer], w_out[layer], ln_g[layer], ln_b[layer])
    return x
```

```python
import contextlib
import numpy as np
import concourse.bass as bass
import concourse.mybir as mybir
from concourse.bass_utils import run_bass_kernel_spmd

F32 = mybir.dt.float32
BF16 = mybir.dt.bfloat16
AF = mybir.ActivationFunctionType
ALU = mybir.AluOpType

_STOP = ""
_KNT = 16
_KB = 9
T = 8192
D = 1024
NCORE = 8
TT = 2048
EPS = 1e-5
ALPHA = 4.0 ** 0.25


class Buf:
    __slots__ = ("name", "w", "r", "dsem", "dcnt")

    def __init__(self, name):
        self.name = name
        self.w = None
        self.r = {}
        self.dsem = None
        self.dcnt = 0


class Tr:
    NDP = 18

    def __init__(self, nc, es):
        self.nc = nc
        self.es = es
        self.engs = {"pe": nc.tensor, "act": nc.scalar, "dve": nc.vector,
                     "pool": nc.gpsimd, "sp": nc.sync}
        self.sem = {k: es.enter_context(nc.semaphore("sem_" + k)) for k in self.engs}
        self.cnt = {k: 0 for k in self.engs}
        self.waited = {k: {} for k in self.engs}
        self.dpool = [[es.enter_context(nc.semaphore(f"dsem{i}")), 0] for i in range(self.NDP)]
        self.free = list(range(self.NDP))
        self.csem = es.enter_context(nc.semaphore("csem"))
        self.ccnt = 0
        self.owners = []
        self.all_bufs = []
        self.nbuf = 0

    def buf(self, name="b"):
        self.nbuf += 1
        b = Buf(f"{name}{self.nbuf}")
        self.all_bufs.append(b)
        return b

    def bufs(self, n, name="b"):
        return [self.buf(name) for _ in range(n)]

    def _waits(self, e, reads, writes):
        need = {}

        def add(tag):
            key, sem, val = tag
            if key not in need or need[key][1] < val:
                need[key] = (sem, val)

        for b in reads:
            if b.w is not None:
                add(b.w)
        for b in writes:
            if b.w is not None:
                add(b.w)
            for tag in b.r.values():
                add(tag)
        for key, (sem, val) in need.items():
            if key == "pe" and e == "pe":
                continue
            if self.waited[e].get(key, 0) >= val:
                continue
            self.engs[e].wait_ge(sem, val)
            self.waited[e][key] = val

    def _mark(self, tag, reads, writes):
        for b in reads:
            b.r[tag[0]] = tag
        for b in writes:
            b.w = tag
            b.r = {}

    def op(self, e, fn, reads=(), writes=(), serial=False):
        self._waits(e, reads, writes)
        if serial and self.cnt[e] > 0 and self.waited[e].get("self", 0) < self.cnt[e]:
            self.engs[e].wait_ge(self.sem[e], self.cnt[e])
            self.waited[e]["self"] = self.cnt[e]
        inst = fn(self.engs[e])
        self.cnt[e] += 1
        inst.then_inc(self.sem[e], 1)
        self._mark((e, self.sem[e], self.cnt[e]), reads, writes)

    def dma(self, q, out, in_, owner, reads=(), writes=()):
        self._waits(q, reads, writes)
        if owner.dsem is None:
            owner.dsem = self.free.pop(0)
            self.owners.append(owner)
        slot = self.dpool[owner.dsem]
        self.engs[q].dma_start(out=out, in_=in_).then_inc(slot[0], 16)
        slot[1] += 16
        self._mark((f"p{owner.dsem}", slot[0], slot[1]), reads, writes)

    def retag(self, bufs, owner):
        slot = self.dpool[owner.dsem]
        for b in bufs:
            b.w = (f"p{owner.dsem}", slot[0], slot[1])

    def coll(self, fn, reads=(), writes=()):
        self._waits("pool", reads, writes)
        fn(self.engs["pool"]).then_inc(self.csem)
        self.ccnt += 1
        self._mark(("coll", self.csem, self.ccnt), reads, writes)

    def _sync_all(self, e):
        for e2 in self.engs:
            if e2 == e or self.cnt[e2] == 0:
                continue
            if self.waited[e].get(e2, 0) >= self.cnt[e2]:
                continue
            self.engs[e].wait_ge(self.sem[e2], self.cnt[e2])
            self.waited[e][e2] = self.cnt[e2]
        for i, (sem, cnt) in enumerate(self.dpool):
            if cnt == 0 or self.waited[e].get(f"p{i}", 0) >= cnt:
                continue
            self.engs[e].wait_ge(sem, cnt)
            self.waited[e][f"p{i}"] = cnt
        if self.ccnt and self.waited[e].get("coll", 0) < self.ccnt:
            self.engs[e].wait_ge(self.csem, self.ccnt)
            self.waited[e]["coll"] = self.ccnt

    def barrier(self):
        for e in self.engs:
            self._sync_all(e)
        for o in self.owners:
            self.free.append(o.dsem)
            o.dsem = None
        self.owners = []
        self.free.sort()
        for b in self.all_bufs:
            b.w = None
            b.r = {}

    def finish(self):
        self._sync_all("sp")


def sb(nc, es, name, shape, dt):
    return es.enter_context(nc.sbuf_tensor(name, shape, dt))


def emit_H(nc, tr, es0, psb, layer, xsrc, wfm, wtm, pp, w2, cst, ysink, pfx):
    op, dma = tr.op, tr.dma
    NB, NT, NCH = T // 128, T // 512, T // 64

    es_c = contextlib.ExitStack()
    es0.enter_context(es_c)
    ident = sb(nc, es_c, pfx + "ident", [128, 128], F32)
    identb = sb(nc, es_c, pfx + "identb", [128, 128], BF16)
    cmask = sb(nc, es_c, pfx + "cmask", [128, 64], F32)
    m01 = sb(nc, es_c, pfx + "m01", [128, 128], F32)
    rmask = sb(nc, es_c, pfx + "rmask", [128, 512], F32)
    onesblk = sb(nc, es_c, pfx + "onesblk", [128, 128], F32)
    zeros = sb(nc, es_c, pfx + "zeros", [128, 512], F32)
    ppt = sb(nc, es_c, pfx + "ppt", [128, 8], F32)
    pp2 = sb(nc, es_c, pfx + "pp2", [128, 8], F32)
    w2f = sb(nc, es_c, pfx + "w2f", [128, 32], F32)
    w2b = sb(nc, es_c, pfx + "w2b", [128, 32], BF16)
    b_c = tr.buf("const")
    for tl, nm in ((ident, "ident"), (cmask, "cmask"), (m01, "m01"), (rmask, "rmask"),
                   (onesblk, "onesblk")):
        dma("sp", tl[:], cst[nm], b_c, writes=[b_c])
    dma("sp", ppt[:], pp, b_c, writes=[b_c])
    dma("sp", w2f[:], w2, b_c, writes=[b_c])
    b_c2 = tr.buf("const2")
    op("pool", lambda e: e.memset(zeros[:], 0.0), writes=[b_c2])
    op("pool", lambda e: e.memset(pp2[:], 0.0), writes=[b_c2])
    op("pool", lambda e: e.memset(pp2[:, 5:6], EPS), reads=[b_c2], writes=[b_c2])
    op("pool", lambda e: e.memset(pp2[:, 6:7], 1.0), reads=[b_c2], writes=[b_c2])
    op("dve", lambda e: e.tensor_copy(out=identb[:], in_=ident[:]), reads=[b_c], writes=[b_c2])
    op("dve", lambda e: e.tensor_copy(out=w2b[:], in_=w2f[:]), reads=[b_c], writes=[b_c2])
    if layer == 0:
        op("pool", lambda e: e.memset(pp2[:, 1:2], 1.0), reads=[b_c2], writes=[b_c2])
    else:
        op("dve", lambda e: e.tensor_tensor(out=pp2[:, 3:4], in0=ppt[:, 0:1], in1=ppt[:, 1:2],
                                            op=ALU.subtract), reads=[b_c, b_c2], writes=[b_c2])
        op("act", lambda e: e.activation(out=pp2[:, 4:5], in_=pp2[:, 3:4], func=AF.Exp),
           reads=[b_c2], writes=[b_c2])
        op("dve", lambda e: e.tensor_scalar(out=pp2[:, 4:5], in0=pp2[:, 4:5], scalar1=1.0,
                                            scalar2=None, op0=ALU.add), reads=[b_c2], writes=[b_c2])
        op("dve", lambda e: e.reciprocal(out=pp2[:, 0:1], in_=pp2[:, 4:5]), reads=[b_c2], writes=[b_c2])
        op("dve", lambda e: e.tensor_scalar(out=pp2[:, 1:2], in0=pp2[:, 0:1], scalar1=-1.0,
                                            scalar2=1.0, op0=ALU.mult, op1=ALU.add),
           reads=[b_c2], writes=[b_c2])
    op("dve", lambda e: e.tensor_scalar(out=pp2[:, 2:3], in0=ppt[:, 2:3], scalar1=-1.0,
                                        scalar2=None, op0=ALU.mult), reads=[b_c, b_c2], writes=[b_c2])
    CONST = [b_c, b_c2]


    S0res = sb(nc, es_c, pfx + "S0res", [128, T], BF16)
    S1res = sb(nc, es_c, pfx + "S1res", [128, T], BF16)
    vsb = sb(nc, es_c, pfx + "vsb", [128, NB, 64], BF16)
    b_S0q = tr.bufs(NT, "S0q")
    b_S0g = tr.bufs(NT, "S0g")
    b_S1 = tr.bufs(NT, "S1")
    b_vsb = tr.bufs(NB, "vsb")

    es_r = contextlib.ExitStack()
    vrec = sb(nc, es_r, pfx + "vrec", [128, NB, 128], BF16)
    qdec = sb(nc, es_r, pfx + "qdec", [128, T], BF16)
    kdec = sb(nc, es_r, pfx + "kdec", [128, T], BF16)
    kdsT = sb(nc, es_r, pfx + "kdsT", [128, NB, 96], BF16)
    RGres = sb(nc, es_r, pfx + "RGres", [128, T], BF16)
    U = sb(nc, es_r, pfx + "U", [128, T], F32)
    Dres = sb(nc, es_r, pfx + "Dres", [128, NCH], F32)
    b_vrec = tr.bufs(NB, "vrec")
    b_qdec = tr.bufs(NT, "qdec")
    b_kdec = tr.bufs(NT, "kdec")
    b_kdsT = tr.bufs(NT, "kdsT")
    b_RG = tr.bufs(NT, "RG")
    b_U = tr.buf("U")
    b_D = tr.buf("D")

    es_a = contextlib.ExitStack()
    xst = [sb(nc, es_a, pfx + f"xst{i}", [128, D], F32) for i in range(2)]
    b_xst = tr.bufs(2, "xst")
    xbf = xsrc[0] == "bf16"
    xstb = [xt_[:].bitcast(BF16) for xt_ in xst]
    xT = sb(nc, es_a, pfx + "xT", [128, 8, 512], BF16)
    b_xT = [tr.bufs(2, "xT") for _ in range(4)]
    wfmb = sb(nc, es_a, pfx + "wfmb", [128, 8, 640], BF16)
    wtmb = sb(nc, es_a, pfx + "wtmb", [128, 8, 192], BF16)
    b_w = tr.buf("w")
    NTMP = 9
    tmp = [sb(nc, es_a, pfx + f"tmp{i}", [128, 512], F32) for i in range(NTMP)]
    bh = tr.bufs(NTMP, "tmph")
    bc = tr.bufs(NTMP, "tmpc")
    kds = sb(nc, es_a, pfx + "kds", [128, 512], BF16)
    b_kds = tr.buf("kds")
    (t_sig, t_f, t_g, t_kk, t_q, t_cum, t_e1, t_e2, t_d) = range(NTMP)

    wfm_v = wfm.rearrange("(j p) c -> j p c", p=128)
    wtm_v = wtm.rearrange("(j p) c -> j p c", p=128)
    for j in range(8):
        s = j % 2
        dma("sp", xst[s][:, 0:640], wfm_v[j], b_xst[s], writes=[b_xst[s]])
        dma("sp", xst[s][:, 640:832], wtm_v[j], b_xst[s], writes=[b_xst[s]])
        if j % 2 == 0:
            op("dve", lambda e, j=j, s=s: e.tensor_copy(out=wfmb[:, j, :], in_=xst[s][:, 0:640]),
               reads=[b_xst[s]], writes=[b_w])
        else:
            op("act", lambda e, j=j, s=s: e.activation(out=wfmb[:, j, :], in_=xst[s][:, 0:640], func=AF.Copy),
               reads=[b_xst[s]], writes=[b_w])
        op("pool", lambda e, j=j, s=s: e.tensor_copy(out=wtmb[:, j, :], in_=xst[s][:, 640:832]),
           reads=[b_xst[s]], writes=[b_w])

    tp = [psb[0], psb[1]]
    fm = [psb[2], psb[3]]
    tmq = [psb[4], psb[5]]
    ktp = psb[6]
    ups = psb[7]
    ktp_bf = ktp[0][:].bitcast(BF16)

    xs_v = None if xbf else xsrc[1].rearrange("(n p) d -> n p d", p=128)
    tp_bf = [psb[0][0][:].bitcast(BF16), psb[1][0][:].bitcast(BF16)]
    evac_rr = [0]

    def evac_copy(out, in_, reads, writes):
        evac_rr[0] ^= 1
        if evac_rr[0]:
            op("act", lambda e: e.activation(out=out, in_=in_, func=AF.Copy), reads=reads, writes=writes)
        else:
            op("dve", lambda e: e.tensor_copy(out=out, in_=in_), reads=reads, writes=writes)

    fmi = [0]

    def fm_chunk(ci):
        bank = fm[fmi[0] % 2]
        fmi[0] += 1
        for j in range(8):
            op("pe", lambda e, j=j, bank=bank: e.matmul(bank[0][:, :], lhsT=wfmb[:, j, ci * 128:(ci + 1) * 128],
                                                        rhs=xT[:, j, :], start=(j == 0), stop=(j == 7)),
               reads=[b_w] + [b_xT[bi][j // 4] for bi in range(4)], writes=[bank[1]])
        return bank

    for ti in range(min(NT, _KNT)):
        cols = slice(ti * 512, (ti + 1) * 512)
        for bi in range(4):
            blk = ti * 4 + bi
            s = blk % 2
            if xbf:
                for hf, src_ap in enumerate(xsrc[1](blk)):
                    dma("sp", xstb[s][:, hf * 512:(hf + 1) * 512], src_ap, b_xst[s], reads=xsrc[2](blk),
                        writes=[b_xst[s]])
            else:
                dma("sp", xst[s][:], xs_v[blk], b_xst[s], writes=[b_xst[s]])
            for half in range(2):
                bank = tp[half]
                for jj in range(4):
                    j = half * 4 + jj
                    if xbf:
                        op("pe", lambda e: e.transpose(
                            tp_bf[half][:, jj * 128:(jj + 1) * 128], xstb[s][:, j * 128:(j + 1) * 128], identb[:]),
                           reads=[b_xst[s]] + CONST, writes=[bank[1]])
                    else:
                        op("pe", lambda e: e.transpose(
                            bank[0][:, jj * 128:(jj + 1) * 128], xst[s][:, j * 128:(j + 1) * 128], ident[:]),
                           reads=[b_xst[s]] + CONST, writes=[bank[1]])
                src_ = tp_bf[half][:, 0:512] if xbf else bank[0][:, :]
                evac_copy(xT[:, half * 4:(half + 1) * 4, bi * 128:(bi + 1) * 128],
                          src_.rearrange("p (j t) -> p j t", t=128),
                          [bank[1]], [b_xT[bi][half]])
            tmb = tmq[bi % 2]
            for j in range(8):
                op("pe", lambda e: e.matmul(
                    tmb[0][:, 0:192], lhsT=xT[:, j, bi * 128:(bi + 1) * 128], rhs=wtmb[:, j, :],
                    start=(j == 0), stop=(j == 7)),
                   reads=[b_w, b_xT[bi][j // 4]], writes=[tmb[1]])
            if bi % 2 == 0:
                op("act", lambda e: e.activation(out=vsb[:, blk, :], in_=tmb[0][:, 0:64], func=AF.Copy),
                   reads=[tmb[1]], writes=[b_vsb[blk]])
                op("act", lambda e: e.activation(out=vrec[:, blk, :], in_=tmb[0][:, 64:192], func=AF.Copy),
                   reads=[tmb[1]], writes=[b_vrec[blk]])
            else:
                op("dve", lambda e: e.tensor_copy(out=vsb[:, blk, :], in_=tmb[0][:, 0:64]),
                   reads=[tmb[1]], writes=[b_vsb[blk]])
                op("dve", lambda e: e.tensor_copy(out=vrec[:, blk, :], in_=tmb[0][:, 64:192]),
                   reads=[tmb[1]], writes=[b_vrec[blk]])

        bank = fm_chunk(0)
        op("dve", lambda e, bank=bank: e.tensor_copy(out=S0res[0:64, cols], in_=bank[0][0:64, :]),
           reads=[bank[1]], writes=[b_S0q[ti]])
        op("act", lambda e, bank=bank: e.activation(out=S0res[64:128, cols], in_=bank[0][64:128, :], func=AF.Silu),
           reads=[bank[1]], writes=[b_S0g[ti]])
        bank = fm_chunk(3)
        op("act", lambda e, bank=bank: e.activation(out=tmp[t_sig][0:64, :], in_=bank[0][0:64, :], func=AF.Sigmoid),
           reads=[bank[1]], writes=[bh[t_sig]])
        op("act", lambda e, bank=bank: e.activation(out=tmp[t_kk][64:96, :], in_=bank[0][64:96, :], func=AF.Copy),
           reads=[bank[1]], writes=[bc[t_kk]])
        op("dve", lambda e: e.tensor_scalar(out=tmp[t_f][0:64, :], in0=tmp[t_sig][0:64, :],
                                            scalar1=pp2[0:64, 1:2], scalar2=pp2[0:64, 0:1],
                                            op0=ALU.mult, op1=ALU.add),
           reads=[bh[t_sig]] + CONST, writes=[bh[t_f]])
        bank = fm_chunk(4)
        op("act", lambda e, bank=bank: e.activation(out=RGres[:, cols], in_=bank[0][:, :], func=AF.Silu),
           reads=[bank[1]], writes=[b_RG[ti]])
        bank = fm_chunk(1)
        op("dve", lambda e, bank=bank: e.tensor_copy(out=S1res[0:96, cols], in_=bank[0][0:96, :]),
           reads=[bank[1]], writes=[b_S1[ti]])
        gp = fm[fmi[0] % 2]
        fmi[0] += 1
        op("pe", lambda e: e.matmul(gp[0][64:96, :], lhsT=w2b[64:80, :], rhs=S1res[64:80, cols],
                                    start=True, stop=True),
           reads=[b_S1[ti]] + CONST, writes=[gp[1]])
        op("act", lambda e: e.activation(out=tmp[t_e1][64:96, :], in_=gp[0][64:96, :], func=AF.Exp,
                                         scale=-1.0, bias=pp2[64:96, 2:3]),
           reads=[gp[1]] + CONST, writes=[bc[t_e1]])
        op("act", lambda e: e.activation(out=tmp[t_e2][64:96, :], in_=tmp[t_e1][64:96, :], func=AF.Ln,
                                         bias=pp2[64:96, 6:7]),
           reads=[bc[t_e1]] + CONST, writes=[bc[t_e2]])
        op("act", lambda e: e.activation(out=tmp[t_g][64:96, :], in_=tmp[t_e2][64:96, :], func=AF.Copy,
                                         scale=-1.0 / 16.0),
           reads=[bc[t_e2]], writes=[bc[t_g]])
        op("act", lambda e: e.activation(out=tmp[t_g][0:64, :], in_=tmp[t_f][0:64, :], func=AF.Ln),
           reads=[bh[t_f]], writes=[bh[t_g]])
        op("pool", lambda e: e.tensor_scalar(out=tmp[t_kk][0:64, :], in0=tmp[t_f][0:64, :],
                                             scalar1=-1.0, scalar2=1.0, op0=ALU.mult, op1=ALU.add),
           reads=[bh[t_f]], writes=[bh[t_kk]])
        bank = fm_chunk(2)
        op("act", lambda e, bank=bank: e.activation(out=tmp[t_q][0:96, :], in_=bank[0][0:96, :], func=AF.Identity,
                                                    scale=ppt[0:96, 4:5]),
           reads=[bank[1]] + CONST, writes=[bh[t_q], bc[t_q]])
        op("dve", lambda e: e.tensor_tensor_scan(out=tmp[t_cum][0:96, :], data0=rmask[0:96, :],
                                                 data1=tmp[t_g][0:96, :], initial=0.0,
                                                 op0=ALU.mult, op1=ALU.add),
           reads=[bh[t_g], bc[t_g]] + CONST, writes=[bh[t_cum], bc[t_cum]])
        op("act", lambda e: e.activation(out=tmp[t_e1][0:96, :], in_=tmp[t_cum][0:96, :], func=AF.Exp),
           reads=[bh[t_cum], bc[t_cum]], writes=[bh[t_e1], bc[t_e1]])
        op("pool", lambda e: e.tensor_tensor(out=qdec[0:96, cols], in0=tmp[t_q][0:96, :],
                                             in1=tmp[t_e1][0:96, :], op=ALU.mult),
           reads=[bh[t_q], bc[t_q], bh[t_e1], bc[t_e1]], writes=[b_qdec[ti]])
        op("act", lambda e: e.activation(out=tmp[t_e2][0:96, :], in_=tmp[t_cum][0:96, :], func=AF.Exp,
                                         scale=-1.0),
           reads=[bh[t_cum], bc[t_cum]], writes=[bh[t_e2], bc[t_e2]])
        op("dve", lambda e: e.tensor_tensor(out=kdec[0:96, cols], in0=tmp[t_kk][0:96, :],
                                            in1=tmp[t_e2][0:96, :], op=ALU.mult),
           reads=[bh[t_kk], bc[t_kk], bh[t_e2], bc[t_e2]], writes=[b_kdec[ti]])
        cum3 = tmp[t_cum][0:96, :].rearrange("p (c t) -> p c t", t=64)
        d3 = tmp[t_d][0:96, :].rearrange("p (c t) -> p c t", t=64)
        op("pool", lambda e: e.tensor_tensor(out=d3, in0=cum3[:, :, 63:64].to_broadcast([96, 8, 64]),
                                             in1=cum3, op=ALU.subtract),
           reads=[bh[t_cum], bc[t_cum]], writes=[bh[t_d], bc[t_d]])
        op("act", lambda e: e.activation(out=tmp[t_d][0:96, :], in_=tmp[t_d][0:96, :], func=AF.Exp),
           reads=[bh[t_d], bc[t_d]], writes=[bh[t_d], bc[t_d]])
        op("pool", lambda e: e.tensor_tensor(out=kds[0:96, :], in0=tmp[t_kk][0:96, :],
                                             in1=tmp[t_d][0:96, :], op=ALU.mult),
           reads=[bh[t_kk], bc[t_kk], bh[t_d], bc[t_d]], writes=[b_kds])
        op("act", lambda e: e.activation(out=Dres[0:96, ti * 8:(ti + 1) * 8],
                                         in_=tmp[t_cum][0:96, 63:512:64], func=AF.Exp),
           reads=[bh[t_cum], bc[t_cum]], writes=[b_D])
        for bi in range(4):
            op("pe", lambda e, bi=bi: e.transpose(ktp_bf[:, bi * 96:(bi + 1) * 96],
                                                  kds[0:96, bi * 128:(bi + 1) * 128], identb[0:96, 0:96]),
               reads=[b_kds] + CONST, writes=[ktp[1]])
        evac_copy(kdsT[:, ti * 4:(ti + 1) * 4, :], ktp_bf[:, 0:384].rearrange("p (b c) -> p b c", c=96),
                  [ktp[1]], [b_kdsT[ti]])
        for cc in range(8):
            c = ti * 8 + cc
            blk, par = c // 2, c % 2
            rows = slice(64 * par, 64 * par + 64)
            pc = slice(cc * 64, (cc + 1) * 64)
            op("pe", lambda e, blk=blk, rows=rows, pc=pc: e.matmul(
                ups[0][0:64, pc], lhsT=kdsT[rows, blk, 0:64], rhs=vrec[rows, blk, 0:64], start=True, stop=True),
               reads=[b_kdsT[ti], b_vrec[blk]], writes=[ups[1]], serial=True)
            op("pe", lambda e, blk=blk, rows=rows, pc=pc: e.matmul(
                ups[0][64:96, pc], lhsT=kdsT[rows, blk, 64:96], rhs=vrec[rows, blk, 64:128], start=True, stop=True),
               reads=[b_kdsT[ti], b_vrec[blk]], writes=[ups[1]], serial=True)
        op("dve", lambda e: e.tensor_copy(
            out=U[0:96, :].rearrange("p (v c) -> p v c", c=128)[:, :, ti * 8:(ti + 1) * 8],
            in_=ups[0][0:96, :].rearrange("p (c v) -> p v c", v=64)),
           reads=[ups[1]], writes=[b_U])

    tr.barrier()
    es_a.close()
    if _STOP == "A":
        es_r.close()
        es_c.close()
        return
    es_b = contextlib.ExitStack()
    Sprev = sb(nc, es_b, pfx + "Sprev", [128, NCH * 64], BF16)
    b_Sp = tr.buf("Sprev")
    scm = [[sb(nc, es_b, pfx + f"scm{h}{i}", [128, 512], BF16) for i in range(2)] for h in range(2)]
    b_scm = [tr.bufs(2, "scm") for _ in range(2)]
    osb = [sb(nc, es_b, pfx + f"osb{i}", [128, 512], F32) for i in range(2)]
    b_osb = tr.bufs(2, "osb")
    sq = [sb(nc, es_b, pfx + f"sq{i}", [128, 512], F32) for i in range(2)]
    b_sq = tr.bufs(2, "sq")
    rstd = [sb(nc, es_b, pfx + f"rstd{i}", [128, 512], F32) for i in range(2)]
    b_rstd = tr.bufs(2, "rstd")
    yt = [sb(nc, es_b, pfx + f"yt{i}", [128, 512], BF16) for i in range(2)]
    b_yt = tr.bufs(2, "yt")

    op("pool", lambda e: e.memset(Sprev[:, 0:64], 0.0), writes=[b_Sp])
    Sp3 = Sprev[0:96, :].rearrange("p (c v) -> p c v", v=64)
    for v in range(64):
        op("dve", lambda e, v=v: e.tensor_tensor_scan(
            out=Sp3[:, 1:128, v], data0=Dres[0:96, 0:127], data1=U[0:96, v * 128:v * 128 + 127],
            initial=0.0, op0=ALU.mult, op1=ALU.add),
           reads=[b_U, b_D], writes=[b_Sp])

    scb = [psb[0], psb[1]]
    opsb = [psb[2], psb[3]]
    msb = [psb[4], psb[5]]
    for ti in range(NT if _KB > 0 else 0):
        cols = slice(ti * 512, (ti + 1) * 512)
        i2 = ti % 2
        for cc in range(8):
            c = ti * 8 + cc
            blk, par = c // 2, c % 2
            rows = slice(64 * par, 64 * par + 64)
            pc = slice(cc * 64, (cc + 1) * 64)
            tc_ = slice(c * 64, (c + 1) * 64)
            op("pe", lambda e, rows=rows, pc=pc, tc_=tc_: e.matmul(
                scb[0][0][rows, pc], lhsT=kdec[0:64, tc_], rhs=qdec[0:64, tc_], start=True, stop=True),
               reads=[b_kdec[ti], b_qdec[ti]], writes=[scb[0][1]])
            op("pe", lambda e, rows=rows, pc=pc, tc_=tc_: e.matmul(
                scb[1][0][rows, pc], lhsT=kdec[64:96, tc_], rhs=qdec[64:96, tc_], start=True, stop=True),
               reads=[b_kdec[ti], b_qdec[ti]], writes=[scb[1][1]])
        for h in range(2):
            for par in range(2):
                rows = slice(64 * par, 64 * par + 64)
                src = scb[h][0][rows, :].rearrange("p (c two t) -> p c two t", two=2, t=64)[:, :, par, :]
                dst = scm[h][i2][rows, :].rearrange("p (c two t) -> p c two t", two=2, t=64)[:, :, par, :]
                msk = cmask[rows, :].unsqueeze(1).to_broadcast([64, 4, 64])
                op("dve", lambda e, src=src, dst=dst, msk=msk: e.tensor_tensor(out=dst, in0=src, in1=msk, op=ALU.mult),
                   reads=[scb[h][1]] + CONST, writes=[b_scm[h][i2]])
        if _KB < 2:
            continue
        ob = opsb[i2]
        for cc in range(8):
            c = ti * 8 + cc
            blk, par = c // 2, c % 2
            rows = slice(64 * par, 64 * par + 64)
            pc = slice(cc * 64, (cc + 1) * 64)
            tc_ = slice(c * 64, (c + 1) * 64)
            sc_ = slice(c * 64, (c + 1) * 64)
            op("pe", lambda e, pc=pc, tc_=tc_, sc_=sc_: e.matmul(
                ob[0][0:64, pc], lhsT=Sprev[0:64, sc_], rhs=qdec[0:64, tc_], start=True, stop=False),
               reads=[b_Sp, b_qdec[ti]], writes=[ob[1]], serial=True)
            op("pe", lambda e, pc=pc, rows=rows, blk=blk: e.matmul(
                ob[0][0:64, pc], lhsT=vrec[rows, blk, 0:64], rhs=scm[0][i2][rows, pc], start=False, stop=True),
               reads=[b_vrec[blk], b_scm[0][i2]], writes=[ob[1]], serial=True)
            op("pe", lambda e, pc=pc, tc_=tc_, sc_=sc_: e.matmul(
                ob[0][64:128, pc], lhsT=Sprev[64:96, sc_], rhs=qdec[64:96, tc_], start=True, stop=False),
               reads=[b_Sp, b_qdec[ti]], writes=[ob[1]], serial=True)
            op("pe", lambda e, pc=pc, rows=rows, blk=blk: e.matmul(
                ob[0][64:128, pc], lhsT=vrec[rows, blk, 64:128], rhs=scm[1][i2][rows, pc], start=False, stop=True),
               reads=[b_vrec[blk], b_scm[1][i2]], writes=[ob[1]], serial=True)
        if _KB < 3:
            continue
        op("act", lambda e: e.activation(out=osb[i2][:, :], in_=ob[0][:, :], func=AF.Copy),
           reads=[ob[1]], writes=[b_osb[i2]])
        op("act", lambda e: e.activation(out=sq[i2][:, :], in_=ob[0][:, :], func=AF.Square),
           reads=[ob[1]], writes=[b_sq[i2]])
        mb = msb[i2]
        op("pe", lambda e: e.matmul(mb[0][:, :], lhsT=onesblk[:, :], rhs=sq[i2][:, :], start=True, stop=True),
           reads=[b_sq[i2]] + CONST, writes=[mb[1]])
        op("act", lambda e: e.activation(out=rstd[i2][:, :], in_=mb[0][:, :], func=AF.Sqrt, bias=pp2[:, 5:6]),
           reads=[mb[1]] + CONST, writes=[b_rstd[i2]])
        op("dve", lambda e: e.reciprocal(out=rstd[i2][:, :], in_=rstd[i2][:, :]),
           reads=[b_rstd[i2]], writes=[b_rstd[i2]])
        op("pool", lambda e: e.tensor_tensor(out=osb[i2][:, :], in0=osb[i2][:, :], in1=rstd[i2][:, :], op=ALU.mult),
           reads=[b_osb[i2], b_rstd[i2]], writes=[b_osb[i2]])
        op("dve", lambda e: e.scalar_tensor_tensor(out=yt[i2][:, :], in0=osb[i2][:, :], scalar=ppt[:, 3:4],
                                                   in1=RGres[:, cols], op0=ALU.mult, op1=ALU.mult),
           reads=[b_osb[i2], b_RG[ti]] + CONST, writes=[b_yt[i2]])
        ysink["bc"](ti, yt[i2], b_yt[i2])

    tr.barrier()
    es_b.close()
    es_r.close()
    if _STOP == "B":
        es_c.close()
        return
    if "after_bc" in ysink:
        ysink["after_bc"]()
    es_cw = contextlib.ExitStack()
    NR = 4
    om = [sb(nc, es_cw, pfx + f"om{i}", [128, 512], F32) for i in range(NR)]
    b_om = tr.bufs(NR, "om")
    Pb = [sb(nc, es_cw, pfx + f"Pb{i}", [128, 512], F32) for i in range(NR)]
    b_Pb = tr.bufs(NR, "Pb")
    wbf = [sb(nc, es_cw, pfx + f"wbf{i}", [128, 512], BF16) for i in range(NR)]
    b_wbf = tr.bufs(NR, "wbf")
    wT = [sb(nc, es_cw, pfx + f"wT{i}", [128, 512], BF16) for i in range(NR)]
    b_wT = tr.bufs(NR, "wT")
    ya = [sb(nc, es_cw, pfx + f"ya{i}", [128, 512], BF16) for i in range(2)]
    b_ya = tr.bufs(2, "ya")
    zps = [psb[0], psb[1], psb[2], psb[7]]
    wtp = [psb[3], psb[4]]
    wtp_bf = [w_[0][:].bitcast(BF16) for w_ in wtp]
    acc = [psb[5], psb[6]]

    items = []
    for tb in range(NB):
        hi = tb * 128 + 128
        first = True
        while hi > 0:
            lo = max(0, hi - 512)
            items.append(dict(tb=tb, lo=lo, hi=hi, W=hi - lo, first=first, last=(lo == 0)))
            hi = lo
            first = False
    n = len(items)

    def stA(g):
        it = items[g]
        s3 = g % NR
        tb, lo, hi, W = it["tb"], it["lo"], it["hi"], it["W"]
        t0 = tb * 128
        op("pe", lambda e: e.matmul(zps[s3][0][:, 0:W], lhsT=S0res[0:64, t0:t0 + 128], rhs=S1res[0:64, lo:hi],
                                    start=True, stop=True),
           reads=[b_S0q[tb // 4]] + [b_S1[k] for k in range(lo // 512, (hi - 1) // 512 + 1)],
           writes=[zps[s3][1]])

    def stB(g):
        it = items[g]
        s3 = g % NR
        W = it["W"]
        if it["first"]:
            cin, cin_deps = pp2[:, 6:7], CONST
        else:
            sp_ = (g - 1) % NR
            cin, cin_deps = Pb[sp_][:, 0:1], [b_Pb[sp_]]
        op("act", lambda e: e.activation(out=om[s3][:, 0:W], in_=zps[s3][0][:, 0:W], func=AF.Sigmoid, scale=-0.125),
           reads=[zps[s3][1]], writes=[b_om[s3]])
        if it["first"]:
            op("dve", lambda e: e.tensor_tensor(out=om[s3][:, W - 128:W], in0=om[s3][:, W - 128:W], in1=m01[:, :],
                                                op=ALU.max),
               reads=[b_om[s3]] + CONST, writes=[b_om[s3]])
        op("dve", lambda e: e.tensor_tensor_scan(out=Pb[s3][:, 0:W][:, ::-1], data0=om[s3][:, 0:W][:, ::-1],
                                                 data1=zeros[:, 0:W], initial=cin,
                                                 op0=ALU.mult, op1=ALU.add),
           reads=[b_om[s3]] + list(cin_deps) + CONST, writes=[b_Pb[s3]])
        op("pool", lambda e: e.tensor_tensor(out=wbf[s3][:, 0:W - 1], in0=Pb[s3][:, 1:W], in1=Pb[s3][:, 0:W - 1],
                                             op=ALU.subtract),
           reads=[b_Pb[s3]], writes=[b_wbf[s3]])
        op("pool", lambda e: e.tensor_tensor(out=wbf[s3][:, W - 1:W], in0=cin, in1=Pb[s3][:, W - 1:W],
                                             op=ALU.subtract),
           reads=[b_Pb[s3], b_wbf[s3]] + list(cin_deps), writes=[b_wbf[s3]])

    def stC(g):
        it = items[g]
        s3 = g % NR
        s2 = g % 2
        W = it["W"]
        for kbi in range(W // 128):
            op("pe", lambda e, kbi=kbi: e.transpose(wtp_bf[s2][:, kbi * 128:(kbi + 1) * 128],
                                                    wbf[s3][:, kbi * 128:(kbi + 1) * 128], identb[:, :]),
               reads=[b_wbf[s3]] + CONST, writes=[wtp[s2][1]])
        op("act", lambda e: e.activation(out=wT[s3][:, 0:W], in_=wtp_bf[s2][:, 0:W], func=AF.Copy),
           reads=[wtp[s2][1]], writes=[b_wT[s3]])

    def stD(g):
        it = items[g]
        s3 = g % NR
        tb, lo, W = it["tb"], it["lo"], it["W"]
        ab = acc[(tb // 4) % 2]
        ac = slice((tb % 4) * 128, (tb % 4 + 1) * 128)
        nk = W // 128
        for kbi in range(nk):
            kb = lo // 128 + kbi
            op("pe", lambda e, kbi=kbi, kb=kb: e.matmul(
                ab[0][64:128, ac], lhsT=vsb[:, kb, :], rhs=wT[s3][:, kbi * 128:(kbi + 1) * 128],
                start=(it["first"] and kbi == 0), stop=(it["last"] and kbi == nk - 1)),
               reads=[b_vsb[kb], b_wT[s3]], writes=[ab[1]])
        if it["last"] and tb % 4 == 3:
            q4 = tb // 4
            i2 = q4 % 2
            cols = slice(q4 * 512, (q4 + 1) * 512)
            op("dve", lambda e: e.tensor_tensor(out=ya[i2][64:128, :], in0=ab[0][64:128, :],
                                                in1=S0res[64:128, cols], op=ALU.mult),
               reads=[ab[1], b_S0g[q4]], writes=[b_ya[i2]])
            ysink["a"](q4, ya[i2], b_ya[i2])

    for i in range(n + 3):
        if i < n:
            stA(i)
        if 0 <= i - 1 < n:
            stB(i - 1)
        if 0 <= i - 2 < n:
            stC(i - 2)
        if 0 <= i - 3 < n:
            stD(i - 3)

    tr.barrier()
    es_cw.close()
    es_c.close()
    if "after_a" in ysink:
        ysink["after_a"]()


def emit_T(nc, tr, es0, psb, xt, xt_deps, yload, wg, wup, wout, lngb, cst, osink, pfx, use_pool=True):
    PE2 = "pool" if use_pool else "dve"
    op, dma = tr.op, tr.dma
    es_t = contextlib.ExitStack()
    es0.enter_context(es_t)
    ident = sb(nc, es_t, pfx + "ident", [128, 128], F32)
    lng = sb(nc, es_t, pfx + "lng", [128, D], F32)
    lnb = sb(nc, es_t, pfx + "lnb", [128, D], F32)
    b_c = tr.buf("tconst")
    dma("sp", ident[:], cst["ident"], b_c, writes=[b_c])
    dma("sp", lng[:], lngb[0:1, :].to_broadcast([128, D]), b_c, writes=[b_c])
    dma("sp", lnb[:], lngb[1:2, :].to_broadcast([128, D]), b_c, writes=[b_c])
    CONST = [b_c]
    wgb = sb(nc, es_t, pfx + "wgb", [128, 8, 3072], BF16)
    wupb = sb(nc, es_t, pfx + "wupb", [128, 6, D], BF16)
    woutb = sb(nc, es_t, pfx + "woutb", [128, 8, D], BF16)
    b_w = tr.buf("tw")
    stg = [sb(nc, es_t, pfx + f"stg{i}", [128, D], F32) for i in range(3)]
    b_stg = tr.bufs(3, "stg")
    cp_rr = [0]

    def cast_copy(out, in_, reads, writes):
        k = cp_rr[0] % 2
        cp_rr[0] += 1
        if k == 0:
            op("dve", lambda e: e.tensor_copy(out=out, in_=in_), reads=reads, writes=writes)
        elif k == 1:
            op("act", lambda e: e.activation(out=out, in_=in_, func=AF.Copy), reads=reads, writes=writes)
        else:
            op("pool", lambda e: e.tensor_copy(out=out, in_=in_), reads=reads, writes=writes)

    si = 0
    wg_v = wg.rearrange("(j p) c -> j p c", p=128)
    for j in range(8):
        for q in range(3):
            s = si % 3
            si += 1
            dma("sp", stg[s][:], wg_v[j][:, q * 1024:(q + 1) * 1024], b_stg[s], writes=[b_stg[s]])
            cast_copy(wgb[:, j, q * 1024:(q + 1) * 1024], stg[s][:], [b_stg[s]], [b_w])
    wup_v = wup.rearrange("(j p) c -> j p c", p=128)
    for j in range(6):
        s = si % 3
        si += 1
        dma("sp", stg[s][:], wup_v[j], b_stg[s], writes=[b_stg[s]])
        cast_copy(wupb[:, j, :], stg[s][:], [b_stg[s]], [b_w])
    wout_v = wout.rearrange("(j p) c -> j p c", p=128)
    for j in range(8):
        s = si % 3
        si += 1
        dma("sp", stg[s][:], wout_v[j], b_stg[s], writes=[b_stg[s]])
        cast_copy(woutb[:, j, :], stg[s][:], [b_stg[s]], [b_w])

    xres = sb(nc, es_t, pfx + "xres", [128, 4, D], F32)
    b_xres = tr.bufs(4, "xres")
    xT = sb(nc, es_t, pfx + "xT", [128, 8, 512], BF16)
    b_xT = [tr.bufs(2, "xT") for _ in range(4)]
    ybf = sb(nc, es_t, pfx + "ybf", [128, 6, 512], BF16)
    b_ybf = tr.bufs(6, "ybf")
    gsb = [sb(nc, es_t, pfx + f"gsb{i}", [128, 512], F32) for i in range(3)]
    b_gsb = tr.bufs(3, "gsb")
    tm_ = [sb(nc, es_t, pfx + f"tm{i}", [128, 512], F32) for i in range(2)]
    b_tm = tr.bufs(2, "tm")
    macc = [sb(nc, es_t, pfx + f"macc{i}", [128, 512], F32) for i in range(2)]
    b_macc = tr.bufs(2, "macc")
    mT = sb(nc, es_t, pfx + "mT", [128, 8, 512], BF16)
    b_mT = tr.bufs(8, "mT")
    rr = [sb(nc, es_t, pfx + f"rr{i}", [128, D], F32) for i in range(2)]
    b_rr = tr.bufs(2, "rr")
    sqj = [sb(nc, es_t, pfx + f"sqj{i}", [128, D], F32) for i in range(2)]
    b_sqj = tr.bufs(2, "sqj")
    st = [sb(nc, es_t, pfx + f"st{i}", [128, 8], F32) for i in range(2)]
    b_st = tr.bufs(2, "st")
    yo_ = [sb(nc, es_t, pfx + f"yo{i}", [128, D], F32) for i in range(2)]
    b_yo = tr.bufs(2, "yo")

    tp = [psb[0], psb[1]]
    gpb = [psb[2], psb[3], psb[4]]
    upb = [psb[5], psb[6]]
    outb = [psb[6], psb[7]]
    xt_v = xt.rearrange("(n p) d -> n p d", p=128)
    evr = [0]

    def evac_copy(out, in_, reads, writes):
        evr[0] ^= 1
        if evr[0]:
            op("act", lambda e: e.activation(out=out, in_=in_, func=AF.Copy), reads=reads, writes=writes)
        else:
            op("dve", lambda e: e.tensor_copy(out=out, in_=in_), reads=reads, writes=writes)

    gi = 0
    ui = 0
    for ti in range(TT // 512):
        cols = slice(ti * 512, (ti + 1) * 512)
        for bi in range(4):
            blk = ti * 4 + bi
            dma("sp", xres[:, bi, :], xt_v[blk], b_xres[bi], reads=xt_deps, writes=[b_xres[bi]])
            for half in range(2):
                bank = tp[half]
                for jj in range(4):
                    j = half * 4 + jj
                    op("pe", lambda e: e.transpose(bank[0][:, jj * 128:(jj + 1) * 128],
                                                   xres[:, bi, j * 128:(j + 1) * 128], ident[:]),
                       reads=[b_xres[bi]] + CONST, writes=[bank[1]])
                evac_copy(xT[:, half * 4:(half + 1) * 4, bi * 128:(bi + 1) * 128],
                          bank[0][:, :].rearrange("p (j t) -> p j t", t=128),
                          [bank[1]], [b_xT[bi][half]])
        for r in range(6):
            yload(r, ti, ybf[:, r, :], b_ybf[r])
        for dc in range(8):
            mi = dc % 2
            for nb_ in range(3):
                gb = gpb[gi % 3]
                gs = gi % 3
                gi += 1
                for j in range(8):
                    op("pe", lambda e: e.matmul(
                        gb[0][:, :], lhsT=wgb[:, j, nb_ * 1024 + dc * 128:nb_ * 1024 + (dc + 1) * 128],
                        rhs=xT[:, j, :], start=(j == 0), stop=(j == 7)),
                       reads=[b_w] + [b_xT[bi][j // 4] for bi in range(4)], writes=[gb[1]])
                op("act", lambda e: e.activation(out=gsb[gs][:, :], in_=gb[0][:, :], func=AF.Sigmoid),
                   reads=[gb[1]], writes=[b_gsb[gs]])
                ub = upb[ui % 2]
                ui += 1
                for r in range(2):
                    op("pe", lambda e: e.matmul(
                        ub[0][:, :], lhsT=wupb[:, nb_ * 2 + r, dc * 128:(dc + 1) * 128],
                        rhs=ybf[:, nb_ * 2 + r, :], start=(r == 0), stop=(r == 1)),
                       reads=[b_w, b_ybf[nb_ * 2 + r]], writes=[ub[1]])
                if nb_ == 0:
                    op("dve", lambda e: e.tensor_tensor(out=macc[mi][:, :], in0=gsb[gs][:, :], in1=ub[0][:, :],
                                                        op=ALU.mult),
                       reads=[b_gsb[gs], ub[1]], writes=[b_macc[mi]])
                else:
                    t2 = nb_ % 2
                    op("dve", lambda e: e.tensor_tensor(out=tm_[t2][:, :], in0=gsb[gs][:, :], in1=ub[0][:, :],
                                                        op=ALU.mult),
                       reads=[b_gsb[gs], ub[1]], writes=[b_tm[t2]])
                    if nb_ == 1:
                        op(PE2, lambda e: e.tensor_tensor(out=macc[mi][:, :], in0=macc[mi][:, :],
                                                             in1=tm_[t2][:, :], op=ALU.add),
                           reads=[b_macc[mi], b_tm[t2]], writes=[b_macc[mi]])
                    else:
                        op(PE2, lambda e: e.tensor_tensor(out=mT[:, dc, :], in0=macc[mi][:, :],
                                                             in1=tm_[t2][:, :], op=ALU.add),
                           reads=[b_macc[mi], b_tm[t2]], writes=[b_mT[dc]])
        for bi in range(4):
            blk = ti * 4 + bi
            i2 = blk % 2
            for hf in range(2):
                ob = outb[hf]
                for dc in range(8):
                    op("pe", lambda e: e.matmul(
                        ob[0][:, :], lhsT=mT[:, dc, bi * 128:(bi + 1) * 128],
                        rhs=woutb[:, dc, hf * 512:(hf + 1) * 512], start=(dc == 0), stop=(dc == 7)),
                       reads=[b_w, b_mT[dc]], writes=[ob[1]])
                op("dve", lambda e: e.scalar_tensor_tensor(
                    out=rr[i2][:, hf * 512:(hf + 1) * 512], in0=xres[:, bi, hf * 512:(hf + 1) * 512],
                    scalar=ALPHA, in1=ob[0][:, :], op0=ALU.mult, op1=ALU.add),
                   reads=[b_xres[bi], ob[1]], writes=[b_rr[i2]])
            op("dve", lambda e: e.reduce_sum(out=st[i2][:, 0:1], in_=rr[i2][:, :], axis=mybir.AxisListType.X),
               reads=[b_rr[i2]], writes=[b_st[i2]])
            op("dve", lambda e: e.tensor_scalar(out=st[i2][:, 1:2], in0=st[i2][:, 0:1], scalar1=-1.0 / D,
                                                scalar2=None, op0=ALU.mult),
               reads=[b_st[i2]], writes=[b_st[i2]])
            op("act", lambda e: e.activation(out=rr[i2][:, :], in_=rr[i2][:, :], func=AF.Identity,
                                             bias=st[i2][:, 1:2], scale=1.0),
               reads=[b_rr[i2], b_st[i2]], writes=[b_rr[i2]])
            op(PE2, lambda e: e.tensor_tensor(out=sqj[i2][:, :], in0=rr[i2][:, :], in1=rr[i2][:, :], op=ALU.mult),
               reads=[b_rr[i2]], writes=[b_sqj[i2]])
            op("dve", lambda e: e.reduce_sum(out=st[i2][:, 2:3], in_=sqj[i2][:, :], axis=mybir.AxisListType.X),
               reads=[b_sqj[i2], b_st[i2]], writes=[b_st[i2]])
            op("dve", lambda e: e.tensor_scalar(out=st[i2][:, 3:4], in0=st[i2][:, 2:3], scalar1=1.0 / D,
                                                scalar2=EPS, op0=ALU.mult, op1=ALU.add),
               reads=[b_st[i2]], writes=[b_st[i2]])
            op("act", lambda e: e.activation(out=st[i2][:, 5:6], in_=st[i2][:, 3:4], func=AF.Sqrt),
               reads=[b_st[i2]], writes=[b_st[i2]])
            op("dve", lambda e: e.reciprocal(out=st[i2][:, 4:5], in_=st[i2][:, 5:6]),
               reads=[b_st[i2]], writes=[b_st[i2]])
            op("dve", lambda e: e.scalar_tensor_tensor(out=yo_[i2][:, :], in0=rr[i2][:, :], scalar=st[i2][:, 4:5],
                                                       in1=lng[:, :], op0=ALU.mult, op1=ALU.mult),
               reads=[b_rr[i2], b_st[i2]] + CONST, writes=[b_yo[i2]])
            op(PE2, lambda e: e.tensor_tensor(out=yo_[i2][:, :], in0=yo_[i2][:, :], in1=lnb[:, :], op=ALU.add),
               reads=[b_yo[i2]] + CONST, writes=[b_yo[i2]])
            osink(blk, ti, bi, yo_[i2], b_yo[i2])
    tr.barrier()
    es_t.close()


def _consts():
    ident = np.eye(128, dtype=np.float32)
    p = np.arange(128)[:, None]
    cmask = ((p % 64) <= np.arange(64)[None, :]).astype(np.float32)
    m01 = (np.arange(128)[None, :] >= p).astype(np.float32)
    rmask = np.ones((128, 512), np.float32)
    rmask[:, ::64] = 0.0
    onesblk = np.zeros((128, 128), np.float32)
    onesblk[0:64, 0:64] = 1.0 / 64
    onesblk[64:128, 64:128] = 1.0 / 64
    return {"ident": ident, "cmask": cmask, "m01": m01, "rmask": rmask, "onesblk": onesblk}


_OFF = dict(a_q=0, a_k=256, a_v=512, a_g=768, h_f=1024, h_i=1280, h_q=1536, h_g=1792,
            c_q=2048, c_k=2176, c_v=2304, c_g=2560, c_r=2816, gates=2832)


def _h_weights(w_in_l, h):
    def col(name, width):
        o = _OFF[name] + h * width
        return w_in_l[:, o:o + width]
    z = lambda n: np.zeros((D, n), np.float32)
    wfm = np.concatenate([
        col("a_q", 64), col("a_g", 64),
        col("a_k", 64), w_in_l[:, _OFF["c_r"]:_OFF["c_r"] + 16], z(48),
        col("h_q", 64), col("c_q", 32), z(32),
        col("h_f", 64), col("c_k", 32), z(32),
        col("h_g", 64), col("c_g", 64),
    ], axis=1)
    wtm = np.concatenate([col("a_v", 64), col("h_i", 64), col("c_v", 64)], axis=1)
    return np.ascontiguousarray(wfm), np.ascontiguousarray(wtm)


def _h_params(inp, layer, h):
    pp = np.zeros((128, 8), np.float32)
    pp[0:64, 0] = inp["hgrn_lb_logits"][0, h * 64:(h + 1) * 64]
    pp[0:64, 1] = inp["hgrn_lb_logits"][1, h * 64:(h + 1) * 64]
    pp[64:96, 2] = inp["gla_gate_b"][layer, h * 32:(h + 1) * 32]
    pp[0:64, 3] = inp["hgrn_norm_g"][layer, h * 64:(h + 1) * 64]
    pp[64:128, 3] = inp["gla_norm_g"][layer, h * 64:(h + 1) * 64]
    pp[:, 4] = 1.0
    pp[64:96, 4] = 32.0 ** -0.5
    w2 = np.zeros((128, 32), np.float32)
    w2[64:80, :] = inp["gla_gate_w2"][layer][:, h * 32:(h + 1) * 32]
    return pp, w2


def _psum_banks(nc, tr, es):
    banks = []
    for i in range(8):
        t = es.enter_context(nc.psum_tensor(f"ps{i}", [128, 512], F32))
        banks.append((t, tr.buf("ps")))
    return banks


_CACHE = {}
HC = ("ident", "cmask", "m01", "rmask", "onesblk")
I32 = mybir.dt.int32
RG = [[0, 1, 2, 3], [4, 5, 6, 7]]


def build_fused():
    if "F" in _CACHE:
        return _CACHE["F"]
    nc = bass.Bass("TRN2", target_bir_lowering=False)
    ein = lambda name, shape: nc.dram_tensor(name, shape, F32, kind="ExternalInput").ap()
    xs = ein("xs", [T, D])
    xt = ein("xt", [TT, D])
    oh_in = ein("oh", [128, 8])
    cst = {k: ein("c_" + k, list(v.shape)) for k, v in _consts().items()}
    LW = []
    for l in range(2):
        LW.append(dict(wfm=ein(f"wfm{l}", [D, 640]), wtm=ein(f"wtm{l}", [D, 192]), pp=ein(f"pp{l}", [128, 8]),
                       w2=ein(f"w2{l}", [128, 32]), wg=ein(f"wg{l}", [D, 3072]), wup=ein(f"wup{l}", [768, D]),
                       wout=ein(f"wout{l}", [D, D]), lngb=ein(f"lngb{l}", [2, D])))
    xo = nc.dram_tensor("xo", [TT, D], F32, kind="ExternalOutput").ap()

    ybc_in = [nc.dram_tensor(f"ybc_in{q}", [512, 1024], F32) for q in range(4)]
    ya_in = [nc.dram_tensor(f"ya_in{q}", [256, 1024], F32) for q in range(4)]
    ybc_out = [nc.dram_tensor(f"ybc_out{q}", [512, 1024], F32) for q in range(4)]
    ya_out = [nc.dram_tensor(f"ya_out{q}", [256, 1024], F32) for q in range(4)]
    x1_in = [[nc.dram_tensor(f"x1_in{t}_{hf}", [2048, 256], F32) for hf in range(2)] for t in range(4)]
    x1_g = [[nc.dram_tensor(f"x1_g{t}_{hf}", [2048, 256], F32) for hf in range(2)] for t in range(4)]
    x1loc = nc.dram_tensor("x1loc", [TT, D], F32).ap()
    bf = lambda t: t.ap().bitcast(BF16)
    bf3 = lambda t: t.ap().bitcast(BF16).rearrange("(r p) c -> r p c", r=4)
    dyn = lambda v: v.rearrange("o p c -> (o p) c")

    with contextlib.ExitStack() as es:
        tr = Tr(nc, es)
        op, dma = tr.op, tr.dma
        psb = _psum_banks(nc, tr, es)

        b_ybc_in = tr.bufs(4, "ybc_in")
        b_ya_in = tr.bufs(4, "ya_in")
        b_ybc_out = tr.bufs(4, "ybc_out")
        b_ya_out = tr.bufs(4, "ya_out")
        b_x1_in = tr.bufs(4, "x1_in")
        b_x1_g = tr.bufs(4, "x1_g")
        b_x1loc = tr.buf("x1loc")

        oh = sb(nc, es, "oh_sb", [128, 8], F32)
        b_oh = tr.buf("oh")
        dma("sp", oh[:], oh_in, b_oh, writes=[b_oh])
        slot = [sb(nc, es, f"slot{i}", [128, 512], BF16) for i in range(4)]
        b_slot = tr.bufs(4, "slot")
        sl = [0]

        def scatter(tile_ap, nfree, dsts, reads, rows=slice(0, 128)):
            for j in range(4):
                k = sl[0] % 4
                sl[0] += 1
                op("act", lambda e: e.activation(out=slot[k][rows, 0:nfree], in_=tile_ap, func=AF.Identity,
                                                 scale=oh[rows, j:j + 1], bias=oh[rows, 4:5]),
                   reads=list(reads) + [b_oh], writes=[b_slot[k]])
                for (dram_ap, srows, dbuf) in dsts[j]:
                    dma("sp", dram_ap, slot[k][srows, 0:nfree], b_slot[k], reads=[b_slot[k]], writes=[dbuf])

        def run_layer(l):
            W = LW[l]

            def sink_bc(ti, tile, buf):
                q, c0 = ti // 4, (ti % 4) * 512
                v = bf(ybc_in[q])
                scatter(tile[:, :], 512,
                        [[(v[j * 64:(j + 1) * 64, c0:c0 + 512], slice(0, 64), b_ybc_in[q]),
                          (v[256 + j * 64:256 + (j + 1) * 64, c0:c0 + 512], slice(64, 128), b_ybc_in[q])]
                         for j in range(4)], [buf])

            def sink_a(q4, tile, buf):
                q, c0 = q4 // 4, (q4 % 4) * 512
                v = bf(ya_in[q])
                scatter(tile[64:128, :], 512,
                        [[(v[j * 64:(j + 1) * 64, c0:c0 + 512], slice(64, 128), b_ya_in[q])] for j in range(4)], [buf],
                        rows=slice(64, 128))

            def after_bc():
                for q in range(4):
                    tr.coll(lambda e: e.collective_compute(
                        "AllReduce", ALU.add, replica_groups=RG,
                        ins=[ybc_in[q].ap().opt()], outs=[ybc_out[q].ap().opt()]),
                        reads=[b_ybc_in[q]], writes=[b_ybc_out[q]])

            def after_a():
                for q in range(4):
                    tr.coll(lambda e: e.collective_compute(
                        "AllReduce", ALU.add, replica_groups=RG,
                        ins=[ya_in[q].ap().opt()], outs=[ya_out[q].ap().opt()]),
                        reads=[b_ya_in[q]], writes=[b_ya_out[q]])

            if l == 0:
                xsrc = ("f32", xs)
            else:
                def xblk(blk):
                    rk, t, bi = blk // 16, (blk // 4) % 4, blk % 4
                    return [bf(x1_g[t][hf])[rk * 512 + bi * 128:rk * 512 + (bi + 1) * 128, :] for hf in range(2)]
                xsrc = ("bf16", xblk, lambda blk: [b_x1_g[(blk // 4) % 4]])
            emit_H(nc, tr, es, psb, l, xsrc, W["wfm"], W["wtm"], W["pp"], W["w2"], cst,
                   dict(bc=sink_bc, a=sink_a, after_bc=after_bc, after_a=after_a), f"h{l}_")

            def yload(r6, ti, dst, buf):
                c0 = ti * 512
                for j in range(4):
                    if r6 < 2:
                        src = bf(ya_out[j])[r6 * 128:(r6 + 1) * 128, c0:c0 + 512]
                        sbuf_ = b_ya_out[j]
                    else:
                        src = bf(ybc_out[j])[(r6 - 2) * 128:(r6 - 1) * 128, c0:c0 + 512]
                        sbuf_ = b_ybc_out[j]
                    k = sl[0] % 4
                    sl[0] += 1
                    dma("sp", slot[k][:, 0:512], src, b_slot[k], reads=[sbuf_], writes=[b_slot[k]])
                    eng = "dve"
                    if j == 0:
                        op(eng, lambda e: e.tensor_scalar(out=dst, in0=slot[k][:, 0:512], scalar1=oh[:, 0:1],
                                                          scalar2=None, op0=ALU.mult),
                           reads=[b_slot[k], b_oh], writes=[buf])
                    else:
                        op("dve", lambda e: e.scalar_tensor_tensor(out=dst, in0=slot[k][:, 0:512], scalar=oh[:, j:j + 1],
                                                                 in1=dst, op0=ALU.mult, op1=ALU.add),
                           reads=[b_slot[k], b_oh, buf], writes=[buf])

            if l == 0:
                def osink(blk, ti, bi, tile, buf):
                    dma("sp", x1loc[blk * 128:(blk + 1) * 128, :], tile[:, :], buf, reads=[buf], writes=[b_x1loc])
                    for hf in range(2):
                        v = bf(x1_in[ti][hf])
                        scatter(tile[:, hf * 512:(hf + 1) * 512], 512,
                                [[(v[j * 512 + bi * 128:j * 512 + (bi + 1) * 128, :],
                                   slice(0, 128), b_x1_in[ti])] for j in range(4)], [buf])
                    if bi == 3:
                        for hf in range(2):
                            tr.coll(lambda e: e.collective_compute(
                                "AllReduce", ALU.add, replica_groups=RG,
                                ins=[x1_in[ti][hf].ap().opt()], outs=[x1_g[ti][hf].ap().opt()]),
                                reads=[b_x1_in[ti]], writes=[b_x1_g[ti]])
                emit_T(nc, tr, es, psb, xt, [], yload, W["wg"], W["wup"], W["wout"], W["lngb"], cst,
                       osink, f"t{l}_", use_pool=False)
            else:
                def osink(blk, ti, bi, tile, buf):
                    dma("sp", xo[blk * 128:(blk + 1) * 128, :], tile[:, :], buf, reads=[buf])
                emit_T(nc, tr, es, psb, x1loc, [b_x1loc], yload, W["wg"], W["wup"], W["wout"], W["lngb"], cst,
                       osink, f"t{l}_")

        for l in range(2):
            run_layer(l)
        tr.finish()
    _CACHE["F"] = nc
    return nc


def kernel(**inputs):
    inp = {k: np.asarray(v, dtype=np.float32) for k, v in inputs.items()}
    nc = build_fused()
    c = _consts()
    x = inp["x"]
    xf = x.reshape(2 * T, D)
    shared = {"c_" + k: v for k, v in c.items()}
    for l in range(2):
        shared[f"wg{l}"] = np.ascontiguousarray(inp["w_in"][l][:, _OFF["gates"]:_OFF["gates"] + 3072])
        shared[f"wup{l}"] = np.ascontiguousarray(inp["w_up"][l].reshape(768, D))
        shared[f"wout{l}"] = np.ascontiguousarray(inp["w_out"][l])
        shared[f"lngb{l}"] = np.stack([inp["ln_g"][l], inp["ln_b"][l]]).astype(np.float32)
    maps = []
    for core in range(NCORE):
        b, h = core // 4, core % 4
        m = dict(shared)
        m["xs"] = np.ascontiguousarray(x[b])
        m["xt"] = np.ascontiguousarray(xf[core * TT:(core + 1) * TT])
        ohm = np.zeros((128, 8), np.float32)
        ohm[:, h] = 1.0
        m["oh"] = ohm
        for l in range(2):
            wfm, wtm = _h_weights(inp["w_in"][l], h)
            pp, w2 = _h_params(inp, l, h)
            m[f"wfm{l}"], m[f"wtm{l}"], m[f"pp{l}"], m[f"w2{l}"] = wfm, wtm, pp, w2
        maps.append(m)
    res = run_bass_kernel_spmd(nc, maps, core_ids=list(range(NCORE)))
    out = np.concatenate([res.results[core]["xo"] for core in range(NCORE)], axis=0)
    return out.reshape(2, T, D).astype(np.float32)
```

```python
import contextlib
import numpy as np
import concourse.bass as bass
import concourse.mybir as mybir
from concourse.bass_utils import run_bass_kernel_spmd

F32 = mybir.dt.float32
BF16 = mybir.dt.bfloat16
AF = mybir.ActivationFunctionType
ALU = mybir.AluOpType

_STOP = ""
_KNT = 16
_KB = 9
T = 8192
D = 1024
NCORE = 8
TT = 2048
EPS = 1e-5
ALPHA = 4.0 ** 0.25


class Buf:
    __slots__ = ("name", "w", "r", "dsem", "dcnt")

    def __init__(self, name):
        self.name = name
        self.w = None
        self.r = {}
        self.dsem = None
        self.dcnt = 0


class Tr:
    NDP = 18

    def __init__(self, nc, es):
        self.nc = nc
        self.es = es
        self.engs = {"pe": nc.tensor, "act": nc.scalar, "dve": nc.vector,
                     "pool": nc.gpsimd, "sp": nc.sync}
        self.sem = {k: es.enter_context(nc.semaphore("sem_" + k)) for k in self.engs}
        self.cnt = {k: 0 for k in self.engs}
        self.waited = {k: {} for k in self.engs}
        self.dpool = [[es.enter_context(nc.semaphore(f"dsem{i}")), 0] for i in range(self.NDP)]
        self.free = list(range(self.NDP))
        self.csem = es.enter_context(nc.semaphore("csem"))
        self.ccnt = 0
        self.owners = []
        self.all_bufs = []
        self.nbuf = 0

    def buf(self, name="b"):
        self.nbuf += 1
        b = Buf(f"{name}{self.nbuf}")
        self.all_bufs.append(b)
        return b

    def bufs(self, n, name="b"):
        return [self.buf(name) for _ in range(n)]

    def _waits(self, e, reads, writes):
        need = {}

        def add(tag):
            key, sem, val = tag
            if key not in need or need[key][1] < val:
                need[key] = (sem, val)

        for b in reads:
            if b.w is not None:
                add(b.w)
        for b in writes:
            if b.w is not None:
                add(b.w)
            for tag in b.r.values():
                add(tag)
        for key, (sem, val) in need.items():
            if key == "pe" and e == "pe":
                continue
            if self.waited[e].get(key, 0) >= val:
                continue
            self.engs[e].wait_ge(sem, val)
            self.waited[e][key] = val

    def _mark(self, tag, reads, writes):
        for b in reads:
            b.r[tag[0]] = tag
        for b in writes:
            b.w = tag
            b.r = {}

    def op(self, e, fn, reads=(), writes=(), serial=False):
        self._waits(e, reads, writes)
        if serial and self.cnt[e] > 0 and self.waited[e].get("self", 0) < self.cnt[e]:
            self.engs[e].wait_ge(self.sem[e], self.cnt[e])
            self.waited[e]["self"] = self.cnt[e]
        inst = fn(self.engs[e])
        self.cnt[e] += 1
        inst.then_inc(self.sem[e], 1)
        self._mark((e, self.sem[e], self.cnt[e]), reads, writes)

    def dma(self, q, out, in_, owner, reads=(), writes=()):
        self._waits(q, reads, writes)
        if owner.dsem is None:
            owner.dsem = self.free.pop(0)
            self.owners.append(owner)
        slot = self.dpool[owner.dsem]
        self.engs[q].dma_start(out=out, in_=in_).then_inc(slot[0], 16)
        slot[1] += 16
        self._mark((f"p{owner.dsem}", slot[0], slot[1]), reads, writes)

    def retag(self, bufs, owner):
        slot = self.dpool[owner.dsem]
        for b in bufs:
            b.w = (f"p{owner.dsem}", slot[0], slot[1])

    def coll(self, fn, reads=(), writes=()):
        self._waits("pool", reads, writes)
        fn(self.engs["pool"]).then_inc(self.csem)
        self.ccnt += 1
        self._mark(("coll", self.csem, self.ccnt), reads, writes)

    def _sync_all(self, e):
        for e2 in self.engs:
            if e2 == e or self.cnt[e2] == 0:
                continue
            if self.waited[e].get(e2, 0) >= self.cnt[e2]:
                continue
            self.engs[e].wait_ge(self.sem[e2], self.cnt[e2])
            self.waited[e][e2] = self.cnt[e2]
        for i, (sem, cnt) in enumerate(self.dpool):
            if cnt == 0 or self.waited[e].get(f"p{i}", 0) >= cnt:
                continue
            self.engs[e].wait_ge(sem, cnt)
            self.waited[e][f"p{i}"] = cnt
        if self.ccnt and self.waited[e].get("coll", 0) < self.ccnt:
            self.engs[e].wait_ge(self.csem, self.ccnt)
            self.waited[e]["coll"] = self.ccnt

    def barrier(self):
        for e in self.engs:
            self._sync_all(e)
        for o in self.owners:
            self.free.append(o.dsem)
            o.dsem = None
        self.owners = []
        self.free.sort()
        for b in self.all_bufs:
            b.w = None
            b.r = {}

    def finish(self):
        self._sync_all("sp")


def sb(nc, es, name, shape, dt):
    return es.enter_context(nc.sbuf_tensor(name, shape, dt))


def emit_H(nc, tr, es0, psb, layer, xsrc, wfm, wtm, pp, w2, cst, ysink, pfx):
    op, dma = tr.op, tr.dma
    NB, NT, NCH = T // 128, T // 512, T // 64

    es_c = contextlib.ExitStack()
    es0.enter_context(es_c)
    ident = sb(nc, es_c, pfx + "ident", [128, 128], F32)
    identb = sb(nc, es_c, pfx + "identb", [128, 128], BF16)
    cmask = sb(nc, es_c, pfx + "cmask", [128, 64], F32)
    m01 = sb(nc, es_c, pfx + "m01", [128, 128], F32)
    rmask = sb(nc, es_c, pfx + "rmask", [128, 512], F32)
    onesblk = sb(nc, es_c, pfx + "onesblk", [128, 128], F32)
    zeros = sb(nc, es_c, pfx + "zeros", [128, 512], F32)
    ppt = sb(nc, es_c, pfx + "ppt", [128, 8], F32)
    pp2 = sb(nc, es_c, pfx + "pp2", [128, 8], F32)
    w2f = sb(nc, es_c, pfx + "w2f", [128, 32], F32)
    w2b = sb(nc, es_c, pfx + "w2b", [128, 32], BF16)
    b_c = tr.buf("const")
    for tl, nm in ((ident, "ident"), (cmask, "cmask"), (m01, "m01"), (rmask, "rmask"),
                   (onesblk, "onesblk")):
        dma("sp", tl[:], cst[nm], b_c, writes=[b_c])
    dma("sp", ppt[:], pp, b_c, writes=[b_c])
    dma("sp", w2f[:], w2, b_c, writes=[b_c])
    b_c2 = tr.buf("const2")
    op("pool", lambda e: e.memset(zeros[:], 0.0), writes=[b_c2])
    op("pool", lambda e: e.memset(pp2[:], 0.0), writes=[b_c2])
    op("pool", lambda e: e.memset(pp2[:, 5:6], EPS), reads=[b_c2], writes=[b_c2])
    op("pool", lambda e: e.memset(pp2[:, 6:7], 1.0), reads=[b_c2], writes=[b_c2])
    op("dve", lambda e: e.tensor_copy(out=identb[:], in_=ident[:]), reads=[b_c], writes=[b_c2])
    op("dve", lambda e: e.tensor_copy(out=w2b[:], in_=w2f[:]), reads=[b_c], writes=[b_c2])
    if layer == 0:
        op("pool", lambda e: e.memset(pp2[:, 1:2], 1.0), reads=[b_c2], writes=[b_c2])
    else:
        op("dve", lambda e: e.tensor_tensor(out=pp2[:, 3:4], in0=ppt[:, 0:1], in1=ppt[:, 1:2],
                                            op=ALU.subtract), reads=[b_c, b_c2], writes=[b_c2])
        op("act", lambda e: e.activation(out=pp2[:, 4:5], in_=pp2[:, 3:4], func=AF.Exp),
           reads=[b_c2], writes=[b_c2])
        op("dve", lambda e: e.tensor_scalar(out=pp2[:, 4:5], in0=pp2[:, 4:5], scalar1=1.0,
                                            scalar2=None, op0=ALU.add), reads=[b_c2], writes=[b_c2])
        op("dve", lambda e: e.reciprocal(out=pp2[:, 0:1], in_=pp2[:, 4:5]), reads=[b_c2], writes=[b_c2])
        op("dve", lambda e: e.tensor_scalar(out=pp2[:, 1:2], in0=pp2[:, 0:1], scalar1=-1.0,
                                            scalar2=1.0, op0=ALU.mult, op1=ALU.add),
           reads=[b_c2], writes=[b_c2])
    op("dve", lambda e: e.tensor_scalar(out=pp2[:, 2:3], in0=ppt[:, 2:3], scalar1=-1.0,
                                        scalar2=None, op0=ALU.mult), reads=[b_c, b_c2], writes=[b_c2])
    CONST = [b_c, b_c2]


    S0res = sb(nc, es_c, pfx + "S0res", [128, T], BF16)
    S1res = sb(nc, es_c, pfx + "S1res", [128, T], BF16)
    vsb = sb(nc, es_c, pfx + "vsb", [128, NB, 64], BF16)
    b_S0q = tr.bufs(NT, "S0q")
    b_S0g = tr.bufs(NT, "S0g")
    b_S1 = tr.bufs(NT, "S1")
    b_vsb = tr.bufs(NB, "vsb")

    es_r = contextlib.ExitStack()
    vrec = sb(nc, es_r, pfx + "vrec", [128, NB, 128], BF16)
    qdec = sb(nc, es_r, pfx + "qdec", [128, T], BF16)
    kdec = sb(nc, es_r, pfx + "kdec", [128, T], BF16)
    kdsT = sb(nc, es_r, pfx + "kdsT", [128, NB, 96], BF16)
    RGres = sb(nc, es_r, pfx + "RGres", [128, T], BF16)
    U = sb(nc, es_r, pfx + "U", [128, T], F32)
    Dres = sb(nc, es_r, pfx + "Dres", [128, NCH], F32)
    b_vrec = tr.bufs(NB, "vrec")
    b_qdec = tr.bufs(NT, "qdec")
    b_kdec = tr.bufs(NT, "kdec")
    b_kdsT = tr.bufs(NT, "kdsT")
    b_RG = tr.bufs(NT, "RG")
    b_U = tr.buf("U")
    b_D = tr.buf("D")

    es_a = contextlib.ExitStack()
    xst = [sb(nc, es_a, pfx + f"xst{i}", [128, D], F32) for i in range(2)]
    b_xst = tr.bufs(2, "xst")
    xbf = xsrc[0] == "bf16"
    xstb = [xt_[:].bitcast(BF16) for xt_ in xst]
    xT = sb(nc, es_a, pfx + "xT", [128, 8, 512], BF16)
    b_xT = [tr.bufs(2, "xT") for _ in range(4)]
    wfmb = sb(nc, es_a, pfx + "wfmb", [128, 8, 640], BF16)
    wtmb = sb(nc, es_a, pfx + "wtmb", [128, 8, 192], BF16)
    b_w = tr.buf("w")
    NTMP = 9
    tmp = [sb(nc, es_a, pfx + f"tmp{i}", [128, 512], F32) for i in range(NTMP)]
    bh = tr.bufs(NTMP, "tmph")
    bc = tr.bufs(NTMP, "tmpc")
    kds = sb(nc, es_a, pfx + "kds", [128, 512], BF16)
    b_kds = tr.buf("kds")
    (t_sig, t_f, t_g, t_kk, t_q, t_cum, t_e1, t_e2, t_d) = range(NTMP)

    wfm_v = wfm.rearrange("(j p) c -> j p c", p=128)
    wtm_v = wtm.rearrange("(j p) c -> j p c", p=128)
    for j in range(8):
        s = j % 2
        dma("sp", xst[s][:, 0:640], wfm_v[j], b_xst[s], writes=[b_xst[s]])
        dma("sp", xst[s][:, 640:832], wtm_v[j], b_xst[s], writes=[b_xst[s]])
        if j % 2 == 0:
            op("dve", lambda e, j=j, s=s: e.tensor_copy(out=wfmb[:, j, :], in_=xst[s][:, 0:640]),
               reads=[b_xst[s]], writes=[b_w])
        else:
            op("act", lambda e, j=j, s=s: e.activation(out=wfmb[:, j, :], in_=xst[s][:, 0:640], func=AF.Copy),
               reads=[b_xst[s]], writes=[b_w])
        op("pool", lambda e, j=j, s=s: e.tensor_copy(out=wtmb[:, j, :], in_=xst[s][:, 640:832]),
           reads=[b_xst[s]], writes=[b_w])

    tp = [psb[0], psb[1]]
    fm = [psb[2], psb[3]]
    tmq = [psb[4], psb[5]]
    ktp = psb[6]
    ups = psb[7]
    ktp_bf = ktp[0][:].bitcast(BF16)

    xs_v = None if xbf else xsrc[1].rearrange("(n p) d -> n p d", p=128)
    tp_bf = [psb[0][0][:].bitcast(BF16), psb[1][0][:].bitcast(BF16)]
    evac_rr = [0]

    def evac_copy(out, in_, reads, writes):
        evac_rr[0] ^= 1
        if evac_rr[0]:
            op("act", lambda e: e.activation(out=out, in_=in_, func=AF.Copy), reads=reads, writes=writes)
        else:
            op("dve", lambda e: e.tensor_copy(out=out, in_=in_), reads=reads, writes=writes)

    fmi = [0]

    def fm_chunk(ci):
        bank = fm[fmi[0] % 2]
        fmi[0] += 1
        for j in range(8):
            op("pe", lambda e, j=j, bank=bank: e.matmul(bank[0][:, :], lhsT=wfmb[:, j, ci * 128:(ci + 1) * 128],
                                                        rhs=xT[:, j, :], start=(j == 0), stop=(j == 7)),
               reads=[b_w] + [b_xT[bi][j // 4] for bi in range(4)], writes=[bank[1]])
        return bank

    for ti in range(min(NT, _KNT)):
        cols = slice(ti * 512, (ti + 1) * 512)
        for bi in range(4):
            blk = ti * 4 + bi
            s = blk % 2
            if xbf:
                for hf, src_ap in enumerate(xsrc[1](blk)):
                    dma("sp", xstb[s][:, hf * 512:(hf + 1) * 512], src_ap, b_xst[s], reads=xsrc[2](blk),
                        writes=[b_xst[s]])
            else:
                dma("sp", xst[s][:], xs_v[blk], b_xst[s], writes=[b_xst[s]])
            for half in range(2):
                bank = tp[half]
                for jj in range(4):
                    j = half * 4 + jj
                    if xbf:
                        op("pe", lambda e: e.transpose(
                            tp_bf[half][:, jj * 128:(jj + 1) * 128], xstb[s][:, j * 128:(j + 1) * 128], identb[:]),
                           reads=[b_xst[s]] + CONST, writes=[bank[1]])
                    else:
                        op("pe", lambda e: e.transpose(
                            bank[0][:, jj * 128:(jj + 1) * 128], xst[s][:, j * 128:(j + 1) * 128], ident[:]),
                           reads=[b_xst[s]] + CONST, writes=[bank[1]])
                src_ = tp_bf[half][:, 0:512] if xbf else bank[0][:, :]
                evac_copy(xT[:, half * 4:(half + 1) * 4, bi * 128:(bi + 1) * 128],
                          src_.rearrange("p (j t) -> p j t", t=128),
                          [bank[1]], [b_xT[bi][half]])
            tmb = tmq[bi % 2]
            for j in range(8):
                op("pe", lambda e: e.matmul(
                    tmb[0][:, 0:192], lhsT=xT[:, j, bi * 128:(bi + 1) * 128], rhs=wtmb[:, j, :],
                    start=(j == 0), stop=(j == 7)),
                   reads=[b_w, b_xT[bi][j // 4]], writes=[tmb[1]])
            if bi % 2 == 0:
                op("act", lambda e: e.activation(out=vsb[:, blk, :], in_=tmb[0][:, 0:64], func=AF.Copy),
                   reads=[tmb[1]], writes=[b_vsb[blk]])
                op("act", lambda e: e.activation(out=vrec[:, blk, :], in_=tmb[0][:, 64:192], func=AF.Copy),
                   reads=[tmb[1]], writes=[b_vrec[blk]])
            else:
                op("dve", lambda e: e.tensor_copy(out=vsb[:, blk, :], in_=tmb[0][:, 0:64]),
                   reads=[tmb[1]], writes=[b_vsb[blk]])
                op("dve", lambda e: e.tensor_copy(out=vrec[:, blk, :], in_=tmb[0][:, 64:192]),
                   reads=[tmb[1]], writes=[b_vrec[blk]])

        bank = fm_chunk(0)
        op("dve", lambda e, bank=bank: e.tensor_copy(out=S0res[0:64, cols], in_=bank[0][0:64, :]),
           reads=[bank[1]], writes=[b_S0q[ti]])
        op("act", lambda e, bank=bank: e.activation(out=S0res[64:128, cols], in_=bank[0][64:128, :], func=AF.Silu),
           reads=[bank[1]], writes=[b_S0g[ti]])
        bank = fm_chunk(3)
        op("act", lambda e, bank=bank: e.activation(out=tmp[t_sig][0:64, :], in_=bank[0][0:64, :], func=AF.Sigmoid),
           reads=[bank[1]], writes=[bh[t_sig]])
        op("act", lambda e, bank=bank: e.activation(out=tmp[t_kk][64:96, :], in_=bank[0][64:96, :], func=AF.Copy),
           reads=[bank[1]], writes=[bc[t_kk]])
        op("dve", lambda e: e.tensor_scalar(out=tmp[t_f][0:64, :], in0=tmp[t_sig][0:64, :],
                                            scalar1=pp2[0:64, 1:2], scalar2=pp2[0:64, 0:1],
                                            op0=ALU.mult, op1=ALU.add),
           reads=[bh[t_sig]] + CONST, writes=[bh[t_f]])
        bank = fm_chunk(4)
        op("act", lambda e, bank=bank: e.activation(out=RGres[:, cols], in_=bank[0][:, :], func=AF.Silu),
           reads=[bank[1]], writes=[b_RG[ti]])
        bank = fm_chunk(1)
        op("dve", lambda e, bank=bank: e.tensor_copy(out=S1res[0:96, cols], in_=bank[0][0:96, :]),
           reads=[bank[1]], writes=[b_S1[ti]])
        gp = fm[fmi[0] % 2]
        fmi[0] += 1
        op("pe", lambda e: e.matmul(gp[0][64:96, :], lhsT=w2b[64:80, :], rhs=S1res[64:80, cols],
                                    start=True, stop=True),
           reads=[b_S1[ti]] + CONST, writes=[gp[1]])
        op("act", lambda e: e.activation(out=tmp[t_e1][64:96, :], in_=gp[0][64:96, :], func=AF.Exp,
                                         scale=-1.0, bias=pp2[64:96, 2:3]),
           reads=[gp[1]] + CONST, writes=[bc[t_e1]])
        op("act", lambda e: e.activation(out=tmp[t_e2][64:96, :], in_=tmp[t_e1][64:96, :], func=AF.Ln,
                                         bias=pp2[64:96, 6:7]),
           reads=[bc[t_e1]] + CONST, writes=[bc[t_e2]])
        op("act", lambda e: e.activation(out=tmp[t_g][64:96, :], in_=tmp[t_e2][64:96, :], func=AF.Copy,
                                         scale=-1.0 / 16.0),
           reads=[bc[t_e2]], writes=[bc[t_g]])
        op("act", lambda e: e.activation(out=tmp[t_g][0:64, :], in_=tmp[t_f][0:64, :], func=AF.Ln),
           reads=[bh[t_f]], writes=[bh[t_g]])
        op("pool", lambda e: e.tensor_scalar(out=tmp[t_kk][0:64, :], in0=tmp[t_f][0:64, :],
                                             scalar1=-1.0, scalar2=1.0, op0=ALU.mult, op1=ALU.add),
           reads=[bh[t_f]], writes=[bh[t_kk]])
        bank = fm_chunk(2)
        op("act", lambda e, bank=bank: e.activation(out=tmp[t_q][0:96, :], in_=bank[0][0:96, :], func=AF.Identity,
                                                    scale=ppt[0:96, 4:5]),
           reads=[bank[1]] + CONST, writes=[bh[t_q], bc[t_q]])
        op("dve", lambda e: e.tensor_tensor_scan(out=tmp[t_cum][0:96, :], data0=rmask[0:96, :],
                                                 data1=tmp[t_g][0:96, :], initial=0.0,
                                                 op0=ALU.mult, op1=ALU.add),
           reads=[bh[t_g], bc[t_g]] + CONST, writes=[bh[t_cum], bc[t_cum]])
        op("act", lambda e: e.activation(out=tmp[t_e1][0:96, :], in_=tmp[t_cum][0:96, :], func=AF.Exp),
           reads=[bh[t_cum], bc[t_cum]], writes=[bh[t_e1], bc[t_e1]])
        op("pool", lambda e: e.tensor_tensor(out=qdec[0:96, cols], in0=tmp[t_q][0:96, :],
                                             in1=tmp[t_e1][0:96, :], op=ALU.mult),
           reads=[bh[t_q], bc[t_q], bh[t_e1], bc[t_e1]], writes=[b_qdec[ti]])
        op("act", lambda e: e.activation(out=tmp[t_e2][0:96, :], in_=tmp[t_cum][0:96, :], func=AF.Exp,
                                         scale=-1.0),
           reads=[bh[t_cum], bc[t_cum]], writes=[bh[t_e2], bc[t_e2]])
        op("dve", lambda e: e.tensor_tensor(out=kdec[0:96, cols], in0=tmp[t_kk][0:96, :],
                                            in1=tmp[t_e2][0:96, :], op=ALU.mult),
           reads=[bh[t_kk], bc[t_kk], bh[t_e2], bc[t_e2]], writes=[b_kdec[ti]])
        cum3 = tmp[t_cum][0:96, :].rearrange("p (c t) -> p c t", t=64)
        d3 = tmp[t_d][0:96, :].rearrange("p (c t) -> p c t", t=64)
        op("pool", lambda e: e.tensor_tensor(out=d3, in0=cum3[:, :, 63:64].to_broadcast([96, 8, 64]),
                                             in1=cum3, op=ALU.subtract),
           reads=[bh[t_cum], bc[t_cum]], writes=[bh[t_d], bc[t_d]])
        op("act", lambda e: e.activation(out=tmp[t_d][0:96, :], in_=tmp[t_d][0:96, :], func=AF.Exp),
           reads=[bh[t_d], bc[t_d]], writes=[bh[t_d], bc[t_d]])
        op("pool", lambda e: e.tensor_tensor(out=kds[0:96, :], in0=tmp[t_kk][0:96, :],
                                             in1=tmp[t_d][0:96, :], op=ALU.mult),
           reads=[bh[t_kk], bc[t_kk], bh[t_d], bc[t_d]], writes=[b_kds])
        op("act", lambda e: e.activation(out=Dres[0:96, ti * 8:(ti + 1) * 8],
                                         in_=tmp[t_cum][0:96, 63:512:64], func=AF.Exp),
           reads=[bh[t_cum], bc[t_cum]], writes=[b_D])
        for bi in range(4):
            op("pe", lambda e, bi=bi: e.transpose(ktp_bf[:, bi * 96:(bi + 1) * 96],
                                                  kds[0:96, bi * 128:(bi + 1) * 128], identb[0:96, 0:96]),
               reads=[b_kds] + CONST, writes=[ktp[1]])
        evac_copy(kdsT[:, ti * 4:(ti + 1) * 4, :], ktp_bf[:, 0:384].rearrange("p (b c) -> p b c", c=96),
                  [ktp[1]], [b_kdsT[ti]])
        for cc in range(8):
            c = ti * 8 + cc
            blk, par = c // 2, c % 2
            rows = slice(64 * par, 64 * par + 64)
            pc = slice(cc * 64, (cc + 1) * 64)
            op("pe", lambda e, blk=blk, rows=rows, pc=pc: e.matmul(
                ups[0][0:64, pc], lhsT=kdsT[rows, blk, 0:64], rhs=vrec[rows, blk, 0:64], start=True, stop=True),
               reads=[b_kdsT[ti], b_vrec[blk]], writes=[ups[1]], serial=True)
            op("pe", lambda e, blk=blk, rows=rows, pc=pc: e.matmul(
                ups[0][64:96, pc], lhsT=kdsT[rows, blk, 64:96], rhs=vrec[rows, blk, 64:128], start=True, stop=True),
               reads=[b_kdsT[ti], b_vrec[blk]], writes=[ups[1]], serial=True)
        op("dve", lambda e: e.tensor_copy(
            out=U[0:96, :].rearrange("p (v c) -> p v c", c=128)[:, :, ti * 8:(ti + 1) * 8],
            in_=ups[0][0:96, :].rearrange("p (c v) -> p v c", v=64)),
           reads=[ups[1]], writes=[b_U])

    tr.barrier()
    es_a.close()
    if _STOP == "A":
        es_r.close()
        es_c.close()
        return
    es_b = contextlib.ExitStack()
    Sprev = sb(nc, es_b, pfx + "Sprev", [128, NCH * 64], BF16)
    b_Sp = tr.buf("Sprev")
    scm = [[sb(nc, es_b, pfx + f"scm{h}{i}", [128, 512], BF16) for i in range(2)] for h in range(2)]
    b_scm = [tr.bufs(2, "scm") for _ in range(2)]
    osb = [sb(nc, es_b, pfx + f"osb{i}", [128, 512], F32) for i in range(2)]
    b_osb = tr.bufs(2, "osb")
    sq = [sb(nc, es_b, pfx + f"sq{i}", [128, 512], F32) for i in range(2)]
    b_sq = tr.bufs(2, "sq")
    rstd = [sb(nc, es_b, pfx + f"rstd{i}", [128, 512], F32) for i in range(2)]
    b_rstd = tr.bufs(2, "rstd")
    yt = [sb(nc, es_b, pfx + f"yt{i}", [128, 512], BF16) for i in range(2)]
    b_yt = tr.bufs(2, "yt")

    op("pool", lambda e: e.memset(Sprev[:, 0:64], 0.0), writes=[b_Sp])
    Sp3 = Sprev[0:96, :].rearrange("p (c v) -> p c v", v=64)
    for v in range(64):
        op("dve", lambda e, v=v: e.tensor_tensor_scan(
            out=Sp3[:, 1:128, v], data0=Dres[0:96, 0:127], data1=U[0:96, v * 128:v * 128 + 127],
            initial=0.0, op0=ALU.mult, op1=ALU.add),
           reads=[b_U, b_D], writes=[b_Sp])

    scb = [psb[0], psb[1]]
    opsb = [psb[2], psb[3]]
    msb = [psb[4], psb[5]]
    for ti in range(NT if _KB > 0 else 0):
        cols = slice(ti * 512, (ti + 1) * 512)
        i2 = ti % 2
        for cc in range(8):
            c = ti * 8 + cc
            blk, par = c // 2, c % 2
            rows = slice(64 * par, 64 * par + 64)
            pc = slice(cc * 64, (cc + 1) * 64)
            tc_ = slice(c * 64, (c + 1) * 64)
            op("pe", lambda e, rows=rows, pc=pc, tc_=tc_: e.matmul(
                scb[0][0][rows, pc], lhsT=kdec[0:64, tc_], rhs=qdec[0:64, tc_], start=True, stop=True),
               reads=[b_kdec[ti], b_qdec[ti]], writes=[scb[0][1]])
            op("pe", lambda e, rows=rows, pc=pc, tc_=tc_: e.matmul(
                scb[1][0][rows, pc], lhsT=kdec[64:96, tc_], rhs=qdec[64:96, tc_], start=True, stop=True),
               reads=[b_kdec[ti], b_qdec[ti]], writes=[scb[1][1]])
        for h in range(2):
            for par in range(2):
                rows = slice(64 * par, 64 * par + 64)
                src = scb[h][0][rows, :].rearrange("p (c two t) -> p c two t", two=2, t=64)[:, :, par, :]
                dst = scm[h][i2][rows, :].rearrange("p (c two t) -> p c two t", two=2, t=64)[:, :, par, :]
                msk = cmask[rows, :].unsqueeze(1).to_broadcast([64, 4, 64])
                op("dve", lambda e, src=src, dst=dst, msk=msk: e.tensor_tensor(out=dst, in0=src, in1=msk, op=ALU.mult),
                   reads=[scb[h][1]] + CONST, writes=[b_scm[h][i2]])
        if _KB < 2:
            continue
        ob = opsb[i2]
        for cc in range(8):
            c = ti * 8 + cc
            blk, par = c // 2, c % 2
            rows = slice(64 * par, 64 * par + 64)
            pc = slice(cc * 64, (cc + 1) * 64)
            tc_ = slice(c * 64, (c + 1) * 64)
            sc_ = slice(c * 64, (c + 1) * 64)
            op("pe", lambda e, pc=pc, tc_=tc_, sc_=sc_: e.matmul(
                ob[0][0:64, pc], lhsT=Sprev[0:64, sc_], rhs=qdec[0:64, tc_], start=True, stop=False),
               reads=[b_Sp, b_qdec[ti]], writes=[ob[1]], serial=True)
            op("pe", lambda e, pc=pc, rows=rows, blk=blk: e.matmul(
                ob[0][0:64, pc], lhsT=vrec[rows, blk, 0:64], rhs=scm[0][i2][rows, pc], start=False, stop=True),
               reads=[b_vrec[blk], b_scm[0][i2]], writes=[ob[1]], serial=True)
            op("pe", lambda e, pc=pc, tc_=tc_, sc_=sc_: e.matmul(
                ob[0][64:128, pc], lhsT=Sprev[64:96, sc_], rhs=qdec[64:96, tc_], start=True, stop=False),
               reads=[b_Sp, b_qdec[ti]], writes=[ob[1]], serial=True)
            op("pe", lambda e, pc=pc, rows=rows, blk=blk: e.matmul(
                ob[0][64:128, pc], lhsT=vrec[rows, blk, 64:128], rhs=scm[1][i2][rows, pc], start=False, stop=True),
               reads=[b_vrec[blk], b_scm[1][i2]], writes=[ob[1]], serial=True)
        if _KB < 3:
            continue
        op("act", lambda e: e.activation(out=osb[i2][:, :], in_=ob[0][:, :], func=AF.Copy),
           reads=[ob[1]], writes=[b_osb[i2]])
        op("act", lambda e: e.activation(out=sq[i2][:, :], in_=ob[0][:, :], func=AF.Square),
           reads=[ob[1]], writes=[b_sq[i2]])
        mb = msb[i2]
        op("pe", lambda e: e.matmul(mb[0][:, :], lhsT=onesblk[:, :], rhs=sq[i2][:, :], start=True, stop=True),
           reads=[b_sq[i2]] + CONST, writes=[mb[1]])
        op("act", lambda e: e.activation(out=rstd[i2][:, :], in_=mb[0][:, :], func=AF.Sqrt, bias=pp2[:, 5:6]),
           reads=[mb[1]] + CONST, writes=[b_rstd[i2]])
        op("dve", lambda e: e.reciprocal(out=rstd[i2][:, :], in_=rstd[i2][:, :]),
           reads=[b_rstd[i2]], writes=[b_rstd[i2]])
        op("pool", lambda e: e.tensor_tensor(out=osb[i2][:, :], in0=osb[i2][:, :], in1=rstd[i2][:, :], op=ALU.mult),
           reads=[b_osb[i2], b_rstd[i2]], writes=[b_osb[i2]])
        op("dve", lambda e: e.scalar_tensor_tensor(out=yt[i2][:, :], in0=osb[i2][:, :], scalar=ppt[:, 3:4],
                                                   in1=RGres[:, cols], op0=ALU.mult, op1=ALU.mult),
           reads=[b_osb[i2], b_RG[ti]] + CONST, writes=[b_yt[i2]])
        ysink["bc"](ti, yt[i2], b_yt[i2])

    tr.barrier()
    es_b.close()
    es_r.close()
    if _STOP == "B":
        es_c.close()
        return
    if "after_bc" in ysink:
        ysink["after_bc"]()
    es_cw = contextlib.ExitStack()
    NR = 3
    om = [sb(nc, es_cw, pfx + f"om{i}", [128, 512], F32) for i in range(NR)]
    b_om = tr.bufs(NR, "om")
    Pb = [sb(nc, es_cw, pfx + f"Pb{i}", [128, 513], F32) for i in range(NR)]
    b_Pb = tr.bufs(NR, "Pb")
    NRW, NRT = 6, 4
    KC, KD = 4, 2
    wbf = [sb(nc, es_cw, pfx + f"wbf{i}", [128, 512], BF16) for i in range(NRW)]
    b_wbf = tr.bufs(NRW, "wbf")
    wT = [sb(nc, es_cw, pfx + f"wT{i}", [128, 512], BF16) for i in range(NRT)]
    b_wT = tr.bufs(NRT, "wT")
    ya = [sb(nc, es_cw, pfx + f"ya{i}", [128, 512], BF16) for i in range(2)]
    b_ya = tr.bufs(2, "ya")
    zps = [psb[0], psb[1], psb[2]]
    wtp = [psb[3], psb[4]]
    wtp_bf = [w_[0][:].bitcast(BF16) for w_ in wtp]
    acc = [psb[5], psb[6]]

    items = []
    for tb in range(NB):
        hi = tb * 128 + 128
        first = True
        while hi > 0:
            lo = max(0, hi - 512)
            items.append(dict(tb=tb, lo=lo, hi=hi, W=hi - lo, first=first, last=(lo == 0)))
            hi = lo
            first = False
    n = len(items)

    def stA(g):
        it = items[g]
        s3 = g % NR
        tb, lo, hi, W = it["tb"], it["lo"], it["hi"], it["W"]
        t0 = tb * 128
        op("pe", lambda e: e.matmul(zps[s3][0][:, 0:W], lhsT=S0res[0:64, t0:t0 + 128], rhs=S1res[0:64, lo:hi],
                                    start=True, stop=True),
           reads=[b_S0q[tb // 4]] + [b_S1[k] for k in range(lo // 512, (hi - 1) // 512 + 1)],
           writes=[zps[s3][1]])

    def stB(g):
        it = items[g]
        s3 = g % NR
        W = it["W"]
        op("act", lambda e: e.activation(out=om[s3][:, 0:W], in_=zps[s3][0][:, 0:W], func=AF.Sigmoid, scale=-0.125),
           reads=[zps[s3][1]], writes=[b_om[s3]])
        if it["first"]:
            op("dve", lambda e: e.tensor_tensor(out=om[s3][:, W - 128:W], in0=om[s3][:, W - 128:W], in1=m01[:, :],
                                                op=ALU.max),
               reads=[b_om[s3]] + CONST, writes=[b_om[s3]])
            op("dve", lambda e: e.memset(Pb[s3][:, W:W + 1], 1.0), writes=[b_Pb[s3]])
        op("dve", lambda e: e.tensor_tensor_scan(out=Pb[s3][:, 0:W][:, ::-1], data0=om[s3][:, 0:W][:, ::-1],
                                                 data1=zeros[:, 0:W], initial=Pb[s3][:, W:W + 1],
                                                 op0=ALU.mult, op1=ALU.add),
           reads=[b_om[s3], b_Pb[s3]] + CONST, writes=[b_Pb[s3]])
        if not it["last"]:
            sn = (g + 1) % NR
            Wn = items[g + 1]["W"]
            op("dve", lambda e: e.tensor_copy(out=Pb[sn][:, Wn:Wn + 1], in_=Pb[s3][:, 0:1]),
               reads=[b_Pb[s3]], writes=[b_Pb[sn]])
        sw = g % NRW
        op("pool",
           lambda e: e.tensor_tensor(out=wbf[sw][:, 0:W], in0=Pb[s3][:, 1:W + 1], in1=Pb[s3][:, 0:W],
                                     op=ALU.subtract),
           reads=[b_Pb[s3]], writes=[b_wbf[sw]])

    def stC(g):
        it = items[g]
        sw = g % NRW
        st_ = g % NRT
        s2 = g % 2
        W = it["W"]
        for kbi in range(W // 128):
            op("pe", lambda e, kbi=kbi: e.transpose(wtp_bf[s2][:, kbi * 128:(kbi + 1) * 128],
                                                    wbf[sw][:, kbi * 128:(kbi + 1) * 128], identb[:, :]),
               reads=[b_wbf[sw]] + CONST, writes=[wtp[s2][1]])
        op("act", lambda e: e.activation(out=wT[st_][:, 0:W], in_=wtp_bf[s2][:, 0:W], func=AF.Copy),
           reads=[wtp[s2][1]], writes=[b_wT[st_]])

    def stD(g):
        it = items[g]
        s3 = g % NRT
        tb, lo, W = it["tb"], it["lo"], it["W"]
        ab = acc[(tb // 4) % 2]
        ac = slice((tb % 4) * 128, (tb % 4 + 1) * 128)
        nk = W // 128
        for kbi in range(nk):
            kb = lo // 128 + kbi
            op("pe", lambda e, kbi=kbi, kb=kb: e.matmul(
                ab[0][64:128, ac], lhsT=vsb[:, kb, :], rhs=wT[s3][:, kbi * 128:(kbi + 1) * 128],
                start=(it["first"] and kbi == 0), stop=(it["last"] and kbi == nk - 1)),
               reads=[b_vsb[kb], b_wT[s3]], writes=[ab[1]])
        if it["last"] and tb % 4 == 3:
            q4 = tb // 4
            i2 = q4 % 2
            cols = slice(q4 * 512, (q4 + 1) * 512)
            op("dve", lambda e: e.tensor_tensor(out=ya[i2][64:128, :], in0=ab[0][64:128, :],
                                                in1=S0res[64:128, cols], op=ALU.mult),
               reads=[ab[1], b_S0g[q4]], writes=[b_ya[i2]])
            ysink["a"](q4, ya[i2], b_ya[i2])

    for i in range(n + 1 + KC + KD):
        if i < n:
            stA(i)
        if 0 <= i - 1 < n:
            stB(i - 1)
        if 0 <= i - 1 - KC < n:
            stC(i - 1 - KC)
        if 0 <= i - 1 - KC - KD < n:
            stD(i - 1 - KC - KD)

    tr.barrier()
    es_cw.close()
    es_c.close()
    if "after_a" in ysink:
        ysink["after_a"]()


def emit_T(nc, tr, es0, psb, xt, xt_deps, yload, wg, wup, wout, lngb, cst, osink, pfx, use_pool=True):
    PE2 = "pool" if use_pool else "dve"
    op, dma = tr.op, tr.dma
    es_t = contextlib.ExitStack()
    es0.enter_context(es_t)
    ident = sb(nc, es_t, pfx + "ident", [128, 128], F32)
    lng = sb(nc, es_t, pfx + "lng", [128, D], F32)
    lnb = sb(nc, es_t, pfx + "lnb", [128, D], F32)
    b_c = tr.buf("tconst")
    dma("sp", ident[:], cst["ident"], b_c, writes=[b_c])
    dma("sp", lng[:], lngb[0:1, :].to_broadcast([128, D]), b_c, writes=[b_c])
    dma("sp", lnb[:], lngb[1:2, :].to_broadcast([128, D]), b_c, writes=[b_c])
    CONST = [b_c]
    wgb = sb(nc, es_t, pfx + "wgb", [128, 8, 3072], BF16)
    wupb = sb(nc, es_t, pfx + "wupb", [128, 6, D], BF16)
    woutb = sb(nc, es_t, pfx + "woutb", [128, 8, D], BF16)
    b_w = tr.buf("tw")
    stg = [sb(nc, es_t, pfx + f"stg{i}", [128, D], F32) for i in range(3)]
    b_stg = tr.bufs(3, "stg")
    cp_rr = [0]

    def cast_copy(out, in_, reads, writes):
        k = cp_rr[0] % 2
        cp_rr[0] += 1
        if k == 0:
            op("dve", lambda e: e.tensor_copy(out=out, in_=in_), reads=reads, writes=writes)
        elif k == 1:
            op("act", lambda e: e.activation(out=out, in_=in_, func=AF.Copy), reads=reads, writes=writes)
        else:
            op("pool", lambda e: e.tensor_copy(out=out, in_=in_), reads=reads, writes=writes)

    si = 0
    wg_v = wg.rearrange("(j p) c -> j p c", p=128)
    for j in range(8):
        for q in range(3):
            s = si % 3
            si += 1
            dma("sp", stg[s][:], wg_v[j][:, q * 1024:(q + 1) * 1024], b_stg[s], writes=[b_stg[s]])
            cast_copy(wgb[:, j, q * 1024:(q + 1) * 1024], stg[s][:], [b_stg[s]], [b_w])
    wup_v = wup.rearrange("(j p) c -> j p c", p=128)
    for j in range(6):
        s = si % 3
        si += 1
        dma("sp", stg[s][:], wup_v[j], b_stg[s], writes=[b_stg[s]])
        cast_copy(wupb[:, j, :], stg[s][:], [b_stg[s]], [b_w])
    wout_v = wout.rearrange("(j p) c -> j p c", p=128)
    for j in range(8):
        s = si % 3
        si += 1
        dma("sp", stg[s][:], wout_v[j], b_stg[s], writes=[b_stg[s]])
        cast_copy(woutb[:, j, :], stg[s][:], [b_stg[s]], [b_w])

    xres = sb(nc, es_t, pfx + "xres", [128, 4, D], F32)
    b_xres = tr.bufs(4, "xres")
    xT = sb(nc, es_t, pfx + "xT", [128, 8, 512], BF16)
    b_xT = [tr.bufs(2, "xT") for _ in range(4)]
    ybf = sb(nc, es_t, pfx + "ybf", [128, 6, 512], BF16)
    b_ybf = tr.bufs(6, "ybf")
    gsb = [sb(nc, es_t, pfx + f"gsb{i}", [128, 512], F32) for i in range(3)]
    b_gsb = tr.bufs(3, "gsb")
    tm_ = [sb(nc, es_t, pfx + f"tm{i}", [128, 512], F32) for i in range(2)]
    b_tm = tr.bufs(2, "tm")
    macc = [sb(nc, es_t, pfx + f"macc{i}", [128, 512], F32) for i in range(2)]
    b_macc = tr.bufs(2, "macc")
    mT = sb(nc, es_t, pfx + "mT", [128, 8, 512], BF16)
    b_mT = tr.bufs(8, "mT")
    rr = [sb(nc, es_t, pfx + f"rr{i}", [128, D], F32) for i in range(2)]
    b_rr = tr.bufs(2, "rr")
    sqj = [sb(nc, es_t, pfx + f"sqj{i}", [128, D], F32) for i in range(2)]
    b_sqj = tr.bufs(2, "sqj")
    st = [sb(nc, es_t, pfx + f"st{i}", [128, 8], F32) for i in range(2)]
    b_st = tr.bufs(2, "st")
    yo_ = [sb(nc, es_t, pfx + f"yo{i}", [128, D], F32) for i in range(2)]
    b_yo = tr.bufs(2, "yo")

    tp = [psb[0], psb[1]]
    gpb = [psb[2], psb[3], psb[4]]
    upb = [psb[5], psb[6]]
    outb = [psb[6], psb[7]]
    xt_v = xt.rearrange("(n p) d -> n p d", p=128)
    evr = [0]

    def evac_copy(out, in_, reads, writes):
        evr[0] ^= 1
        if evr[0]:
            op("act", lambda e: e.activation(out=out, in_=in_, func=AF.Copy), reads=reads, writes=writes)
        else:
            op("dve", lambda e: e.tensor_copy(out=out, in_=in_), reads=reads, writes=writes)

    gi = 0
    ui = 0
    for ti in range(TT // 512):
        cols = slice(ti * 512, (ti + 1) * 512)
        for bi in range(4):
            blk = ti * 4 + bi
            dma("sp", xres[:, bi, :], xt_v[blk], b_xres[bi], reads=xt_deps, writes=[b_xres[bi]])
            for half in range(2):
                bank = tp[half]
                for jj in range(4):
                    j = half * 4 + jj
                    op("pe", lambda e: e.transpose(bank[0][:, jj * 128:(jj + 1) * 128],
                                                   xres[:, bi, j * 128:(j + 1) * 128], ident[:]),
                       reads=[b_xres[bi]] + CONST, writes=[bank[1]])
                evac_copy(xT[:, half * 4:(half + 1) * 4, bi * 128:(bi + 1) * 128],
                          bank[0][:, :].rearrange("p (j t) -> p j t", t=128),
                          [bank[1]], [b_xT[bi][half]])
        for r in range(6):
            yload(r, ti, ybf[:, r, :], b_ybf[r])
        for dc in range(8):
            mi = dc % 2
            for nb_ in range(3):
                gb = gpb[gi % 3]
                gs = gi % 3
                gi += 1
                for j in range(8):
                    op("pe", lambda e: e.matmul(
                        gb[0][:, :], lhsT=wgb[:, j, nb_ * 1024 + dc * 128:nb_ * 1024 + (dc + 1) * 128],
                        rhs=xT[:, j, :], start=(j == 0), stop=(j == 7)),
                       reads=[b_w] + [b_xT[bi][j // 4] for bi in range(4)], writes=[gb[1]])
                op("act", lambda e: e.activation(out=gsb[gs][:, :], in_=gb[0][:, :], func=AF.Sigmoid),
                   reads=[gb[1]], writes=[b_gsb[gs]])
                ub = upb[ui % 2]
                ui += 1
                for r in range(2):
                    op("pe", lambda e: e.matmul(
                        ub[0][:, :], lhsT=wupb[:, nb_ * 2 + r, dc * 128:(dc + 1) * 128],
                        rhs=ybf[:, nb_ * 2 + r, :], start=(r == 0), stop=(r == 1)),
                       reads=[b_w, b_ybf[nb_ * 2 + r]], writes=[ub[1]])
                if nb_ == 0:
                    op("dve", lambda e: e.tensor_tensor(out=macc[mi][:, :], in0=gsb[gs][:, :], in1=ub[0][:, :],
                                                        op=ALU.mult),
                       reads=[b_gsb[gs], ub[1]], writes=[b_macc[mi]])
                else:
                    t2 = nb_ % 2
                    op("dve", lambda e: e.tensor_tensor(out=tm_[t2][:, :], in0=gsb[gs][:, :], in1=ub[0][:, :],
                                                        op=ALU.mult),
                       reads=[b_gsb[gs], ub[1]], writes=[b_tm[t2]])
                    if nb_ == 1:
                        op(PE2, lambda e: e.tensor_tensor(out=macc[mi][:, :], in0=macc[mi][:, :],
                                                             in1=tm_[t2][:, :], op=ALU.add),
                           reads=[b_macc[mi], b_tm[t2]], writes=[b_macc[mi]])
                    else:
                        op(PE2, lambda e: e.tensor_tensor(out=mT[:, dc, :], in0=macc[mi][:, :],
                                                             in1=tm_[t2][:, :], op=ALU.add),
                           reads=[b_macc[mi], b_tm[t2]], writes=[b_mT[dc]])
        for bi in range(4):
            blk = ti * 4 + bi
            i2 = blk % 2
            for hf in range(2):
                ob = outb[hf]
                for dc in range(8):
                    op("pe", lambda e: e.matmul(
                        ob[0][:, :], lhsT=mT[:, dc, bi * 128:(bi + 1) * 128],
                        rhs=woutb[:, dc, hf * 512:(hf + 1) * 512], start=(dc == 0), stop=(dc == 7)),
                       reads=[b_w, b_mT[dc]], writes=[ob[1]])
                op("dve", lambda e: e.scalar_tensor_tensor(
                    out=rr[i2][:, hf * 512:(hf + 1) * 512], in0=xres[:, bi, hf * 512:(hf + 1) * 512],
                    scalar=ALPHA, in1=ob[0][:, :], op0=ALU.mult, op1=ALU.add),
                   reads=[b_xres[bi], ob[1]], writes=[b_rr[i2]])
            op("dve", lambda e: e.reduce_sum(out=st[i2][:, 0:1], in_=rr[i2][:, :], axis=mybir.AxisListType.X),
               reads=[b_rr[i2]], writes=[b_st[i2]])
            op("dve", lambda e: e.tensor_scalar(out=st[i2][:, 1:2], in0=st[i2][:, 0:1], scalar1=-1.0 / D,
                                                scalar2=None, op0=ALU.mult),
               reads=[b_st[i2]], writes=[b_st[i2]])
            op("act", lambda e: e.activation(out=rr[i2][:, :], in_=rr[i2][:, :], func=AF.Identity,
                                             bias=st[i2][:, 1:2], scale=1.0),
               reads=[b_rr[i2], b_st[i2]], writes=[b_rr[i2]])
            op(PE2, lambda e: e.tensor_tensor(out=sqj[i2][:, :], in0=rr[i2][:, :], in1=rr[i2][:, :], op=ALU.mult),
               reads=[b_rr[i2]], writes=[b_sqj[i2]])
            op("dve", lambda e: e.reduce_sum(out=st[i2][:, 2:3], in_=sqj[i2][:, :], axis=mybir.AxisListType.X),
               reads=[b_sqj[i2], b_st[i2]], writes=[b_st[i2]])
            op("dve", lambda e: e.tensor_scalar(out=st[i2][:, 3:4], in0=st[i2][:, 2:3], scalar1=1.0 / D,
                                                scalar2=EPS, op0=ALU.mult, op1=ALU.add),
               reads=[b_st[i2]], writes=[b_st[i2]])
            op("act", lambda e: e.activation(out=st[i2][:, 5:6], in_=st[i2][:, 3:4], func=AF.Sqrt),
               reads=[b_st[i2]], writes=[b_st[i2]])
            op("dve", lambda e: e.reciprocal(out=st[i2][:, 4:5], in_=st[i2][:, 5:6]),
               reads=[b_st[i2]], writes=[b_st[i2]])
            op("dve", lambda e: e.scalar_tensor_tensor(out=yo_[i2][:, :], in0=rr[i2][:, :], scalar=st[i2][:, 4:5],
                                                       in1=lng[:, :], op0=ALU.mult, op1=ALU.mult),
               reads=[b_rr[i2], b_st[i2]] + CONST, writes=[b_yo[i2]])
            op(PE2, lambda e: e.tensor_tensor(out=yo_[i2][:, :], in0=yo_[i2][:, :], in1=lnb[:, :], op=ALU.add),
               reads=[b_yo[i2]] + CONST, writes=[b_yo[i2]])
            osink(blk, ti, bi, yo_[i2], b_yo[i2])
    tr.barrier()
    es_t.close()


def _consts():
    ident = np.eye(128, dtype=np.float32)
    p = np.arange(128)[:, None]
    cmask = ((p % 64) <= np.arange(64)[None, :]).astype(np.float32)
    m01 = (np.arange(128)[None, :] >= p).astype(np.float32)
    rmask = np.ones((128, 512), np.float32)
    rmask[:, ::64] = 0.0
    onesblk = np.zeros((128, 128), np.float32)
    onesblk[0:64, 0:64] = 1.0 / 64
    onesblk[64:128, 64:128] = 1.0 / 64
    return {"ident": ident, "cmask": cmask, "m01": m01, "rmask": rmask, "onesblk": onesblk}


_OFF = dict(a_q=0, a_k=256, a_v=512, a_g=768, h_f=1024, h_i=1280, h_q=1536, h_g=1792,
            c_q=2048, c_k=2176, c_v=2304, c_g=2560, c_r=2816, gates=2832)


def _h_weights(w_in_l, h):
    def col(name, width):
        o = _OFF[name] + h * width
        return w_in_l[:, o:o + width]
    z = lambda n: np.zeros((D, n), np.float32)
    wfm = np.concatenate([
        col("a_q", 64), col("a_g", 64),
        col("a_k", 64), w_in_l[:, _OFF["c_r"]:_OFF["c_r"] + 16], z(48),
        col("h_q", 64), col("c_q", 32), z(32),
        col("h_f", 64), col("c_k", 32), z(32),
        col("h_g", 64), col("c_g", 64),
    ], axis=1)
    wtm = np.concatenate([col("a_v", 64), col("h_i", 64), col("c_v", 64)], axis=1)
    return np.ascontiguousarray(wfm), np.ascontiguousarray(wtm)


def _h_params(inp, layer, h):
    pp = np.zeros((128, 8), np.float32)
    pp[0:64, 0] = inp["hgrn_lb_logits"][0, h * 64:(h + 1) * 64]
    pp[0:64, 1] = inp["hgrn_lb_logits"][1, h * 64:(h + 1) * 64]
    pp[64:96, 2] = inp["gla_gate_b"][layer, h * 32:(h + 1) * 32]
    pp[0:64, 3] = inp["hgrn_norm_g"][layer, h * 64:(h + 1) * 64]
    pp[64:128, 3] = inp["gla_norm_g"][layer, h * 64:(h + 1) * 64]
    pp[:, 4] = 1.0
    pp[64:96, 4] = 32.0 ** -0.5
    w2 = np.zeros((128, 32), np.float32)
    w2[64:80, :] = inp["gla_gate_w2"][layer][:, h * 32:(h + 1) * 32]
    return pp, w2


def _psum_banks(nc, tr, es):
    banks = []
    for i in range(8):
        t = es.enter_context(nc.psum_tensor(f"ps{i}", [128, 512], F32))
        banks.append((t, tr.buf("ps")))
    return banks


_CACHE = {}
HC = ("ident", "cmask", "m01", "rmask", "onesblk")
I32 = mybir.dt.int32
RG = [[0, 1, 2, 3], [4, 5, 6, 7]]


def build_fused():
    if "F" in _CACHE:
        return _CACHE["F"]
    nc = bass.Bass("TRN2", target_bir_lowering=False)
    ein = lambda name, shape: nc.dram_tensor(name, shape, F32, kind="ExternalInput").ap()
    xs = ein("xs", [T, D])
    xt = ein("xt", [TT, D])
    oh_in = ein("oh", [128, 8])
    cst = {k: ein("c_" + k, list(v.shape)) for k, v in _consts().items()}
    LW = []
    for l in range(2):
        LW.append(dict(wfm=ein(f"wfm{l}", [D, 640]), wtm=ein(f"wtm{l}", [D, 192]), pp=ein(f"pp{l}", [128, 8]),
                       w2=ein(f"w2{l}", [128, 32]), wg=ein(f"wg{l}", [D, 3072]), wup=ein(f"wup{l}", [768, D]),
                       wout=ein(f"wout{l}", [D, D]), lngb=ein(f"lngb{l}", [2, D])))
    xo = nc.dram_tensor("xo", [TT, D], F32, kind="ExternalOutput").ap()

    ybc_in = [nc.dram_tensor(f"ybc_in{q}", [512, 1024], F32) for q in range(4)]
    ya_in = [nc.dram_tensor(f"ya_in{q}", [256, 1024], F32) for q in range(4)]
    ybc_out = [nc.dram_tensor(f"ybc_out{q}", [512, 1024], F32) for q in range(4)]
    ya_out = [nc.dram_tensor(f"ya_out{q}", [256, 1024], F32) for q in range(4)]
    x1_in = [[nc.dram_tensor(f"x1_in{t}_{hf}", [2048, 256], F32) for hf in range(2)] for t in range(4)]
    x1_g = [[nc.dram_tensor(f"x1_g{t}_{hf}", [2048, 256], F32) for hf in range(2)] for t in range(4)]
    x1loc = nc.dram_tensor("x1loc", [TT, D], F32).ap()
    bf = lambda t: t.ap().bitcast(BF16)
    bf3 = lambda t: t.ap().bitcast(BF16).rearrange("(r p) c -> r p c", r=4)
    dyn = lambda v: v.rearrange("o p c -> (o p) c")

    with contextlib.ExitStack() as es:
        tr = Tr(nc, es)
        op, dma = tr.op, tr.dma
        psb = _psum_banks(nc, tr, es)

        b_ybc_in = tr.bufs(4, "ybc_in")
        b_ya_in = tr.bufs(4, "ya_in")
        b_ybc_out = tr.bufs(4, "ybc_out")
        b_ya_out = tr.bufs(4, "ya_out")
        b_x1_in = tr.bufs(4, "x1_in")
        b_x1_g = tr.bufs(4, "x1_g")
        b_x1loc = tr.buf("x1loc")

        oh = sb(nc, es, "oh_sb", [128, 8], F32)
        b_oh = tr.buf("oh")
        dma("sp", oh[:], oh_in, b_oh, writes=[b_oh])
        slot = [sb(nc, es, f"slot{i}", [128, 512], BF16) for i in range(4)]
        b_slot = tr.bufs(4, "slot")
        sl = [0]

        def scatter(tile_ap, nfree, dsts, reads, rows=slice(0, 128)):
            for j in range(4):
                k = sl[0] % 4
                sl[0] += 1
                op("act", lambda e: e.activation(out=slot[k][rows, 0:nfree], in_=tile_ap, func=AF.Identity,
                                                 scale=oh[rows, j:j + 1], bias=oh[rows, 4:5]),
                   reads=list(reads) + [b_oh], writes=[b_slot[k]])
                for (dram_ap, srows, dbuf) in dsts[j]:
                    dma("sp", dram_ap, slot[k][srows, 0:nfree], b_slot[k], reads=[b_slot[k]], writes=[dbuf])

        def run_layer(l):
            W = LW[l]

            def sink_bc(ti, tile, buf):
                q, c0 = ti // 4, (ti % 4) * 512
                v = bf(ybc_in[q])
                scatter(tile[:, :], 512,
                        [[(v[j * 64:(j + 1) * 64, c0:c0 + 512], slice(0, 64), b_ybc_in[q]),
                          (v[256 + j * 64:256 + (j + 1) * 64, c0:c0 + 512], slice(64, 128), b_ybc_in[q])]
                         for j in range(4)], [buf])

            def sink_a(q4, tile, buf):
                q, c0 = q4 // 4, (q4 % 4) * 512
                v = bf(ya_in[q])
                scatter(tile[64:128, :], 512,
                        [[(v[j * 64:(j + 1) * 64, c0:c0 + 512], slice(64, 128), b_ya_in[q])] for j in range(4)], [buf],
                        rows=slice(64, 128))

            def after_bc():
                for q in range(4):
                    tr.coll(lambda e: e.collective_compute(
                        "AllReduce", ALU.add, replica_groups=RG,
                        ins=[ybc_in[q].ap().opt()], outs=[ybc_out[q].ap().opt()]),
                        reads=[b_ybc_in[q]], writes=[b_ybc_out[q]])

            def after_a():
                for q in range(4):
                    tr.coll(lambda e: e.collective_compute(
                        "AllReduce", ALU.add, replica_groups=RG,
                        ins=[ya_in[q].ap().opt()], outs=[ya_out[q].ap().opt()]),
                        reads=[b_ya_in[q]], writes=[b_ya_out[q]])

            if l == 0:
                xsrc = ("f32", xs)
            else:
                def xblk(blk):
                    rk, t, bi = blk // 16, (blk // 4) % 4, blk % 4
                    return [bf(x1_g[t][hf])[rk * 512 + bi * 128:rk * 512 + (bi + 1) * 128, :] for hf in range(2)]
                xsrc = ("bf16", xblk, lambda blk: [b_x1_g[(blk // 4) % 4]])
            emit_H(nc, tr, es, psb, l, xsrc, W["wfm"], W["wtm"], W["pp"], W["w2"], cst,
                   dict(bc=sink_bc, a=sink_a, after_bc=after_bc, after_a=after_a), f"h{l}_")

            def yload(r6, ti, dst, buf):
                c0 = ti * 512
                for j in range(4):
                    if r6 < 2:
                        src = bf(ya_out[j])[r6 * 128:(r6 + 1) * 128, c0:c0 + 512]
                        sbuf_ = b_ya_out[j]
                    else:
                        src = bf(ybc_out[j])[(r6 - 2) * 128:(r6 - 1) * 128, c0:c0 + 512]
                        sbuf_ = b_ybc_out[j]
                    k = sl[0] % 4
                    sl[0] += 1
                    dma("sp", slot[k][:, 0:512], src, b_slot[k], reads=[sbuf_], writes=[b_slot[k]])
                    eng = "dve"
                    if j == 0:
                        op(eng, lambda e: e.tensor_scalar(out=dst, in0=slot[k][:, 0:512], scalar1=oh[:, 0:1],
                                                          scalar2=None, op0=ALU.mult),
                           reads=[b_slot[k], b_oh], writes=[buf])
                    else:
                        op("dve", lambda e: e.scalar_tensor_tensor(out=dst, in0=slot[k][:, 0:512], scalar=oh[:, j:j + 1],
                                                                 in1=dst, op0=ALU.mult, op1=ALU.add),
                           reads=[b_slot[k], b_oh, buf], writes=[buf])

            if l == 0:
                def osink(blk, ti, bi, tile, buf):
                    dma("sp", x1loc[blk * 128:(blk + 1) * 128, :], tile[:, :], buf, reads=[buf], writes=[b_x1loc])
                    for hf in range(2):
                        v = bf(x1_in[ti][hf])
                        scatter(tile[:, hf * 512:(hf + 1) * 512], 512,
                                [[(v[j * 512 + bi * 128:j * 512 + (bi + 1) * 128, :],
                                   slice(0, 128), b_x1_in[ti])] for j in range(4)], [buf])
                    if bi == 3:
                        for hf in range(2):
                            tr.coll(lambda e: e.collective_compute(
                                "AllReduce", ALU.add, replica_groups=RG,
                                ins=[x1_in[ti][hf].ap().opt()], outs=[x1_g[ti][hf].ap().opt()]),
                                reads=[b_x1_in[ti]], writes=[b_x1_g[ti]])
                emit_T(nc, tr, es, psb, xt, [], yload, W["wg"], W["wup"], W["wout"], W["lngb"], cst,
                       osink, f"t{l}_", use_pool=False)
            else:
                def osink(blk, ti, bi, tile, buf):
                    dma("sp", xo[blk * 128:(blk + 1) * 128, :], tile[:, :], buf, reads=[buf])
                emit_T(nc, tr, es, psb, x1loc, [b_x1loc], yload, W["wg"], W["wup"], W["wout"], W["lngb"], cst,
                       osink, f"t{l}_")

        for l in range(2):
            run_layer(l)
        tr.finish()
    _CACHE["F"] = nc
    return nc


def kernel(**inputs):
    inp = {k: np.asarray(v, dtype=np.float32) for k, v in inputs.items()}
    nc = build_fused()
    c = _consts()
    x = inp["x"]
    xf = x.reshape(2 * T, D)
    shared = {"c_" + k: v for k, v in c.items()}
    for l in range(2):
        shared[f"wg{l}"] = np.ascontiguousarray(inp["w_in"][l][:, _OFF["gates"]:_OFF["gates"] + 3072])
        shared[f"wup{l}"] = np.ascontiguousarray(inp["w_up"][l].reshape(768, D))
        shared[f"wout{l}"] = np.ascontiguousarray(inp["w_out"][l])
        shared[f"lngb{l}"] = np.stack([inp["ln_g"][l], inp["ln_b"][l]]).astype(np.float32)
    maps = []
    for core in range(NCORE):
        b, h = core // 4, core % 4
        m = dict(shared)
        m["xs"] = np.ascontiguousarray(x[b])
        m["xt"] = np.ascontiguousarray(xf[core * TT:(core + 1) * TT])
        ohm = np.zeros((128, 8), np.float32)
        ohm[:, h] = 1.0
        m["oh"] = ohm
        for l in range(2):
            wfm, wtm = _h_weights(inp["w_in"][l], h)
            pp, w2 = _h_params(inp, l, h)
            m[f"wfm{l}"], m[f"wtm{l}"], m[f"pp{l}"], m[f"w2{l}"] = wfm, wtm, pp, w2
        maps.append(m)
    res = run_bass_kernel_spmd(nc, maps, core_ids=list(range(NCORE)))
    out = np.concatenate([res.results[core]["xo"] for core in range(NCORE)], axis=0)
    return out.reshape(2, T, D).astype(np.float32)
```

```python
import contextlib
import numpy as np
import concourse.bass as bass
import concourse.mybir as mybir
from concourse.bass_utils import run_bass_kernel_spmd

F32 = mybir.dt.float32
BF16 = mybir.dt.bfloat16
AF = mybir.ActivationFunctionType
ALU = mybir.AluOpType

_STOP = ""
_KNT = 16
_KB = 9
T = 8192
D = 1024
NCORE = 8
TT = 2048
EPS = 1e-5
ALPHA = 4.0 ** 0.25


class Buf:
    __slots__ = ("name", "w", "r", "dsem", "dcnt")

    def __init__(self, name):
        self.name = name
        self.w = None
        self.r = {}
        self.dsem = None
        self.dcnt = 0


class Tr:
    NDP = 18

    def __init__(self, nc, es):
        self.nc = nc
        self.es = es
        self.engs = {"pe": nc.tensor, "act": nc.scalar, "dve": nc.vector,
                     "pool": nc.gpsimd, "sp": nc.sync}
        self.sem = {k: es.enter_context(nc.semaphore("sem_" + k)) for k in self.engs}
        self.cnt = {k: 0 for k in self.engs}
        self.waited = {k: {} for k in self.engs}
        self.dpool = [[es.enter_context(nc.semaphore(f"dsem{i}")), 0] for i in range(self.NDP)]
        self.free = list(range(self.NDP))
        self.csem = es.enter_context(nc.semaphore("csem"))
        self.ccnt = 0
        self.owners = []
        self.all_bufs = []
        self.nbuf = 0

    def buf(self, name="b"):
        self.nbuf += 1
        b = Buf(f"{name}{self.nbuf}")
        self.all_bufs.append(b)
        return b

    def bufs(self, n, name="b"):
        return [self.buf(name) for _ in range(n)]

    def _waits(self, e, reads, writes):
        need = {}

        def add(tag):
            key, sem, val = tag
            if key not in need or need[key][1] < val:
                need[key] = (sem, val)

        for b in reads:
            if b.w is not None:
                add(b.w)
        for b in writes:
            if b.w is not None:
                add(b.w)
            for tag in b.r.values():
                add(tag)
        for key, (sem, val) in need.items():
            if key == "pe" and e == "pe":
                continue
            if self.waited[e].get(key, 0) >= val:
                continue
            self.engs[e].wait_ge(sem, val)
            self.waited[e][key] = val

    def _mark(self, tag, reads, writes):
        for b in reads:
            b.r[tag[0]] = tag
        for b in writes:
            b.w = tag
            b.r = {}

    def op(self, e, fn, reads=(), writes=(), serial=False):
        self._waits(e, reads, writes)
        if serial and self.cnt[e] > 0 and self.waited[e].get("self", 0) < self.cnt[e]:
            self.engs[e].wait_ge(self.sem[e], self.cnt[e])
            self.waited[e]["self"] = self.cnt[e]
        inst = fn(self.engs[e])
        self.cnt[e] += 1
        inst.then_inc(self.sem[e], 1)
        self._mark((e, self.sem[e], self.cnt[e]), reads, writes)

    def dma(self, q, out, in_, owner, reads=(), writes=()):
        self._waits(q, reads, writes)
        if owner.dsem is None:
            owner.dsem = self.free.pop(0)
            self.owners.append(owner)
        slot = self.dpool[owner.dsem]
        self.engs[q].dma_start(out=out, in_=in_).then_inc(slot[0], 16)
        slot[1] += 16
        self._mark((f"p{owner.dsem}", slot[0], slot[1]), reads, writes)

    def retag(self, bufs, owner):
        slot = self.dpool[owner.dsem]
        for b in bufs:
            b.w = (f"p{owner.dsem}", slot[0], slot[1])

    def coll(self, fn, reads=(), writes=()):
        self._waits("pool", reads, writes)
        fn(self.engs["pool"]).then_inc(self.csem)
        self.ccnt += 1
        self._mark(("coll", self.csem, self.ccnt), reads, writes)

    def _sync_all(self, e):
        for e2 in self.engs:
            if e2 == e or self.cnt[e2] == 0:
                continue
            if self.waited[e].get(e2, 0) >= self.cnt[e2]:
                continue
            self.engs[e].wait_ge(self.sem[e2], self.cnt[e2])
            self.waited[e][e2] = self.cnt[e2]
        for i, (sem, cnt) in enumerate(self.dpool):
            if cnt == 0 or self.waited[e].get(f"p{i}", 0) >= cnt:
                continue
            self.engs[e].wait_ge(sem, cnt)
            self.waited[e][f"p{i}"] = cnt
        if self.ccnt and self.waited[e].get("coll", 0) < self.ccnt:
            self.engs[e].wait_ge(self.csem, self.ccnt)
            self.waited[e]["coll"] = self.ccnt

    def barrier(self):
        for e in self.engs:
            self._sync_all(e)
        for o in self.owners:
            self.free.append(o.dsem)
            o.dsem = None
        self.owners = []
        self.free.sort()
        for b in self.all_bufs:
            b.w = None
            b.r = {}

    def finish(self):
        self._sync_all("sp")


def sb(nc, es, name, shape, dt):
    return es.enter_context(nc.sbuf_tensor(name, shape, dt))


def emit_H(nc, tr, es0, psb, layer, xsrc, wfm, wtm, pp, w2, cst, ysink, pfx):
    op, dma = tr.op, tr.dma
    NB, NT, NCH = T // 128, T // 512, T // 64

    es_c = contextlib.ExitStack()
    es0.enter_context(es_c)
    ident = sb(nc, es_c, pfx + "ident", [128, 128], F32)
    identb = sb(nc, es_c, pfx + "identb", [128, 128], BF16)
    cmask = sb(nc, es_c, pfx + "cmask", [128, 64], F32)
    m01 = sb(nc, es_c, pfx + "m01", [128, 128], F32)
    rmask = sb(nc, es_c, pfx + "rmask", [128, 512], F32)
    onesblk = sb(nc, es_c, pfx + "onesblk", [128, 128], F32)
    zeros = sb(nc, es_c, pfx + "zeros", [128, 512], F32)
    ppt = sb(nc, es_c, pfx + "ppt", [128, 8], F32)
    pp2 = sb(nc, es_c, pfx + "pp2", [128, 8], F32)
    w2f = sb(nc, es_c, pfx + "w2f", [128, 32], F32)
    w2b = sb(nc, es_c, pfx + "w2b", [128, 32], BF16)
    b_c = tr.buf("const")
    for tl, nm in ((ident, "ident"), (cmask, "cmask"), (m01, "m01"), (rmask, "rmask"),
                   (onesblk, "onesblk")):
        dma("sp", tl[:], cst[nm], b_c, writes=[b_c])
    dma("sp", ppt[:], pp, b_c, writes=[b_c])
    dma("sp", w2f[:], w2, b_c, writes=[b_c])
    b_c2 = tr.buf("const2")
    op("pool", lambda e: e.memset(zeros[:], 0.0), writes=[b_c2])
    op("pool", lambda e: e.memset(pp2[:], 0.0), writes=[b_c2])
    op("pool", lambda e: e.memset(pp2[:, 5:6], EPS), reads=[b_c2], writes=[b_c2])
    op("pool", lambda e: e.memset(pp2[:, 6:7], 1.0), reads=[b_c2], writes=[b_c2])
    op("dve", lambda e: e.tensor_copy(out=identb[:], in_=ident[:]), reads=[b_c], writes=[b_c2])
    op("dve", lambda e: e.tensor_copy(out=w2b[:], in_=w2f[:]), reads=[b_c], writes=[b_c2])
    if layer == 0:
        op("pool", lambda e: e.memset(pp2[:, 1:2], 1.0), reads=[b_c2], writes=[b_c2])
    else:
        op("dve", lambda e: e.tensor_tensor(out=pp2[:, 3:4], in0=ppt[:, 0:1], in1=ppt[:, 1:2],
                                            op=ALU.subtract), reads=[b_c, b_c2], writes=[b_c2])
        op("act", lambda e: e.activation(out=pp2[:, 4:5], in_=pp2[:, 3:4], func=AF.Exp),
           reads=[b_c2], writes=[b_c2])
        op("dve", lambda e: e.tensor_scalar(out=pp2[:, 4:5], in0=pp2[:, 4:5], scalar1=1.0,
                                            scalar2=None, op0=ALU.add), reads=[b_c2], writes=[b_c2])
        op("dve", lambda e: e.reciprocal(out=pp2[:, 0:1], in_=pp2[:, 4:5]), reads=[b_c2], writes=[b_c2])
        op("dve", lambda e: e.tensor_scalar(out=pp2[:, 1:2], in0=pp2[:, 0:1], scalar1=-1.0,
                                            scalar2=1.0, op0=ALU.mult, op1=ALU.add),
           reads=[b_c2], writes=[b_c2])
    op("dve", lambda e: e.tensor_scalar(out=pp2[:, 2:3], in0=ppt[:, 2:3], scalar1=-1.0,
                                        scalar2=None, op0=ALU.mult), reads=[b_c, b_c2], writes=[b_c2])
    CONST = [b_c, b_c2]


    S0res = sb(nc, es_c, pfx + "S0res", [128, T], BF16)
    S1res = sb(nc, es_c, pfx + "S1res", [128, T], BF16)
    vsb = sb(nc, es_c, pfx + "vsb", [128, NB, 64], BF16)
    b_S0q = tr.bufs(NT, "S0q")
    b_S0g = tr.bufs(NT, "S0g")
    b_S1 = tr.bufs(NT, "S1")
    b_vsb = tr.bufs(NB, "vsb")

    es_r = contextlib.ExitStack()
    vrec = sb(nc, es_r, pfx + "vrec", [128, NB, 128], BF16)
    qdec = sb(nc, es_r, pfx + "qdec", [128, T], BF16)
    kdec = sb(nc, es_r, pfx + "kdec", [128, T], BF16)
    kdsT = sb(nc, es_r, pfx + "kdsT", [128, NB, 96], BF16)
    RGres = sb(nc, es_r, pfx + "RGres", [128, T], BF16)
    U = sb(nc, es_r, pfx + "U", [128, T], F32)
    Dres = sb(nc, es_r, pfx + "Dres", [128, NCH], F32)
    b_vrec = tr.bufs(NB, "vrec")
    b_qdec = tr.bufs(NT, "qdec")
    b_kdec = tr.bufs(NT, "kdec")
    b_kdsT = tr.bufs(NT, "kdsT")
    b_RG = tr.bufs(NT, "RG")
    b_U = tr.buf("U")
    b_D = tr.buf("D")

    es_a = contextlib.ExitStack()
    xst = [sb(nc, es_a, pfx + f"xst{i}", [128, D], F32) for i in range(2)]
    b_xst = tr.bufs(2, "xst")
    xbf = xsrc[0] == "bf16"
    xstb = [xt_[:].bitcast(BF16) for xt_ in xst]
    xT = sb(nc, es_a, pfx + "xT", [128, 8, 512], BF16)
    b_xT = [tr.bufs(2, "xT") for _ in range(4)]
    wfmb = sb(nc, es_a, pfx + "wfmb", [128, 8, 640], BF16)
    wtmb = sb(nc, es_a, pfx + "wtmb", [128, 8, 192], BF16)
    b_w = tr.buf("w")
    NTMP = 9
    tmp = [sb(nc, es_a, pfx + f"tmp{i}", [128, 512], F32) for i in range(NTMP)]
    bh = tr.bufs(NTMP, "tmph")
    bc = tr.bufs(NTMP, "tmpc")
    kds = sb(nc, es_a, pfx + "kds", [128, 512], BF16)
    b_kds = tr.buf("kds")
    (t_sig, t_f, t_g, t_kk, t_q, t_cum, t_e1, t_e2, t_d) = range(NTMP)

    wfm_v = wfm.rearrange("(j p) c -> j p c", p=128)
    wtm_v = wtm.rearrange("(j p) c -> j p c", p=128)
    for j in range(8):
        s = j % 2
        dma("sp", xst[s][:, 0:640], wfm_v[j], b_xst[s], writes=[b_xst[s]])
        dma("sp", xst[s][:, 640:832], wtm_v[j], b_xst[s], writes=[b_xst[s]])
        if j % 2 == 0:
            op("dve", lambda e, j=j, s=s: e.tensor_copy(out=wfmb[:, j, :], in_=xst[s][:, 0:640]),
               reads=[b_xst[s]], writes=[b_w])
        else:
            op("act", lambda e, j=j, s=s: e.activation(out=wfmb[:, j, :], in_=xst[s][:, 0:640], func=AF.Copy),
               reads=[b_xst[s]], writes=[b_w])
        op("pool", lambda e, j=j, s=s: e.tensor_copy(out=wtmb[:, j, :], in_=xst[s][:, 640:832]),
           reads=[b_xst[s]], writes=[b_w])

    tp = [psb[0], psb[1]]
    fm = [psb[2], psb[3]]
    tmq = [psb[4], psb[5]]
    ktp = psb[6]
    ups = psb[7]
    ktp_bf = ktp[0][:].bitcast(BF16)

    xs_v = None if xbf else xsrc[1].rearrange("(n p) d -> n p d", p=128)
    tp_bf = [psb[0][0][:].bitcast(BF16), psb[1][0][:].bitcast(BF16)]
    evac_rr = [0]

    def evac_copy(out, in_, reads, writes):
        evac_rr[0] ^= 1
        if evac_rr[0]:
            op("act", lambda e: e.activation(out=out, in_=in_, func=AF.Copy), reads=reads, writes=writes)
        else:
            op("dve", lambda e: e.tensor_copy(out=out, in_=in_), reads=reads, writes=writes)

    fmi = [0]

    def fm_chunk(ci):
        bank = fm[fmi[0] % 2]
        fmi[0] += 1
        for j in range(8):
            op("pe", lambda e, j=j, bank=bank: e.matmul(bank[0][:, :], lhsT=wfmb[:, j, ci * 128:(ci + 1) * 128],
                                                        rhs=xT[:, j, :], start=(j == 0), stop=(j == 7)),
               reads=[b_w] + [b_xT[bi][j // 4] for bi in range(4)], writes=[bank[1]])
        return bank

    for ti in range(min(NT, _KNT)):
        cols = slice(ti * 512, (ti + 1) * 512)
        for bi in range(4):
            blk = ti * 4 + bi
            s = blk % 2
            if xbf:
                for hf, src_ap in enumerate(xsrc[1](blk)):
                    dma("sp", xstb[s][:, hf * 512:(hf + 1) * 512], src_ap, b_xst[s], reads=xsrc[2](blk),
                        writes=[b_xst[s]])
            else:
                dma("sp", xst[s][:], xs_v[blk], b_xst[s], writes=[b_xst[s]])
            for half in range(2):
                bank = tp[half]
                for jj in range(4):
                    j = half * 4 + jj
                    if xbf:
                        op("pe", lambda e: e.transpose(
                            tp_bf[half][:, jj * 128:(jj + 1) * 128], xstb[s][:, j * 128:(j + 1) * 128], identb[:]),
                           reads=[b_xst[s]] + CONST, writes=[bank[1]])
                    else:
                        op("pe", lambda e: e.transpose(
                            bank[0][:, jj * 128:(jj + 1) * 128], xst[s][:, j * 128:(j + 1) * 128], ident[:]),
                           reads=[b_xst[s]] + CONST, writes=[bank[1]])
                src_ = tp_bf[half][:, 0:512] if xbf else bank[0][:, :]
                evac_copy(xT[:, half * 4:(half + 1) * 4, bi * 128:(bi + 1) * 128],
                          src_.rearrange("p (j t) -> p j t", t=128),
                          [bank[1]], [b_xT[bi][half]])
            tmb = tmq[bi % 2]
            for j in range(8):
                op("pe", lambda e: e.matmul(
                    tmb[0][:, 0:192], lhsT=xT[:, j, bi * 128:(bi + 1) * 128], rhs=wtmb[:, j, :],
                    start=(j == 0), stop=(j == 7)),
                   reads=[b_w, b_xT[bi][j // 4]], writes=[tmb[1]])
            if bi % 2 == 0:
                op("act", lambda e: e.activation(out=vsb[:, blk, :], in_=tmb[0][:, 0:64], func=AF.Copy),
                   reads=[tmb[1]], writes=[b_vsb[blk]])
                op("act", lambda e: e.activation(out=vrec[:, blk, :], in_=tmb[0][:, 64:192], func=AF.Copy),
                   reads=[tmb[1]], writes=[b_vrec[blk]])
            else:
                op("dve", lambda e: e.tensor_copy(out=vsb[:, blk, :], in_=tmb[0][:, 0:64]),
                   reads=[tmb[1]], writes=[b_vsb[blk]])
                op("dve", lambda e: e.tensor_copy(out=vrec[:, blk, :], in_=tmb[0][:, 64:192]),
                   reads=[tmb[1]], writes=[b_vrec[blk]])

        bank = fm_chunk(0)
        op("dve", lambda e, bank=bank: e.tensor_copy(out=S0res[0:64, cols], in_=bank[0][0:64, :]),
           reads=[bank[1]], writes=[b_S0q[ti]])
        op("act", lambda e, bank=bank: e.activation(out=S0res[64:128, cols], in_=bank[0][64:128, :], func=AF.Silu),
           reads=[bank[1]], writes=[b_S0g[ti]])
        bank = fm_chunk(3)
        op("act", lambda e, bank=bank: e.activation(out=tmp[t_sig][0:64, :], in_=bank[0][0:64, :], func=AF.Sigmoid),
           reads=[bank[1]], writes=[bh[t_sig]])
        op("act", lambda e, bank=bank: e.activation(out=tmp[t_kk][64:96, :], in_=bank[0][64:96, :], func=AF.Copy),
           reads=[bank[1]], writes=[bc[t_kk]])
        op("dve", lambda e: e.tensor_scalar(out=tmp[t_f][0:64, :], in0=tmp[t_sig][0:64, :],
                                            scalar1=pp2[0:64, 1:2], scalar2=pp2[0:64, 0:1],
                                            op0=ALU.mult, op1=ALU.add),
           reads=[bh[t_sig]] + CONST, writes=[bh[t_f]])
        bank = fm_chunk(4)
        op("act", lambda e, bank=bank: e.activation(out=RGres[:, cols], in_=bank[0][:, :], func=AF.Silu),
           reads=[bank[1]], writes=[b_RG[ti]])
        bank = fm_chunk(1)
        op("dve", lambda e, bank=bank: e.tensor_copy(out=S1res[0:96, cols], in_=bank[0][0:96, :]),
           reads=[bank[1]], writes=[b_S1[ti]])
        gp = fm[fmi[0] % 2]
        fmi[0] += 1
        op("pe", lambda e: e.matmul(gp[0][64:96, :], lhsT=w2b[64:80, :], rhs=S1res[64:80, cols],
                                    start=True, stop=True),
           reads=[b_S1[ti]] + CONST, writes=[gp[1]])
        op("act", lambda e: e.activation(out=tmp[t_e1][64:96, :], in_=gp[0][64:96, :], func=AF.Exp,
                                         scale=-1.0, bias=pp2[64:96, 2:3]),
           reads=[gp[1]] + CONST, writes=[bc[t_e1]])
        op("act", lambda e: e.activation(out=tmp[t_e2][64:96, :], in_=tmp[t_e1][64:96, :], func=AF.Ln,
                                         bias=pp2[64:96, 6:7]),
           reads=[bc[t_e1]] + CONST, writes=[bc[t_e2]])
        op("act", lambda e: e.activation(out=tmp[t_g][64:96, :], in_=tmp[t_e2][64:96, :], func=AF.Copy,
                                         scale=-1.0 / 16.0),
           reads=[bc[t_e2]], writes=[bc[t_g]])
        op("act", lambda e: e.activation(out=tmp[t_g][0:64, :], in_=tmp[t_f][0:64, :], func=AF.Ln),
           reads=[bh[t_f]], writes=[bh[t_g]])
        op("pool", lambda e: e.tensor_scalar(out=tmp[t_kk][0:64, :], in0=tmp[t_f][0:64, :],
                                             scalar1=-1.0, scalar2=1.0, op0=ALU.mult, op1=ALU.add),
           reads=[bh[t_f]], writes=[bh[t_kk]])
        bank = fm_chunk(2)
        op("act", lambda e, bank=bank: e.activation(out=tmp[t_q][0:96, :], in_=bank[0][0:96, :], func=AF.Identity,
                                                    scale=ppt[0:96, 4:5]),
           reads=[bank[1]] + CONST, writes=[bh[t_q], bc[t_q]])
        op("dve", lambda e: e.tensor_tensor_scan(out=tmp[t_cum][0:96, :], data0=rmask[0:96, :],
                                                 data1=tmp[t_g][0:96, :], initial=0.0,
                                                 op0=ALU.mult, op1=ALU.add),
           reads=[bh[t_g], bc[t_g]] + CONST, writes=[bh[t_cum], bc[t_cum]])
        op("act", lambda e: e.activation(out=tmp[t_e1][0:96, :], in_=tmp[t_cum][0:96, :], func=AF.Exp),
           reads=[bh[t_cum], bc[t_cum]], writes=[bh[t_e1], bc[t_e1]])
        op("pool", lambda e: e.tensor_tensor(out=qdec[0:96, cols], in0=tmp[t_q][0:96, :],
                                             in1=tmp[t_e1][0:96, :], op=ALU.mult),
           reads=[bh[t_q], bc[t_q], bh[t_e1], bc[t_e1]], writes=[b_qdec[ti]])
        op("act", lambda e: e.activation(out=tmp[t_e2][0:96, :], in_=tmp[t_cum][0:96, :], func=AF.Exp,
                                         scale=-1.0),
           reads=[bh[t_cum], bc[t_cum]], writes=[bh[t_e2], bc[t_e2]])
        op("dve", lambda e: e.tensor_tensor(out=kdec[0:96, cols], in0=tmp[t_kk][0:96, :],
                                            in1=tmp[t_e2][0:96, :], op=ALU.mult),
           reads=[bh[t_kk], bc[t_kk], bh[t_e2], bc[t_e2]], writes=[b_kdec[ti]])
        cum3 = tmp[t_cum][0:96, :].rearrange("p (c t) -> p c t", t=64)
        d3 = tmp[t_d][0:96, :].rearrange("p (c t) -> p c t", t=64)
        op("pool", lambda e: e.tensor_tensor(out=d3, in0=cum3[:, :, 63:64].to_broadcast([96, 8, 64]),
                                             in1=cum3, op=ALU.subtract),
           reads=[bh[t_cum], bc[t_cum]], writes=[bh[t_d], bc[t_d]])
        op("act", lambda e: e.activation(out=tmp[t_d][0:96, :], in_=tmp[t_d][0:96, :], func=AF.Exp),
           reads=[bh[t_d], bc[t_d]], writes=[bh[t_d], bc[t_d]])
        op("pool", lambda e: e.tensor_tensor(out=kds[0:96, :], in0=tmp[t_kk][0:96, :],
                                             in1=tmp[t_d][0:96, :], op=ALU.mult),
           reads=[bh[t_kk], bc[t_kk], bh[t_d], bc[t_d]], writes=[b_kds])
        op("act", lambda e: e.activation(out=Dres[0:96, ti * 8:(ti + 1) * 8],
                                         in_=tmp[t_cum][0:96, 63:512:64], func=AF.Exp),
           reads=[bh[t_cum], bc[t_cum]], writes=[b_D])
        for bi in range(4):
            op("pe", lambda e, bi=bi: e.transpose(ktp_bf[:, bi * 96:(bi + 1) * 96],
                                                  kds[0:96, bi * 128:(bi + 1) * 128], identb[0:96, 0:96]),
               reads=[b_kds] + CONST, writes=[ktp[1]])
        evac_copy(kdsT[:, ti * 4:(ti + 1) * 4, :], ktp_bf[:, 0:384].rearrange("p (b c) -> p b c", c=96),
                  [ktp[1]], [b_kdsT[ti]])
        for cc in range(8):
            c = ti * 8 + cc
            blk, par = c // 2, c % 2
            rows = slice(64 * par, 64 * par + 64)
            pc = slice(cc * 64, (cc + 1) * 64)
            op("pe", lambda e, blk=blk, rows=rows, pc=pc: e.matmul(
                ups[0][0:64, pc], lhsT=kdsT[rows, blk, 0:64], rhs=vrec[rows, blk, 0:64], start=True, stop=True),
               reads=[b_kdsT[ti], b_vrec[blk]], writes=[ups[1]], serial=True)
            op("pe", lambda e, blk=blk, rows=rows, pc=pc: e.matmul(
                ups[0][64:96, pc], lhsT=kdsT[rows, blk, 64:96], rhs=vrec[rows, blk, 64:128], start=True, stop=True),
               reads=[b_kdsT[ti], b_vrec[blk]], writes=[ups[1]], serial=True)
        op("dve", lambda e: e.tensor_copy(
            out=U[0:96, :].rearrange("p (v c) -> p v c", c=128)[:, :, ti * 8:(ti + 1) * 8],
            in_=ups[0][0:96, :].rearrange("p (c v) -> p v c", v=64)),
           reads=[ups[1]], writes=[b_U])

    tr.barrier()
    es_a.close()
    if _STOP == "A":
        es_r.close()
        es_c.close()
        return
    es_b = contextlib.ExitStack()
    Sprev = sb(nc, es_b, pfx + "Sprev", [128, NCH * 64], BF16)
    b_Sp = tr.buf("Sprev")
    scm = [[sb(nc, es_b, pfx + f"scm{h}{i}", [128, 512], BF16) for i in range(2)] for h in range(2)]
    b_scm = [tr.bufs(2, "scm") for _ in range(2)]
    osb = [sb(nc, es_b, pfx + f"osb{i}", [128, 512], F32) for i in range(2)]
    b_osb = tr.bufs(2, "osb")
    sq = [sb(nc, es_b, pfx + f"sq{i}", [128, 512], F32) for i in range(2)]
    b_sq = tr.bufs(2, "sq")
    rstd = [sb(nc, es_b, pfx + f"rstd{i}", [128, 512], F32) for i in range(2)]
    b_rstd = tr.bufs(2, "rstd")
    yt = [sb(nc, es_b, pfx + f"yt{i}", [128, 512], BF16) for i in range(2)]
    b_yt = tr.bufs(2, "yt")

    op("pool", lambda e: e.memset(Sprev[:, 0:64], 0.0), writes=[b_Sp])
    Sp3 = Sprev[0:96, :].rearrange("p (c v) -> p c v", v=64)
    for v in range(64):
        op("dve", lambda e, v=v: e.tensor_tensor_scan(
            out=Sp3[:, 1:128, v], data0=Dres[0:96, 0:127], data1=U[0:96, v * 128:v * 128 + 127],
            initial=0.0, op0=ALU.mult, op1=ALU.add),
           reads=[b_U, b_D], writes=[b_Sp])

    scb = [psb[0], psb[1]]
    opsb = [psb[2], psb[3]]
    msb = [psb[4], psb[5]]
    for ti in range(NT if _KB > 0 else 0):
        cols = slice(ti * 512, (ti + 1) * 512)
        i2 = ti % 2
        for cc in range(8):
            c = ti * 8 + cc
            blk, par = c // 2, c % 2
            rows = slice(64 * par, 64 * par + 64)
            pc = slice(cc * 64, (cc + 1) * 64)
            tc_ = slice(c * 64, (c + 1) * 64)
            op("pe", lambda e, rows=rows, pc=pc, tc_=tc_: e.matmul(
                scb[0][0][rows, pc], lhsT=kdec[0:64, tc_], rhs=qdec[0:64, tc_], start=True, stop=True),
               reads=[b_kdec[ti], b_qdec[ti]], writes=[scb[0][1]])
            op("pe", lambda e, rows=rows, pc=pc, tc_=tc_: e.matmul(
                scb[1][0][rows, pc], lhsT=kdec[64:96, tc_], rhs=qdec[64:96, tc_], start=True, stop=True),
               reads=[b_kdec[ti], b_qdec[ti]], writes=[scb[1][1]])
        for h in range(2):
            for par in range(2):
                rows = slice(64 * par, 64 * par + 64)
                src = scb[h][0][rows, :].rearrange("p (c two t) -> p c two t", two=2, t=64)[:, :, par, :]
                dst = scm[h][i2][rows, :].rearrange("p (c two t) -> p c two t", two=2, t=64)[:, :, par, :]
                msk = cmask[rows, :].unsqueeze(1).to_broadcast([64, 4, 64])
                op("dve", lambda e, src=src, dst=dst, msk=msk: e.tensor_tensor(out=dst, in0=src, in1=msk, op=ALU.mult),
                   reads=[scb[h][1]] + CONST, writes=[b_scm[h][i2]])
        if _KB < 2:
            continue
        ob = opsb[i2]
        for cc in range(8):
            c = ti * 8 + cc
            blk, par = c // 2, c % 2
            rows = slice(64 * par, 64 * par + 64)
            pc = slice(cc * 64, (cc + 1) * 64)
            tc_ = slice(c * 64, (c + 1) * 64)
            sc_ = slice(c * 64, (c + 1) * 64)
            op("pe", lambda e, pc=pc, tc_=tc_, sc_=sc_: e.matmul(
                ob[0][0:64, pc], lhsT=Sprev[0:64, sc_], rhs=qdec[0:64, tc_], start=True, stop=False),
               reads=[b_Sp, b_qdec[ti]], writes=[ob[1]], serial=True)
            op("pe", lambda e, pc=pc, rows=rows, blk=blk: e.matmul(
                ob[0][0:64, pc], lhsT=vrec[rows, blk, 0:64], rhs=scm[0][i2][rows, pc], start=False, stop=True),
               reads=[b_vrec[blk], b_scm[0][i2]], writes=[ob[1]], serial=True)
            op("pe", lambda e, pc=pc, tc_=tc_, sc_=sc_: e.matmul(
                ob[0][64:128, pc], lhsT=Sprev[64:96, sc_], rhs=qdec[64:96, tc_], start=True, stop=False),
               reads=[b_Sp, b_qdec[ti]], writes=[ob[1]], serial=True)
            op("pe", lambda e, pc=pc, rows=rows, blk=blk: e.matmul(
                ob[0][64:128, pc], lhsT=vrec[rows, blk, 64:128], rhs=scm[1][i2][rows, pc], start=False, stop=True),
               reads=[b_vrec[blk], b_scm[1][i2]], writes=[ob[1]], serial=True)
        if _KB < 3:
            continue
        op("act", lambda e: e.activation(out=osb[i2][:, :], in_=ob[0][:, :], func=AF.Copy),
           reads=[ob[1]], writes=[b_osb[i2]])
        op("act", lambda e: e.activation(out=sq[i2][:, :], in_=ob[0][:, :], func=AF.Square),
           reads=[ob[1]], writes=[b_sq[i2]])
        mb = msb[i2]
        op("pe", lambda e: e.matmul(mb[0][:, :], lhsT=onesblk[:, :], rhs=sq[i2][:, :], start=True, stop=True),
           reads=[b_sq[i2]] + CONST, writes=[mb[1]])
        op("act", lambda e: e.activation(out=rstd[i2][:, :], in_=mb[0][:, :], func=AF.Sqrt, bias=pp2[:, 5:6]),
           reads=[mb[1]] + CONST, writes=[b_rstd[i2]])
        op("dve", lambda e: e.reciprocal(out=rstd[i2][:, :], in_=rstd[i2][:, :]),
           reads=[b_rstd[i2]], writes=[b_rstd[i2]])
        op("pool", lambda e: e.tensor_tensor(out=osb[i2][:, :], in0=osb[i2][:, :], in1=rstd[i2][:, :], op=ALU.mult),
           reads=[b_osb[i2], b_rstd[i2]], writes=[b_osb[i2]])
        op("dve", lambda e: e.scalar_tensor_tensor(out=yt[i2][:, :], in0=osb[i2][:, :], scalar=ppt[:, 3:4],
                                                   in1=RGres[:, cols], op0=ALU.mult, op1=ALU.mult),
           reads=[b_osb[i2], b_RG[ti]] + CONST, writes=[b_yt[i2]])
        ysink["bc"](ti, yt[i2], b_yt[i2])

    tr.barrier()
    es_b.close()
    es_r.close()
    if _STOP == "B":
        es_c.close()
        return
    if "after_bc" in ysink:
        ysink["after_bc"]()
    es_cw = contextlib.ExitStack()
    NR = 3
    G = 4
    NRG = 3
    GW = G * 512
    omg = [sb(nc, es_cw, pfx + f"omg{i}", [128, GW], F32) for i in range(NRG)]
    b_omg = [tr.bufs(G, "omg") for _ in range(NRG)]
    Pbg = [sb(nc, es_cw, pfx + f"Pbg{i}", [128, GW + 1], F32) for i in range(NRG)]
    b_Pbg = tr.bufs(NRG, "Pbg")
    NRW, NRT = 12, 4
    KC, KD = 6, 2
    wbf = [sb(nc, es_cw, pfx + f"wbf{i}", [128, 512], BF16) for i in range(NRW)]
    b_wbf = tr.bufs(NRW, "wbf")
    wT = [sb(nc, es_cw, pfx + f"wT{i}", [128, 512], BF16) for i in range(NRT)]
    b_wT = tr.bufs(NRT, "wT")
    ya = [sb(nc, es_cw, pfx + f"ya{i}", [128, 512], BF16) for i in range(2)]
    b_ya = tr.bufs(2, "ya")
    zps = [psb[0], psb[1], psb[2]]
    wtp = [psb[3], psb[4]]
    wtp_bf = [w_[0][:].bitcast(BF16) for w_ in wtp]
    acc = [psb[5], psb[6]]

    items = []
    for tb in range(NB):
        hi = tb * 128 + 128
        first = True
        while hi > 0:
            lo = max(0, hi - 512)
            items.append(dict(tb=tb, lo=lo, hi=hi, W=hi - lo, first=first, last=(lo == 0)))
            hi = lo
            first = False
    n = len(items)
    groups, cur = [], []
    for g_, it_ in enumerate(items):
        if cur and (items[cur[0]]["tb"] != it_["tb"] or len(cur) == G):
            groups.append(cur)
            cur = []
        cur.append(g_)
    groups.append(cur)
    for gi_, grp_ in enumerate(groups):
        wt_ = sum(items[g_]["W"] for g_ in grp_)
        off_ = wt_
        for p_, g_ in enumerate(grp_):
            off_ -= items[g_]["W"]
            items[g_].update(grp=gi_, c0=off_, Wtot=wt_, glast=(p_ == len(grp_) - 1), gp=p_, members=grp_)

    def stA(g):
        it = items[g]
        s3 = g % NR
        tb, lo, hi, W = it["tb"], it["lo"], it["hi"], it["W"]
        t0 = tb * 128
        op("pe", lambda e: e.matmul(zps[s3][0][:, 0:W], lhsT=S0res[0:64, t0:t0 + 128], rhs=S1res[0:64, lo:hi],
                                    start=True, stop=True),
           reads=[b_S0q[tb // 4]] + [b_S1[k] for k in range(lo // 512, (hi - 1) // 512 + 1)],
           writes=[zps[s3][1]])

    def stB(g):
        it = items[g]
        s3 = g % NR
        W, c0, Wtot, gi, p = it["W"], it["c0"], it["Wtot"], it["grp"], it["gp"]
        sg = gi % NRG
        op("act", lambda e: e.activation(out=omg[sg][:, c0:c0 + W], in_=zps[s3][0][:, 0:W], func=AF.Sigmoid,
                                         scale=-0.125),
           reads=[zps[s3][1]], writes=[b_omg[sg][p]])
        if it["first"]:
            op("dve", lambda e: e.tensor_tensor(out=omg[sg][:, c0 + W - 128:c0 + W],
                                                in0=omg[sg][:, c0 + W - 128:c0 + W], in1=m01[:, :], op=ALU.max),
               reads=[b_omg[sg][p]] + CONST, writes=[b_omg[sg][p]])
            op("dve", lambda e: e.memset(Pbg[sg][:, Wtot:Wtot + 1], 1.0), writes=[b_Pbg[sg]])
        if not it["glast"]:
            return
        op("dve", lambda e: e.tensor_tensor_scan(out=Pbg[sg][:, 0:Wtot][:, ::-1], data0=omg[sg][:, 0:Wtot][:, ::-1],
                                                 data1=zeros[:, 0:1].to_broadcast([128, Wtot]),
                                                 initial=Pbg[sg][:, Wtot:Wtot + 1],
                                                 op0=ALU.mult, op1=ALU.add),
           reads=[b_omg[sg][q] for q in range(len(it["members"]))] + [b_Pbg[sg]] + CONST,
           writes=[b_Pbg[sg]])
        if not it["last"]:
            sgn = (gi + 1) % NRG
            Wn = items[g + 1]["Wtot"]
            op("dve", lambda e: e.tensor_copy(out=Pbg[sgn][:, Wn:Wn + 1], in_=Pbg[sg][:, 0:1]),
               reads=[b_Pbg[sg]], writes=[b_Pbg[sgn]])
        for q in it["members"]:
            Wq, cq = items[q]["W"], items[q]["c0"]
            sw = q % NRW
            op("pool", lambda e: e.tensor_tensor(out=wbf[sw][:, 0:Wq], in0=Pbg[sg][:, cq + 1:cq + Wq + 1],
                                                 in1=Pbg[sg][:, cq:cq + Wq], op=ALU.subtract),
               reads=[b_Pbg[sg]], writes=[b_wbf[sw]])

    def stC(g):
        it = items[g]
        sw = g % NRW
        st_ = g % NRT
        s2 = g % 2
        W = it["W"]
        for kbi in range(W // 128):
            op("pe", lambda e, kbi=kbi: e.transpose(wtp_bf[s2][:, kbi * 128:(kbi + 1) * 128],
                                                    wbf[sw][:, kbi * 128:(kbi + 1) * 128], identb[:, :]),
               reads=[b_wbf[sw]] + CONST, writes=[wtp[s2][1]])
        op("act", lambda e: e.activation(out=wT[st_][:, 0:W], in_=wtp_bf[s2][:, 0:W], func=AF.Copy),
           reads=[wtp[s2][1]], writes=[b_wT[st_]])

    def stD(g):
        it = items[g]
        s3 = g % NRT
        tb, lo, W = it["tb"], it["lo"], it["W"]
        ab = acc[(tb // 4) % 2]
        ac = slice((tb % 4) * 128, (tb % 4 + 1) * 128)
        nk = W // 128
        for kbi in range(nk):
            kb = lo // 128 + kbi
            op("pe", lambda e, kbi=kbi, kb=kb: e.matmul(
                ab[0][64:128, ac], lhsT=vsb[:, kb, :], rhs=wT[s3][:, kbi * 128:(kbi + 1) * 128],
                start=(it["first"] and kbi == 0), stop=(it["last"] and kbi == nk - 1)),
               reads=[b_vsb[kb], b_wT[s3]], writes=[ab[1]])
        if it["last"] and tb % 4 == 3:
            q4 = tb // 4
            i2 = q4 % 2
            cols = slice(q4 * 512, (q4 + 1) * 512)
            op("dve", lambda e: e.tensor_tensor(out=ya[i2][64:128, :], in0=ab[0][64:128, :],
                                                in1=S0res[64:128, cols], op=ALU.mult),
               reads=[ab[1], b_S0g[q4]], writes=[b_ya[i2]])
            ysink["a"](q4, ya[i2], b_ya[i2])

    for i in range(n + 1 + KC + KD):
        if i < n:
            stA(i)
        if 0 <= i - 1 < n:
            stB(i - 1)
        if 0 <= i - 1 - KC < n:
            stC(i - 1 - KC)
        if 0 <= i - 1 - KC - KD < n:
            stD(i - 1 - KC - KD)

    tr.barrier()
    es_cw.close()
    es_c.close()
    if "after_a" in ysink:
        ysink["after_a"]()


def emit_T(nc, tr, es0, psb, xt, xt_deps, yload, wg, wup, wout, lngb, cst, osink, pfx, use_pool=True):
    PE2 = "pool" if use_pool else "dve"
    op, dma = tr.op, tr.dma
    es_t = contextlib.ExitStack()
    es0.enter_context(es_t)
    ident = sb(nc, es_t, pfx + "ident", [128, 128], F32)
    lng = sb(nc, es_t, pfx + "lng", [128, D], F32)
    lnb = sb(nc, es_t, pfx + "lnb", [128, D], F32)
    b_c = tr.buf("tconst")
    dma("sp", ident[:], cst["ident"], b_c, writes=[b_c])
    dma("sp", lng[:], lngb[0:1, :].to_broadcast([128, D]), b_c, writes=[b_c])
    dma("sp", lnb[:], lngb[1:2, :].to_broadcast([128, D]), b_c, writes=[b_c])
    CONST = [b_c]
    wgb = sb(nc, es_t, pfx + "wgb", [128, 8, 3072], BF16)
    wupb = sb(nc, es_t, pfx + "wupb", [128, 6, D], BF16)
    woutb = sb(nc, es_t, pfx + "woutb", [128, 8, D], BF16)
    b_w = tr.buf("tw")
    stg = [sb(nc, es_t, pfx + f"stg{i}", [128, D], F32) for i in range(3)]
    b_stg = tr.bufs(3, "stg")
    cp_rr = [0]

    def cast_copy(out, in_, reads, writes):
        k = cp_rr[0] % 2
        cp_rr[0] += 1
        if k == 0:
            op("dve", lambda e: e.tensor_copy(out=out, in_=in_), reads=reads, writes=writes)
        elif k == 1:
            op("act", lambda e: e.activation(out=out, in_=in_, func=AF.Copy), reads=reads, writes=writes)
        else:
            op("pool", lambda e: e.tensor_copy(out=out, in_=in_), reads=reads, writes=writes)

    si = 0
    wg_v = wg.rearrange("(j p) c -> j p c", p=128)
    for j in range(8):
        for q in range(3):
            s = si % 3
            si += 1
            dma("sp", stg[s][:], wg_v[j][:, q * 1024:(q + 1) * 1024], b_stg[s], writes=[b_stg[s]])
            cast_copy(wgb[:, j, q * 1024:(q + 1) * 1024], stg[s][:], [b_stg[s]], [b_w])
    wup_v = wup.rearrange("(j p) c -> j p c", p=128)
    for j in range(6):
        s = si % 3
        si += 1
        dma("sp", stg[s][:], wup_v[j], b_stg[s], writes=[b_stg[s]])
        cast_copy(wupb[:, j, :], stg[s][:], [b_stg[s]], [b_w])
    wout_v = wout.rearrange("(j p) c -> j p c", p=128)
    for j in range(8):
        s = si % 3
        si += 1
        dma("sp", stg[s][:], wout_v[j], b_stg[s], writes=[b_stg[s]])
        cast_copy(woutb[:, j, :], stg[s][:], [b_stg[s]], [b_w])

    xres = sb(nc, es_t, pfx + "xres", [128, 4, D], F32)
    b_xres = tr.bufs(4, "xres")
    xT = sb(nc, es_t, pfx + "xT", [128, 8, 512], BF16)
    b_xT = [tr.bufs(2, "xT") for _ in range(4)]
    ybf = sb(nc, es_t, pfx + "ybf", [128, 6, 512], BF16)
    b_ybf = tr.bufs(6, "ybf")
    gsb = [sb(nc, es_t, pfx + f"gsb{i}", [128, 512], F32) for i in range(3)]
    b_gsb = tr.bufs(3, "gsb")
    tm_ = [sb(nc, es_t, pfx + f"tm{i}", [128, 512], F32) for i in range(2)]
    b_tm = tr.bufs(2, "tm")
    macc = [sb(nc, es_t, pfx + f"macc{i}", [128, 512], F32) for i in range(2)]
    b_macc = tr.bufs(2, "macc")
    mT = sb(nc, es_t, pfx + "mT", [128, 8, 512], BF16)
    b_mT = tr.bufs(8, "mT")
    rr = [sb(nc, es_t, pfx + f"rr{i}", [128, D], F32) for i in range(2)]
    b_rr = tr.bufs(2, "rr")
    sqj = [sb(nc, es_t, pfx + f"sqj{i}", [128, D], F32) for i in range(2)]
    b_sqj = tr.bufs(2, "sqj")
    st = [sb(nc, es_t, pfx + f"st{i}", [128, 8], F32) for i in range(2)]
    b_st = tr.bufs(2, "st")
    yo_ = [sb(nc, es_t, pfx + f"yo{i}", [128, D], F32) for i in range(2)]
    b_yo = tr.bufs(2, "yo")

    tp = [psb[0], psb[1]]
    gpb = [psb[2], psb[3], psb[4]]
    upb = [psb[5], psb[6]]
    outb = [psb[6], psb[7]]
    xt_v = xt.rearrange("(n p) d -> n p d", p=128)
    evr = [0]

    def evac_copy(out, in_, reads, writes):
        evr[0] ^= 1
        if evr[0]:
            op("act", lambda e: e.activation(out=out, in_=in_, func=AF.Copy), reads=reads, writes=writes)
        else:
            op("dve", lambda e: e.tensor_copy(out=out, in_=in_), reads=reads, writes=writes)

    gi = 0
    ui = 0
    for ti in range(TT // 512):
        cols = slice(ti * 512, (ti + 1) * 512)
        for bi in range(4):
            blk = ti * 4 + bi
            dma("sp", xres[:, bi, :], xt_v[blk], b_xres[bi], reads=xt_deps, writes=[b_xres[bi]])
            for half in range(2):
                bank = tp[half]
                for jj in range(4):
                    j = half * 4 + jj
                    op("pe", lambda e: e.transpose(bank[0][:, jj * 128:(jj + 1) * 128],
                                                   xres[:, bi, j * 128:(j + 1) * 128], ident[:]),
                       reads=[b_xres[bi]] + CONST, writes=[bank[1]])
                evac_copy(xT[:, half * 4:(half + 1) * 4, bi * 128:(bi + 1) * 128],
                          bank[0][:, :].rearrange("p (j t) -> p j t", t=128),
                          [bank[1]], [b_xT[bi][half]])
        for r in range(6):
            yload(r, ti, ybf[:, r, :], b_ybf[r])
        for dc in range(8):
            mi = dc % 2
            for nb_ in range(3):
                gb = gpb[gi % 3]
                gs = gi % 3
                gi += 1
                for j in range(8):
                    op("pe", lambda e: e.matmul(
                        gb[0][:, :], lhsT=wgb[:, j, nb_ * 1024 + dc * 128:nb_ * 1024 + (dc + 1) * 128],
                        rhs=xT[:, j, :], start=(j == 0), stop=(j == 7)),
                       reads=[b_w] + [b_xT[bi][j // 4] for bi in range(4)], writes=[gb[1]])
                op("act", lambda e: e.activation(out=gsb[gs][:, :], in_=gb[0][:, :], func=AF.Sigmoid),
                   reads=[gb[1]], writes=[b_gsb[gs]])
                ub = upb[ui % 2]
                ui += 1
                for r in range(2):
                    op("pe", lambda e: e.matmul(
                        ub[0][:, :], lhsT=wupb[:, nb_ * 2 + r, dc * 128:(dc + 1) * 128],
                        rhs=ybf[:, nb_ * 2 + r, :], start=(r == 0), stop=(r == 1)),
                       reads=[b_w, b_ybf[nb_ * 2 + r]], writes=[ub[1]])
                if nb_ == 0:
                    op("dve", lambda e: e.tensor_tensor(out=macc[mi][:, :], in0=gsb[gs][:, :], in1=ub[0][:, :],
                                                        op=ALU.mult),
                       reads=[b_gsb[gs], ub[1]], writes=[b_macc[mi]])
                else:
                    t2 = nb_ % 2
                    op("dve", lambda e: e.tensor_tensor(out=tm_[t2][:, :], in0=gsb[gs][:, :], in1=ub[0][:, :],
                                                        op=ALU.mult),
                       reads=[b_gsb[gs], ub[1]], writes=[b_tm[t2]])
                    if nb_ == 1:
                        op(PE2, lambda e: e.tensor_tensor(out=macc[mi][:, :], in0=macc[mi][:, :],
                                                             in1=tm_[t2][:, :], op=ALU.add),
                           reads=[b_macc[mi], b_tm[t2]], writes=[b_macc[mi]])
                    else:
                        op(PE2, lambda e: e.tensor_tensor(out=mT[:, dc, :], in0=macc[mi][:, :],
                                                             in1=tm_[t2][:, :], op=ALU.add),
                           reads=[b_macc[mi], b_tm[t2]], writes=[b_mT[dc]])
        for bi in range(4):
            blk = ti * 4 + bi
            i2 = blk % 2
            for hf in range(2):
                ob = outb[hf]
                for dc in range(8):
                    op("pe", lambda e: e.matmul(
                        ob[0][:, :], lhsT=mT[:, dc, bi * 128:(bi + 1) * 128],
                        rhs=woutb[:, dc, hf * 512:(hf + 1) * 512], start=(dc == 0), stop=(dc == 7)),
                       reads=[b_w, b_mT[dc]], writes=[ob[1]])
                op("dve", lambda e: e.scalar_tensor_tensor(
                    out=rr[i2][:, hf * 512:(hf + 1) * 512], in0=xres[:, bi, hf * 512:(hf + 1) * 512],
                    scalar=ALPHA, in1=ob[0][:, :], op0=ALU.mult, op1=ALU.add),
                   reads=[b_xres[bi], ob[1]], writes=[b_rr[i2]])
            op("dve", lambda e: e.reduce_sum(out=st[i2][:, 0:1], in_=rr[i2][:, :], axis=mybir.AxisListType.X),
               reads=[b_rr[i2]], writes=[b_st[i2]])
            op("dve", lambda e: e.tensor_scalar(out=st[i2][:, 1:2], in0=st[i2][:, 0:1], scalar1=-1.0 / D,
                                                scalar2=None, op0=ALU.mult),
               reads=[b_st[i2]], writes=[b_st[i2]])
            op("act", lambda e: e.activation(out=rr[i2][:, :], in_=rr[i2][:, :], func=AF.Identity,
                                             bias=st[i2][:, 1:2], scale=1.0),
               reads=[b_rr[i2], b_st[i2]], writes=[b_rr[i2]])
            op(PE2, lambda e: e.tensor_tensor(out=sqj[i2][:, :], in0=rr[i2][:, :], in1=rr[i2][:, :], op=ALU.mult),
               reads=[b_rr[i2]], writes=[b_sqj[i2]])
            op("dve", lambda e: e.reduce_sum(out=st[i2][:, 2:3], in_=sqj[i2][:, :], axis=mybir.AxisListType.X),
               reads=[b_sqj[i2], b_st[i2]], writes=[b_st[i2]])
            op("dve", lambda e: e.tensor_scalar(out=st[i2][:, 3:4], in0=st[i2][:, 2:3], scalar1=1.0 / D,
                                                scalar2=EPS, op0=ALU.mult, op1=ALU.add),
               reads=[b_st[i2]], writes=[b_st[i2]])
            op("act", lambda e: e.activation(out=st[i2][:, 5:6], in_=st[i2][:, 3:4], func=AF.Sqrt),
               reads=[b_st[i2]], writes=[b_st[i2]])
            op("dve", lambda e: e.reciprocal(out=st[i2][:, 4:5], in_=st[i2][:, 5:6]),
               reads=[b_st[i2]], writes=[b_st[i2]])
            op("dve", lambda e: e.scalar_tensor_tensor(out=yo_[i2][:, :], in0=rr[i2][:, :], scalar=st[i2][:, 4:5],
                                                       in1=lng[:, :], op0=ALU.mult, op1=ALU.mult),
               reads=[b_rr[i2], b_st[i2]] + CONST, writes=[b_yo[i2]])
            op(PE2, lambda e: e.tensor_tensor(out=yo_[i2][:, :], in0=yo_[i2][:, :], in1=lnb[:, :], op=ALU.add),
               reads=[b_yo[i2]] + CONST, writes=[b_yo[i2]])
            osink(blk, ti, bi, yo_[i2], b_yo[i2])
    tr.barrier()
    es_t.close()


def _consts():
    ident = np.eye(128, dtype=np.float32)
    p = np.arange(128)[:, None]
    cmask = ((p % 64) <= np.arange(64)[None, :]).astype(np.float32)
    m01 = (np.arange(128)[None, :] >= p).astype(np.float32)
    rmask = np.ones((128, 512), np.float32)
    rmask[:, ::64] = 0.0
    onesblk = np.zeros((128, 128), np.float32)
    onesblk[0:64, 0:64] = 1.0 / 64
    onesblk[64:128, 64:128] = 1.0 / 64
    return {"ident": ident, "cmask": cmask, "m01": m01, "rmask": rmask, "onesblk": onesblk}


_OFF = dict(a_q=0, a_k=256, a_v=512, a_g=768, h_f=1024, h_i=1280, h_q=1536, h_g=1792,
            c_q=2048, c_k=2176, c_v=2304, c_g=2560, c_r=2816, gates=2832)


def _h_weights(w_in_l, h):
    def col(name, width):
        o = _OFF[name] + h * width
        return w_in_l[:, o:o + width]
    z = lambda n: np.zeros((D, n), np.float32)
    wfm = np.concatenate([
        col("a_q", 64), col("a_g", 64),
        col("a_k", 64), w_in_l[:, _OFF["c_r"]:_OFF["c_r"] + 16], z(48),
        col("h_q", 64), col("c_q", 32), z(32),
        col("h_f", 64), col("c_k", 32), z(32),
        col("h_g", 64), col("c_g", 64),
    ], axis=1)
    wtm = np.concatenate([col("a_v", 64), col("h_i", 64), col("c_v", 64)], axis=1)
    return np.ascontiguousarray(wfm), np.ascontiguousarray(wtm)


def _h_params(inp, layer, h):
    pp = np.zeros((128, 8), np.float32)
    pp[0:64, 0] = inp["hgrn_lb_logits"][0, h * 64:(h + 1) * 64]
    pp[0:64, 1] = inp["hgrn_lb_logits"][1, h * 64:(h + 1) * 64]
    pp[64:96, 2] = inp["gla_gate_b"][layer, h * 32:(h + 1) * 32]
    pp[0:64, 3] = inp["hgrn_norm_g"][layer, h * 64:(h + 1) * 64]
    pp[64:128, 3] = inp["gla_norm_g"][layer, h * 64:(h + 1) * 64]
    pp[:, 4] = 1.0
    pp[64:96, 4] = 32.0 ** -0.5
    w2 = np.zeros((128, 32), np.float32)
    w2[64:80, :] = inp["gla_gate_w2"][layer][:, h * 32:(h + 1) * 32]
    return pp, w2


def _psum_banks(nc, tr, es):
    banks = []
    for i in range(8):
        t = es.enter_context(nc.psum_tensor(f"ps{i}", [128, 512], F32))
        banks.append((t, tr.buf("ps")))
    return banks


_CACHE = {}
HC = ("ident", "cmask", "m01", "rmask", "onesblk")
I32 = mybir.dt.int32
RG = [[0, 1, 2, 3], [4, 5, 6, 7]]


def build_fused():
    if "F" in _CACHE:
        return _CACHE["F"]
    nc = bass.Bass("TRN2", target_bir_lowering=False)
    ein = lambda name, shape: nc.dram_tensor(name, shape, F32, kind="ExternalInput").ap()
    xs = ein("xs", [T, D])
    xt = ein("xt", [TT, D])
    oh_in = ein("oh", [128, 8])
    cst = {k: ein("c_" + k, list(v.shape)) for k, v in _consts().items()}
    LW = []
    for l in range(2):
        LW.append(dict(wfm=ein(f"wfm{l}", [D, 640]), wtm=ein(f"wtm{l}", [D, 192]), pp=ein(f"pp{l}", [128, 8]),
                       w2=ein(f"w2{l}", [128, 32]), wg=ein(f"wg{l}", [D, 3072]), wup=ein(f"wup{l}", [768, D]),
                       wout=ein(f"wout{l}", [D, D]), lngb=ein(f"lngb{l}", [2, D])))
    xo = nc.dram_tensor("xo", [TT, D], F32, kind="ExternalOutput").ap()

    ybc_in = [nc.dram_tensor(f"ybc_in{q}", [512, 1024], F32) for q in range(4)]
    ya_in = [nc.dram_tensor(f"ya_in{q}", [256, 1024], F32) for q in range(4)]
    ybc_out = [nc.dram_tensor(f"ybc_out{q}", [512, 1024], F32) for q in range(4)]
    ya_out = [nc.dram_tensor(f"ya_out{q}", [256, 1024], F32) for q in range(4)]
    x1_in = [[nc.dram_tensor(f"x1_in{t}_{hf}", [2048, 256], F32) for hf in range(2)] for t in range(4)]
    x1_g = [[nc.dram_tensor(f"x1_g{t}_{hf}", [2048, 256], F32) for hf in range(2)] for t in range(4)]
    x1loc = nc.dram_tensor("x1loc", [TT, D], F32).ap()
    bf = lambda t: t.ap().bitcast(BF16)
    bf3 = lambda t: t.ap().bitcast(BF16).rearrange("(r p) c -> r p c", r=4)
    dyn = lambda v: v.rearrange("o p c -> (o p) c")

    with contextlib.ExitStack() as es:
        tr = Tr(nc, es)
        op, dma = tr.op, tr.dma
        psb = _psum_banks(nc, tr, es)

        b_ybc_in = tr.bufs(4, "ybc_in")
        b_ya_in = tr.bufs(4, "ya_in")
        b_ybc_out = tr.bufs(4, "ybc_out")
        b_ya_out = tr.bufs(4, "ya_out")
        b_x1_in = tr.bufs(4, "x1_in")
        b_x1_g = tr.bufs(4, "x1_g")
        b_x1loc = tr.buf("x1loc")

        oh = sb(nc, es, "oh_sb", [128, 8], F32)
        b_oh = tr.buf("oh")
        dma("sp", oh[:], oh_in, b_oh, writes=[b_oh])
        slot = [sb(nc, es, f"slot{i}", [128, 512], BF16) for i in range(4)]
        b_slot = tr.bufs(4, "slot")
        sl = [0]

        def scatter(tile_ap, nfree, dsts, reads, rows=slice(0, 128)):
            for j in range(4):
                k = sl[0] % 4
                sl[0] += 1
                op("act", lambda e: e.activation(out=slot[k][rows, 0:nfree], in_=tile_ap, func=AF.Identity,
                                                 scale=oh[rows, j:j + 1], bias=oh[rows, 4:5]),
                   reads=list(reads) + [b_oh], writes=[b_slot[k]])
                for (dram_ap, srows, dbuf) in dsts[j]:
                    dma("sp", dram_ap, slot[k][srows, 0:nfree], b_slot[k], reads=[b_slot[k]], writes=[dbuf])

        def run_layer(l):
            W = LW[l]

            def sink_bc(ti, tile, buf):
                q, c0 = ti // 4, (ti % 4) * 512
                v = bf(ybc_in[q])
                scatter(tile[:, :], 512,
                        [[(v[j * 64:(j + 1) * 64, c0:c0 + 512], slice(0, 64), b_ybc_in[q]),
                          (v[256 + j * 64:256 + (j + 1) * 64, c0:c0 + 512], slice(64, 128), b_ybc_in[q])]
                         for j in range(4)], [buf])

            def sink_a(q4, tile, buf):
                q, c0 = q4 // 4, (q4 % 4) * 512
                v = bf(ya_in[q])
                scatter(tile[64:128, :], 512,
                        [[(v[j * 64:(j + 1) * 64, c0:c0 + 512], slice(64, 128), b_ya_in[q])] for j in range(4)], [buf],
                        rows=slice(64, 128))

            def after_bc():
                for q in range(4):
                    tr.coll(lambda e: e.collective_compute(
                        "AllReduce", ALU.add, replica_groups=RG,
                        ins=[ybc_in[q].ap().opt()], outs=[ybc_out[q].ap().opt()]),
                        reads=[b_ybc_in[q]], writes=[b_ybc_out[q]])

            def after_a():
                for q in range(4):
                    tr.coll(lambda e: e.collective_compute(
                        "AllReduce", ALU.add, replica_groups=RG,
                        ins=[ya_in[q].ap().opt()], outs=[ya_out[q].ap().opt()]),
                        reads=[b_ya_in[q]], writes=[b_ya_out[q]])

            if l == 0:
                xsrc = ("f32", xs)
            else:
                def xblk(blk):
                    rk, t, bi = blk // 16, (blk // 4) % 4, blk % 4
                    return [bf(x1_g[t][hf])[rk * 512 + bi * 128:rk * 512 + (bi + 1) * 128, :] for hf in range(2)]
                xsrc = ("bf16", xblk, lambda blk: [b_x1_g[(blk // 4) % 4]])
            emit_H(nc, tr, es, psb, l, xsrc, W["wfm"], W["wtm"], W["pp"], W["w2"], cst,
                   dict(bc=sink_bc, a=sink_a, after_bc=after_bc, after_a=after_a), f"h{l}_")

            def yload(r6, ti, dst, buf):
                c0 = ti * 512
                for j in range(4):
                    if r6 < 2:
                        src = bf(ya_out[j])[r6 * 128:(r6 + 1) * 128, c0:c0 + 512]
                        sbuf_ = b_ya_out[j]
                    else:
                        src = bf(ybc_out[j])[(r6 - 2) * 128:(r6 - 1) * 128, c0:c0 + 512]
                        sbuf_ = b_ybc_out[j]
                    k = sl[0] % 4
                    sl[0] += 1
                    dma("sp", slot[k][:, 0:512], src, b_slot[k], reads=[sbuf_], writes=[b_slot[k]])
                    eng = "dve"
                    if j == 0:
                        op(eng, lambda e: e.tensor_scalar(out=dst, in0=slot[k][:, 0:512], scalar1=oh[:, 0:1],
                                                          scalar2=None, op0=ALU.mult),
                           reads=[b_slot[k], b_oh], writes=[buf])
                    else:
                        op("dve", lambda e: e.scalar_tensor_tensor(out=dst, in0=slot[k][:, 0:512], scalar=oh[:, j:j + 1],
                                                                 in1=dst, op0=ALU.mult, op1=ALU.add),
                           reads=[b_slot[k], b_oh, buf], writes=[buf])

            if l == 0:
                def osink(blk, ti, bi, tile, buf):
                    dma("sp", x1loc[blk * 128:(blk + 1) * 128, :], tile[:, :], buf, reads=[buf], writes=[b_x1loc])
                    for hf in range(2):
                        v = bf(x1_in[ti][hf])
                        scatter(tile[:, hf * 512:(hf + 1) * 512], 512,
                                [[(v[j * 512 + bi * 128:j * 512 + (bi + 1) * 128, :],
                                   slice(0, 128), b_x1_in[ti])] for j in range(4)], [buf])
                    if bi == 3:
                        for hf in range(2):
                            tr.coll(lambda e: e.collective_compute(
                                "AllReduce", ALU.add, replica_groups=RG,
                                ins=[x1_in[ti][hf].ap().opt()], outs=[x1_g[ti][hf].ap().opt()]),
                                reads=[b_x1_in[ti]], writes=[b_x1_g[ti]])
                emit_T(nc, tr, es, psb, xt, [], yload, W["wg"], W["wup"], W["wout"], W["lngb"], cst,
                       osink, f"t{l}_", use_pool=False)
            else:
                def osink(blk, ti, bi, tile, buf):
                    dma("sp", xo[blk * 128:(blk + 1) * 128, :], tile[:, :], buf, reads=[buf])
                emit_T(nc, tr, es, psb, x1loc, [b_x1loc], yload, W["wg"], W["wup"], W["wout"], W["lngb"], cst,
                       osink, f"t{l}_")

        for l in range(2):
            run_layer(l)
        tr.finish()
    _CACHE["F"] = nc
    return nc


def kernel(**inputs):
    inp = {k: np.asarray(v, dtype=np.float32) for k, v in inputs.items()}
    nc = build_fused()
    c = _consts()
    x = inp["x"]
    xf = x.reshape(2 * T, D)
    shared = {"c_" + k: v for k, v in c.items()}
    for l in range(2):
        shared[f"wg{l}"] = np.ascontiguousarray(inp["w_in"][l][:, _OFF["gates"]:_OFF["gates"] + 3072])
        shared[f"wup{l}"] = np.ascontiguousarray(inp["w_up"][l].reshape(768, D))
        shared[f"wout{l}"] = np.ascontiguousarray(inp["w_out"][l])
        shared[f"lngb{l}"] = np.stack([inp["ln_g"][l], inp["ln_b"][l]]).astype(np.float32)
    maps = []
    for core in range(NCORE):
        b, h = core // 4, core % 4
        m = dict(shared)
        m["xs"] = np.ascontiguousarray(x[b])
        m["xt"] = np.ascontiguousarray(xf[core * TT:(core + 1) * TT])
        ohm = np.zeros((128, 8), np.float32)
        ohm[:, h] = 1.0
        m["oh"] = ohm
        for l in range(2):
            wfm, wtm = _h_weights(inp["w_in"][l], h)
            pp, w2 = _h_params(inp, l, h)
            m[f"wfm{l}"], m[f"wtm{l}"], m[f"pp{l}"], m[f"w2{l}"] = wfm, wtm, pp, w2
        maps.append(m)
    res = run_bass_kernel_spmd(nc, maps, core_ids=list(range(NCORE)))
    out = np.concatenate([res.results[core]["xo"] for core in range(NCORE)], axis=0)
    return out.reshape(2, T, D).astype(np.float32)
```

```python
import contextlib
import numpy as np
import concourse.bass as bass
import concourse.mybir as mybir
from concourse.bass_utils import run_bass_kernel_spmd

F32 = mybir.dt.float32
BF16 = mybir.dt.bfloat16
AF = mybir.ActivationFunctionType
ALU = mybir.AluOpType

_STOP = ""
_KNT = 16
_KB = 9
T = 8192
D = 1024
NCORE = 8
TT = 2048
EPS = 1e-5
ALPHA = 4.0 ** 0.25


class Buf:
    __slots__ = ("name", "w", "r", "dsem", "dcnt")

    def __init__(self, name):
        self.name = name
        self.w = None
        self.r = {}
        self.dsem = None
        self.dcnt = 0


class Tr:
    NDP = 18

    def __init__(self, nc, es):
        self.nc = nc
        self.es = es
        self.engs = {"pe": nc.tensor, "act": nc.scalar, "dve": nc.vector,
                     "pool": nc.gpsimd, "sp": nc.sync}
        self.sem = {k: es.enter_context(nc.semaphore("sem_" + k)) for k in self.engs}
        self.cnt = {k: 0 for k in self.engs}
        self.waited = {k: {} for k in self.engs}
        self.dpool = [[es.enter_context(nc.semaphore(f"dsem{i}")), 0] for i in range(self.NDP)]
        self.free = list(range(self.NDP))
        self.csem = es.enter_context(nc.semaphore("csem"))
        self.ccnt = 0
        self.owners = []
        self.all_bufs = []
        self.nbuf = 0

    def buf(self, name="b"):
        self.nbuf += 1
        b = Buf(f"{name}{self.nbuf}")
        self.all_bufs.append(b)
        return b

    def bufs(self, n, name="b"):
        return [self.buf(name) for _ in range(n)]

    def _waits(self, e, reads, writes):
        need = {}

        def add(tag):
            key, sem, val = tag
            if key not in need or need[key][1] < val:
                need[key] = (sem, val)

        for b in reads:
            if b.w is not None:
                add(b.w)
        for b in writes:
            if b.w is not None:
                add(b.w)
            for tag in b.r.values():
                add(tag)
        for key, (sem, val) in need.items():
            if key == "pe" and e == "pe":
                continue
            if self.waited[e].get(key, 0) >= val:
                continue
            self.engs[e].wait_ge(sem, val)
            self.waited[e][key] = val

    def _mark(self, tag, reads, writes):
        for b in reads:
            b.r[tag[0]] = tag
        for b in writes:
            b.w = tag
            b.r = {}

    def op(self, e, fn, reads=(), writes=(), serial=False):
        self._waits(e, reads, writes)
        if serial and self.cnt[e] > 0 and self.waited[e].get("self", 0) < self.cnt[e]:
            self.engs[e].wait_ge(self.sem[e], self.cnt[e])
            self.waited[e]["self"] = self.cnt[e]
        inst = fn(self.engs[e])
        self.cnt[e] += 1
        inst.then_inc(self.sem[e], 1)
        self._mark((e, self.sem[e], self.cnt[e]), reads, writes)

    def dma(self, q, out, in_, owner, reads=(), writes=()):
        self._waits(q, reads, writes)
        if owner.dsem is None:
            owner.dsem = self.free.pop(0)
            self.owners.append(owner)
        slot = self.dpool[owner.dsem]
        self.engs[q].dma_start(out=out, in_=in_).then_inc(slot[0], 16)
        slot[1] += 16
        self._mark((f"p{owner.dsem}", slot[0], slot[1]), reads, writes)

    def retag(self, bufs, owner):
        slot = self.dpool[owner.dsem]
        for b in bufs:
            b.w = (f"p{owner.dsem}", slot[0], slot[1])

    def coll(self, fn, reads=(), writes=()):
        self._waits("pool", reads, writes)
        fn(self.engs["pool"]).then_inc(self.csem)
        self.ccnt += 1
        self._mark(("coll", self.csem, self.ccnt), reads, writes)

    def _sync_all(self, e):
        for e2 in self.engs:
            if e2 == e or self.cnt[e2] == 0:
                continue
            if self.waited[e].get(e2, 0) >= self.cnt[e2]:
                continue
            self.engs[e].wait_ge(self.sem[e2], self.cnt[e2])
            self.waited[e][e2] = self.cnt[e2]
        for i, (sem, cnt) in enumerate(self.dpool):
            if cnt == 0 or self.waited[e].get(f"p{i}", 0) >= cnt:
                continue
            self.engs[e].wait_ge(sem, cnt)
            self.waited[e][f"p{i}"] = cnt
        if self.ccnt and self.waited[e].get("coll", 0) < self.ccnt:
            self.engs[e].wait_ge(self.csem, self.ccnt)
            self.waited[e]["coll"] = self.ccnt

    def barrier(self):
        for e in self.engs:
            self._sync_all(e)
        for o in self.owners:
            self.free.append(o.dsem)
            o.dsem = None
        self.owners = []
        self.free.sort()
        for b in self.all_bufs:
            b.w = None
            b.r = {}

    def finish(self):
        self._sync_all("sp")


def sb(nc, es, name, shape, dt):
    return es.enter_context(nc.sbuf_tensor(name, shape, dt))


def emit_H(nc, tr, es0, psb, layer, xsrc, wfm, wtm, pp, w2, cst, ysink, pfx):
    op, dma = tr.op, tr.dma
    NB, NT, NCH = T // 128, T // 512, T // 64

    es_c = contextlib.ExitStack()
    es0.enter_context(es_c)
    ident = sb(nc, es_c, pfx + "ident", [128, 128], F32)
    identb = sb(nc, es_c, pfx + "identb", [128, 128], BF16)
    cmask = sb(nc, es_c, pfx + "cmask", [128, 64], F32)
    m01 = sb(nc, es_c, pfx + "m01", [128, 128], F32)
    rmask = sb(nc, es_c, pfx + "rmask", [128, 512], F32)
    onesblk = sb(nc, es_c, pfx + "onesblk", [128, 128], F32)
    zeros = sb(nc, es_c, pfx + "zeros", [128, 512], F32)
    ppt = sb(nc, es_c, pfx + "ppt", [128, 8], F32)
    pp2 = sb(nc, es_c, pfx + "pp2", [128, 8], F32)
    w2f = sb(nc, es_c, pfx + "w2f", [128, 32], F32)
    w2b = sb(nc, es_c, pfx + "w2b", [128, 32], BF16)
    b_c = tr.buf("const")
    for tl, nm in ((ident, "ident"), (cmask, "cmask"), (m01, "m01"), (rmask, "rmask"),
                   (onesblk, "onesblk")):
        dma("sp", tl[:], cst[nm], b_c, writes=[b_c])
    dma("sp", ppt[:], pp, b_c, writes=[b_c])
    dma("sp", w2f[:], w2, b_c, writes=[b_c])
    b_c2 = tr.buf("const2")
    op("pool", lambda e: e.memset(zeros[:], 0.0), writes=[b_c2])
    op("pool", lambda e: e.memset(pp2[:], 0.0), writes=[b_c2])
    op("pool", lambda e: e.memset(pp2[:, 5:6], EPS), reads=[b_c2], writes=[b_c2])
    op("pool", lambda e: e.memset(pp2[:, 6:7], 1.0), reads=[b_c2], writes=[b_c2])
    op("dve", lambda e: e.tensor_copy(out=identb[:], in_=ident[:]), reads=[b_c], writes=[b_c2])
    op("dve", lambda e: e.tensor_copy(out=w2b[:], in_=w2f[:]), reads=[b_c], writes=[b_c2])
    if layer == 0:
        op("pool", lambda e: e.memset(pp2[:, 1:2], 1.0), reads=[b_c2], writes=[b_c2])
    else:
        op("dve", lambda e: e.tensor_tensor(out=pp2[:, 3:4], in0=ppt[:, 0:1], in1=ppt[:, 1:2],
                                            op=ALU.subtract), reads=[b_c, b_c2], writes=[b_c2])
        op("act", lambda e: e.activation(out=pp2[:, 4:5], in_=pp2[:, 3:4], func=AF.Exp),
           reads=[b_c2], writes=[b_c2])
        op("dve", lambda e: e.tensor_scalar(out=pp2[:, 4:5], in0=pp2[:, 4:5], scalar1=1.0,
                                            scalar2=None, op0=ALU.add), reads=[b_c2], writes=[b_c2])
        op("dve", lambda e: e.reciprocal(out=pp2[:, 0:1], in_=pp2[:, 4:5]), reads=[b_c2], writes=[b_c2])
        op("dve", lambda e: e.tensor_scalar(out=pp2[:, 1:2], in0=pp2[:, 0:1], scalar1=-1.0,
                                            scalar2=1.0, op0=ALU.mult, op1=ALU.add),
           reads=[b_c2], writes=[b_c2])
    op("dve", lambda e: e.tensor_scalar(out=pp2[:, 2:3], in0=ppt[:, 2:3], scalar1=-1.0,
                                        scalar2=None, op0=ALU.mult), reads=[b_c, b_c2], writes=[b_c2])
    CONST = [b_c, b_c2]


    S0res = sb(nc, es_c, pfx + "S0res", [128, T], BF16)
    S1res = sb(nc, es_c, pfx + "S1res", [128, T], BF16)
    vsb = sb(nc, es_c, pfx + "vsb", [128, NB, 64], BF16)
    b_S0q = tr.bufs(NT, "S0q")
    b_S0g = tr.bufs(NT, "S0g")
    b_S1 = tr.bufs(NT, "S1")
    b_vsb = tr.bufs(NB, "vsb")

    es_r = contextlib.ExitStack()
    vrec = sb(nc, es_r, pfx + "vrec", [128, NB, 128], BF16)
    qdec = sb(nc, es_r, pfx + "qdec", [128, T], BF16)
    kdec = sb(nc, es_r, pfx + "kdec", [128, T], BF16)
    kdsT = sb(nc, es_r, pfx + "kdsT", [128, NB, 96], BF16)
    RGres = sb(nc, es_r, pfx + "RGres", [128, T], BF16)
    U = sb(nc, es_r, pfx + "U", [128, T], F32)
    Dres = sb(nc, es_r, pfx + "Dres", [128, NCH], F32)
    b_vrec = tr.bufs(NB, "vrec")
    b_qdec = tr.bufs(NT, "qdec")
    b_kdec = tr.bufs(NT, "kdec")
    b_kdsT = tr.bufs(NT, "kdsT")
    b_RG = tr.bufs(NT, "RG")
    b_U = tr.buf("U")
    b_D = tr.buf("D")

    es_a = contextlib.ExitStack()
    xst = [sb(nc, es_a, pfx + f"xst{i}", [128, D], F32) for i in range(2)]
    b_xst = tr.bufs(2, "xst")
    xbf = xsrc[0] == "bf16"
    xstb = [xt_[:].bitcast(BF16) for xt_ in xst]
    xT = sb(nc, es_a, pfx + "xT", [128, 8, 512], BF16)
    b_xT = [tr.bufs(2, "xT") for _ in range(4)]
    wfmb = sb(nc, es_a, pfx + "wfmb", [128, 8, 640], BF16)
    wtmb = sb(nc, es_a, pfx + "wtmb", [128, 8, 192], BF16)
    b_w = tr.buf("w")
    NTMP = 9
    tmp = [sb(nc, es_a, pfx + f"tmp{i}", [128, 512], F32) for i in range(NTMP)]
    bh = tr.bufs(NTMP, "tmph")
    bc = tr.bufs(NTMP, "tmpc")
    kds = sb(nc, es_a, pfx + "kds", [128, 512], BF16)
    b_kds = tr.buf("kds")
    (t_sig, t_f, t_g, t_kk, t_q, t_cum, t_e1, t_e2, t_d) = range(NTMP)

    wfm_v = wfm.rearrange("(j p) c -> j p c", p=128)
    wtm_v = wtm.rearrange("(j p) c -> j p c", p=128)
    for j in range(8):
        s = j % 2
        dma("sp", xst[s][:, 0:640], wfm_v[j], b_xst[s], writes=[b_xst[s]])
        dma("sp", xst[s][:, 640:832], wtm_v[j], b_xst[s], writes=[b_xst[s]])
        if j % 2 == 0:
            op("dve", lambda e, j=j, s=s: e.tensor_copy(out=wfmb[:, j, :], in_=xst[s][:, 0:640]),
               reads=[b_xst[s]], writes=[b_w])
        else:
            op("act", lambda e, j=j, s=s: e.activation(out=wfmb[:, j, :], in_=xst[s][:, 0:640], func=AF.Copy),
               reads=[b_xst[s]], writes=[b_w])
        op("pool", lambda e, j=j, s=s: e.tensor_copy(out=wtmb[:, j, :], in_=xst[s][:, 640:832]),
           reads=[b_xst[s]], writes=[b_w])

    tp = [psb[0], psb[1]]
    fm = [psb[2], psb[3]]
    tmq = [psb[4], psb[5]]
    ktp = psb[6]
    ups = psb[7]
    ktp_bf = ktp[0][:].bitcast(BF16)

    xs_v = None if xbf else xsrc[1].rearrange("(n p) d -> n p d", p=128)
    tp_bf = [psb[0][0][:].bitcast(BF16), psb[1][0][:].bitcast(BF16)]
    evac_rr = [0]

    def evac_copy(out, in_, reads, writes):
        evac_rr[0] ^= 1
        if evac_rr[0]:
            op("act", lambda e: e.activation(out=out, in_=in_, func=AF.Copy), reads=reads, writes=writes)
        else:
            op("dve", lambda e: e.tensor_copy(out=out, in_=in_), reads=reads, writes=writes)

    fmi = [0]

    def fm_chunk(ci):
        bank = fm[fmi[0] % 2]
        fmi[0] += 1
        for j in range(8):
            op("pe", lambda e, j=j, bank=bank: e.matmul(bank[0][:, :], lhsT=wfmb[:, j, ci * 128:(ci + 1) * 128],
                                                        rhs=xT[:, j, :], start=(j == 0), stop=(j == 7)),
               reads=[b_w] + [b_xT[bi][j // 4] for bi in range(4)], writes=[bank[1]])
        return bank

    for ti in range(min(NT, _KNT)):
        cols = slice(ti * 512, (ti + 1) * 512)
        for bi in range(4):
            blk = ti * 4 + bi
            s = blk % 2
            if xbf:
                for hf, src_ap in enumerate(xsrc[1](blk)):
                    dma("sp", xstb[s][:, hf * 512:(hf + 1) * 512], src_ap, b_xst[s], reads=xsrc[2](blk),
                        writes=[b_xst[s]])
            else:
                dma("sp", xst[s][:], xs_v[blk], b_xst[s], writes=[b_xst[s]])
            for half in range(2):
                bank = tp[half]
                for jj in range(4):
                    j = half * 4 + jj
                    if xbf:
                        op("pe", lambda e: e.transpose(
                            tp_bf[half][:, jj * 128:(jj + 1) * 128], xstb[s][:, j * 128:(j + 1) * 128], identb[:]),
                           reads=[b_xst[s]] + CONST, writes=[bank[1]])
                    else:
                        op("pe", lambda e: e.transpose(
                            bank[0][:, jj * 128:(jj + 1) * 128], xst[s][:, j * 128:(j + 1) * 128], ident[:]),
                           reads=[b_xst[s]] + CONST, writes=[bank[1]])
                src_ = tp_bf[half][:, 0:512] if xbf else bank[0][:, :]
                evac_copy(xT[:, half * 4:(half + 1) * 4, bi * 128:(bi + 1) * 128],
                          src_.rearrange("p (j t) -> p j t", t=128),
                          [bank[1]], [b_xT[bi][half]])
            tmb = tmq[bi % 2]
            for j in range(8):
                op("pe", lambda e: e.matmul(
                    tmb[0][:, 0:192], lhsT=xT[:, j, bi * 128:(bi + 1) * 128], rhs=wtmb[:, j, :],
                    start=(j == 0), stop=(j == 7)),
                   reads=[b_w, b_xT[bi][j // 4]], writes=[tmb[1]])
            if bi % 2 == 0:
                op("act", lambda e: e.activation(out=vsb[:, blk, :], in_=tmb[0][:, 0:64], func=AF.Copy),
                   reads=[tmb[1]], writes=[b_vsb[blk]])
                op("act", lambda e: e.activation(out=vrec[:, blk, :], in_=tmb[0][:, 64:192], func=AF.Copy),
                   reads=[tmb[1]], writes=[b_vrec[blk]])
            else:
                op("dve", lambda e: e.tensor_copy(out=vsb[:, blk, :], in_=tmb[0][:, 0:64]),
                   reads=[tmb[1]], writes=[b_vsb[blk]])
                op("dve", lambda e: e.tensor_copy(out=vrec[:, blk, :], in_=tmb[0][:, 64:192]),
                   reads=[tmb[1]], writes=[b_vrec[blk]])

        bank = fm_chunk(0)
        op("dve", lambda e, bank=bank: e.tensor_copy(out=S0res[0:64, cols], in_=bank[0][0:64, :]),
           reads=[bank[1]], writes=[b_S0q[ti]])
        op("act", lambda e, bank=bank: e.activation(out=S0res[64:128, cols], in_=bank[0][64:128, :], func=AF.Silu),
           reads=[bank[1]], writes=[b_S0g[ti]])
        bank = fm_chunk(3)
        op("act", lambda e, bank=bank: e.activation(out=tmp[t_sig][0:64, :], in_=bank[0][0:64, :], func=AF.Sigmoid),
           reads=[bank[1]], writes=[bh[t_sig]])
        op("act", lambda e, bank=bank: e.activation(out=tmp[t_kk][64:96, :], in_=bank[0][64:96, :], func=AF.Copy),
           reads=[bank[1]], writes=[bc[t_kk]])
        op("dve", lambda e: e.tensor_scalar(out=tmp[t_f][0:64, :], in0=tmp[t_sig][0:64, :],
                                            scalar1=pp2[0:64, 1:2], scalar2=pp2[0:64, 0:1],
                                            op0=ALU.mult, op1=ALU.add),
           reads=[bh[t_sig]] + CONST, writes=[bh[t_f]])
        bank = fm_chunk(4)
        op("act", lambda e, bank=bank: e.activation(out=RGres[:, cols], in_=bank[0][:, :], func=AF.Silu),
           reads=[bank[1]], writes=[b_RG[ti]])
        bank = fm_chunk(1)
        op("dve", lambda e, bank=bank: e.tensor_copy(out=S1res[0:96, cols], in_=bank[0][0:96, :]),
           reads=[bank[1]], writes=[b_S1[ti]])
        gp = fm[fmi[0] % 2]
        fmi[0] += 1
        op("pe", lambda e: e.matmul(gp[0][64:96, :], lhsT=w2b[64:80, :], rhs=S1res[64:80, cols],
                                    start=True, stop=True),
           reads=[b_S1[ti]] + CONST, writes=[gp[1]])
        op("act", lambda e: e.activation(out=tmp[t_e1][64:96, :], in_=gp[0][64:96, :], func=AF.Exp,
                                         scale=-1.0, bias=pp2[64:96, 2:3]),
           reads=[gp[1]] + CONST, writes=[bc[t_e1]])
        op("act", lambda e: e.activation(out=tmp[t_e2][64:96, :], in_=tmp[t_e1][64:96, :], func=AF.Ln,
                                         bias=pp2[64:96, 6:7]),
           reads=[bc[t_e1]] + CONST, writes=[bc[t_e2]])
        op("act", lambda e: e.activation(out=tmp[t_g][64:96, :], in_=tmp[t_e2][64:96, :], func=AF.Copy,
                                         scale=-1.0 / 16.0),
           reads=[bc[t_e2]], writes=[bc[t_g]])
        op("act", lambda e: e.activation(out=tmp[t_g][0:64, :], in_=tmp[t_f][0:64, :], func=AF.Ln),
           reads=[bh[t_f]], writes=[bh[t_g]])
        op("pool", lambda e: e.tensor_scalar(out=tmp[t_kk][0:64, :], in0=tmp[t_f][0:64, :],
                                             scalar1=-1.0, scalar2=1.0, op0=ALU.mult, op1=ALU.add),
           reads=[bh[t_f]], writes=[bh[t_kk]])
        bank = fm_chunk(2)
        op("act", lambda e, bank=bank: e.activation(out=tmp[t_q][0:96, :], in_=bank[0][0:96, :], func=AF.Identity,
                                                    scale=ppt[0:96, 4:5]),
           reads=[bank[1]] + CONST, writes=[bh[t_q], bc[t_q]])
        op("dve", lambda e: e.tensor_tensor_scan(out=tmp[t_cum][0:96, :], data0=rmask[0:96, :],
                                                 data1=tmp[t_g][0:96, :], initial=0.0,
                                                 op0=ALU.mult, op1=ALU.add),
           reads=[bh[t_g], bc[t_g]] + CONST, writes=[bh[t_cum], bc[t_cum]])
        op("act", lambda e: e.activation(out=tmp[t_e1][0:96, :], in_=tmp[t_cum][0:96, :], func=AF.Exp),
           reads=[bh[t_cum], bc[t_cum]], writes=[bh[t_e1], bc[t_e1]])
        op("pool", lambda e: e.tensor_tensor(out=qdec[0:96, cols], in0=tmp[t_q][0:96, :],
                                             in1=tmp[t_e1][0:96, :], op=ALU.mult),
           reads=[bh[t_q], bc[t_q], bh[t_e1], bc[t_e1]], writes=[b_qdec[ti]])
        op("act", lambda e: e.activation(out=tmp[t_e2][0:96, :], in_=tmp[t_cum][0:96, :], func=AF.Exp,
                                         scale=-1.0),
           reads=[bh[t_cum], bc[t_cum]], writes=[bh[t_e2], bc[t_e2]])
        op("dve", lambda e: e.tensor_tensor(out=kdec[0:96, cols], in0=tmp[t_kk][0:96, :],
                                            in1=tmp[t_e2][0:96, :], op=ALU.mult),
           reads=[bh[t_kk], bc[t_kk], bh[t_e2], bc[t_e2]], writes=[b_kdec[ti]])
        cum3 = tmp[t_cum][0:96, :].rearrange("p (c t) -> p c t", t=64)
        d3 = tmp[t_d][0:96, :].rearrange("p (c t) -> p c t", t=64)
        op("pool", lambda e: e.tensor_tensor(out=d3, in0=cum3[:, :, 63:64].to_broadcast([96, 8, 64]),
                                             in1=cum3, op=ALU.subtract),
           reads=[bh[t_cum], bc[t_cum]], writes=[bh[t_d], bc[t_d]])
        op("act", lambda e: e.activation(out=tmp[t_d][0:96, :], in_=tmp[t_d][0:96, :], func=AF.Exp),
           reads=[bh[t_d], bc[t_d]], writes=[bh[t_d], bc[t_d]])
        op("pool", lambda e: e.tensor_tensor(out=kds[0:96, :], in0=tmp[t_kk][0:96, :],
                                             in1=tmp[t_d][0:96, :], op=ALU.mult),
           reads=[bh[t_kk], bc[t_kk], bh[t_d], bc[t_d]], writes=[b_kds])
        op("act", lambda e: e.activation(out=Dres[0:96, ti * 8:(ti + 1) * 8],
                                         in_=tmp[t_cum][0:96, 63:512:64], func=AF.Exp),
           reads=[bh[t_cum], bc[t_cum]], writes=[b_D])
        for bi in range(4):
            op("pe", lambda e, bi=bi: e.transpose(ktp_bf[:, bi * 96:(bi + 1) * 96],
                                                  kds[0:96, bi * 128:(bi + 1) * 128], identb[0:96, 0:96]),
               reads=[b_kds] + CONST, writes=[ktp[1]])
        evac_copy(kdsT[:, ti * 4:(ti + 1) * 4, :], ktp_bf[:, 0:384].rearrange("p (b c) -> p b c", c=96),
                  [ktp[1]], [b_kdsT[ti]])
        for cc in range(8):
            c = ti * 8 + cc
            blk, par = c // 2, c % 2
            rows = slice(64 * par, 64 * par + 64)
            pc = slice(cc * 64, (cc + 1) * 64)
            op("pe", lambda e, blk=blk, rows=rows, pc=pc: e.matmul(
                ups[0][0:64, pc], lhsT=kdsT[rows, blk, 0:64], rhs=vrec[rows, blk, 0:64], start=True, stop=True),
               reads=[b_kdsT[ti], b_vrec[blk]], writes=[ups[1]], serial=True)
            op("pe", lambda e, blk=blk, rows=rows, pc=pc: e.matmul(
                ups[0][64:96, pc], lhsT=kdsT[rows, blk, 64:96], rhs=vrec[rows, blk, 64:128], start=True, stop=True),
               reads=[b_kdsT[ti], b_vrec[blk]], writes=[ups[1]], serial=True)
        op("dve", lambda e: e.tensor_copy(
            out=U[0:96, :].rearrange("p (v c) -> p v c", c=128)[:, :, ti * 8:(ti + 1) * 8],
            in_=ups[0][0:96, :].rearrange("p (c v) -> p v c", v=64)),
           reads=[ups[1]], writes=[b_U])

    tr.barrier()
    es_a.close()
    if _STOP == "A":
        es_r.close()
        es_c.close()
        return
    es_b = contextlib.ExitStack()
    Sprev = sb(nc, es_b, pfx + "Sprev", [128, NCH * 64], BF16)
    b_Sp = tr.buf("Sprev")
    scm = [[sb(nc, es_b, pfx + f"scm{h}{i}", [128, 512], BF16) for i in range(2)] for h in range(2)]
    b_scm = [tr.bufs(2, "scm") for _ in range(2)]
    osb = [sb(nc, es_b, pfx + f"osb{i}", [128, 512], F32) for i in range(2)]
    b_osb = tr.bufs(2, "osb")
    sq = [sb(nc, es_b, pfx + f"sq{i}", [128, 512], F32) for i in range(2)]
    b_sq = tr.bufs(2, "sq")
    rstd = [sb(nc, es_b, pfx + f"rstd{i}", [128, 512], F32) for i in range(2)]
    b_rstd = tr.bufs(2, "rstd")
    yt = [sb(nc, es_b, pfx + f"yt{i}", [128, 512], BF16) for i in range(2)]
    b_yt = tr.bufs(2, "yt")

    op("pool", lambda e: e.memset(Sprev[:, 0:64], 0.0), writes=[b_Sp])
    Sp3 = Sprev[0:96, :].rearrange("p (c v) -> p c v", v=64)
    for v in range(64):
        op("dve", lambda e, v=v: e.tensor_tensor_scan(
            out=Sp3[:, 1:128, v], data0=Dres[0:96, 0:127], data1=U[0:96, v * 128:v * 128 + 127],
            initial=0.0, op0=ALU.mult, op1=ALU.add),
           reads=[b_U, b_D], writes=[b_Sp])

    scb = [psb[0], psb[1]]
    opsb = [psb[2], psb[3]]
    msb = [psb[4], psb[5]]
    for ti in range(NT if _KB > 0 else 0):
        cols = slice(ti * 512, (ti + 1) * 512)
        i2 = ti % 2
        for cc in range(8):
            c = ti * 8 + cc
            blk, par = c // 2, c % 2
            rows = slice(64 * par, 64 * par + 64)
            pc = slice(cc * 64, (cc + 1) * 64)
            tc_ = slice(c * 64, (c + 1) * 64)
            op("pe", lambda e, rows=rows, pc=pc, tc_=tc_: e.matmul(
                scb[0][0][rows, pc], lhsT=kdec[0:64, tc_], rhs=qdec[0:64, tc_], start=True, stop=True),
               reads=[b_kdec[ti], b_qdec[ti]], writes=[scb[0][1]])
            op("pe", lambda e, rows=rows, pc=pc, tc_=tc_: e.matmul(
                scb[1][0][rows, pc], lhsT=kdec[64:96, tc_], rhs=qdec[64:96, tc_], start=True, stop=True),
               reads=[b_kdec[ti], b_qdec[ti]], writes=[scb[1][1]])
        for h in range(2):
            for par in range(2):
                rows = slice(64 * par, 64 * par + 64)
                src = scb[h][0][rows, :].rearrange("p (c two t) -> p c two t", two=2, t=64)[:, :, par, :]
                dst = scm[h][i2][rows, :].rearrange("p (c two t) -> p c two t", two=2, t=64)[:, :, par, :]
                msk = cmask[rows, :].unsqueeze(1).to_broadcast([64, 4, 64])
                op("dve", lambda e, src=src, dst=dst, msk=msk: e.tensor_tensor(out=dst, in0=src, in1=msk, op=ALU.mult),
                   reads=[scb[h][1]] + CONST, writes=[b_scm[h][i2]])
        if _KB < 2:
            continue
        ob = opsb[i2]
        for cc in range(8):
            c = ti * 8 + cc
            blk, par = c // 2, c % 2
            rows = slice(64 * par, 64 * par + 64)
            pc = slice(cc * 64, (cc + 1) * 64)
            tc_ = slice(c * 64, (c + 1) * 64)
            sc_ = slice(c * 64, (c + 1) * 64)
            op("pe", lambda e, pc=pc, tc_=tc_, sc_=sc_: e.matmul(
                ob[0][0:64, pc], lhsT=Sprev[0:64, sc_], rhs=qdec[0:64, tc_], start=True, stop=False),
               reads=[b_Sp, b_qdec[ti]], writes=[ob[1]], serial=True)
            op("pe", lambda e, pc=pc, rows=rows, blk=blk: e.matmul(
                ob[0][0:64, pc], lhsT=vrec[rows, blk, 0:64], rhs=scm[0][i2][rows, pc], start=False, stop=True),
               reads=[b_vrec[blk], b_scm[0][i2]], writes=[ob[1]], serial=True)
            op("pe", lambda e, pc=pc, tc_=tc_, sc_=sc_: e.matmul(
                ob[0][64:128, pc], lhsT=Sprev[64:96, sc_], rhs=qdec[64:96, tc_], start=True, stop=False),
               reads=[b_Sp, b_qdec[ti]], writes=[ob[1]], serial=True)
            op("pe", lambda e, pc=pc, rows=rows, blk=blk: e.matmul(
                ob[0][64:128, pc], lhsT=vrec[rows, blk, 64:128], rhs=scm[1][i2][rows, pc], start=False, stop=True),
               reads=[b_vrec[blk], b_scm[1][i2]], writes=[ob[1]], serial=True)
        if _KB < 3:
            continue
        op("act", lambda e: e.activation(out=osb[i2][:, :], in_=ob[0][:, :], func=AF.Copy),
           reads=[ob[1]], writes=[b_osb[i2]])
        op("act", lambda e: e.activation(out=sq[i2][:, :], in_=ob[0][:, :], func=AF.Square),
           reads=[ob[1]], writes=[b_sq[i2]])
        mb = msb[i2]
        op("pe", lambda e: e.matmul(mb[0][:, :], lhsT=onesblk[:, :], rhs=sq[i2][:, :], start=True, stop=True),
           reads=[b_sq[i2]] + CONST, writes=[mb[1]])
        op("act", lambda e: e.activation(out=rstd[i2][:, :], in_=mb[0][:, :], func=AF.Sqrt, bias=pp2[:, 5:6]),
           reads=[mb[1]] + CONST, writes=[b_rstd[i2]])
        op("dve", lambda e: e.reciprocal(out=rstd[i2][:, :], in_=rstd[i2][:, :]),
           reads=[b_rstd[i2]], writes=[b_rstd[i2]])
        op("pool", lambda e: e.tensor_tensor(out=osb[i2][:, :], in0=osb[i2][:, :], in1=rstd[i2][:, :], op=ALU.mult),
           reads=[b_osb[i2], b_rstd[i2]], writes=[b_osb[i2]])
        op("dve", lambda e: e.scalar_tensor_tensor(out=yt[i2][:, :], in0=osb[i2][:, :], scalar=ppt[:, 3:4],
                                                   in1=RGres[:, cols], op0=ALU.mult, op1=ALU.mult),
           reads=[b_osb[i2], b_RG[ti]] + CONST, writes=[b_yt[i2]])
        ysink["bc"](ti, yt[i2], b_yt[i2])

    tr.barrier()
    es_b.close()
    es_r.close()
    if _STOP == "B":
        es_c.close()
        return
    if "after_bc" in ysink:
        ysink["after_bc"]()
    es_cw = contextlib.ExitStack()
    NR = 4
    G = 4
    NRG = 3
    GW = G * 512
    omg = [sb(nc, es_cw, pfx + f"omg{i}", [128, GW], F32) for i in range(NRG)]
    b_omg = [tr.bufs(G, "omg") for _ in range(NRG)]
    Pbg = [sb(nc, es_cw, pfx + f"Pbg{i}", [128, GW + 1], F32) for i in range(NRG)]
    b_Pbg = tr.bufs(NRG, "Pbg")
    NRW, NRT = 12, 5
    KC, KD = 6, 3
    wbf = [sb(nc, es_cw, pfx + f"wbf{i}", [128, 512], BF16) for i in range(NRW)]
    b_wbf = tr.bufs(NRW, "wbf")
    wT = [sb(nc, es_cw, pfx + f"wT{i}", [128, 512], BF16) for i in range(NRT)]
    b_wT = tr.bufs(NRT, "wT")
    ya = [sb(nc, es_cw, pfx + f"ya{i}", [128, 512], BF16) for i in range(2)]
    b_ya = tr.bufs(2, "ya")
    zps = [psb[0], psb[1], psb[2], psb[7]]
    wtp = [psb[3], psb[4]]
    wtp_bf = [w_[0][:].bitcast(BF16) for w_ in wtp]
    acc = [psb[5], psb[6]]

    items = []
    for tb in range(NB):
        hi = tb * 128 + 128
        first = True
        while hi > 0:
            lo = max(0, hi - 512)
            items.append(dict(tb=tb, lo=lo, hi=hi, W=hi - lo, first=first, last=(lo == 0)))
            hi = lo
            first = False
    n = len(items)
    groups, cur = [], []
    for g_, it_ in enumerate(items):
        if cur and (items[cur[0]]["tb"] != it_["tb"] or len(cur) == G):
            groups.append(cur)
            cur = []
        cur.append(g_)
    groups.append(cur)
    for gi_, grp_ in enumerate(groups):
        wt_ = sum(items[g_]["W"] for g_ in grp_)
        off_ = wt_
        for p_, g_ in enumerate(grp_):
            off_ -= items[g_]["W"]
            items[g_].update(grp=gi_, c0=off_, Wtot=wt_, glast=(p_ == len(grp_) - 1), gp=p_, members=grp_)

    def stA(g):
        it = items[g]
        s3 = g % NR
        tb, lo, hi, W = it["tb"], it["lo"], it["hi"], it["W"]
        t0 = tb * 128
        op("pe", lambda e: e.matmul(zps[s3][0][:, 0:W], lhsT=S0res[0:64, t0:t0 + 128], rhs=S1res[0:64, lo:hi],
                                    start=True, stop=True),
           reads=[b_S0q[tb // 4]] + [b_S1[k] for k in range(lo // 512, (hi - 1) // 512 + 1)],
           writes=[zps[s3][1]])

    def stB(g):
        it = items[g]
        s3 = g % NR
        W, c0, Wtot, gi, p = it["W"], it["c0"], it["Wtot"], it["grp"], it["gp"]
        sg = gi % NRG
        op("act", lambda e: e.activation(out=omg[sg][:, c0:c0 + W], in_=zps[s3][0][:, 0:W], func=AF.Sigmoid,
                                         scale=-0.125),
           reads=[zps[s3][1]], writes=[b_omg[sg][p]])
        if it["first"]:
            op("dve", lambda e: e.tensor_tensor(out=omg[sg][:, c0 + W - 128:c0 + W],
                                                in0=omg[sg][:, c0 + W - 128:c0 + W], in1=m01[:, :], op=ALU.max),
               reads=[b_omg[sg][p]] + CONST, writes=[b_omg[sg][p]])
            op("dve", lambda e: e.memset(Pbg[sg][:, Wtot:Wtot + 1], 1.0), writes=[b_Pbg[sg]])
        if not it["glast"]:
            return
        op("dve", lambda e: e.tensor_tensor_scan(out=Pbg[sg][:, 0:Wtot][:, ::-1], data0=omg[sg][:, 0:Wtot][:, ::-1],
                                                 data1=zeros[:, 0:1].to_broadcast([128, Wtot]),
                                                 initial=Pbg[sg][:, Wtot:Wtot + 1],
                                                 op0=ALU.mult, op1=ALU.add),
           reads=[b_omg[sg][q] for q in range(len(it["members"]))] + [b_Pbg[sg]] + CONST,
           writes=[b_Pbg[sg]])
        if not it["last"]:
            sgn = (gi + 1) % NRG
            Wn = items[g + 1]["Wtot"]
            op("dve", lambda e: e.tensor_copy(out=Pbg[sgn][:, Wn:Wn + 1], in_=Pbg[sg][:, 0:1]),
               reads=[b_Pbg[sg]], writes=[b_Pbg[sgn]])
        for q in it["members"]:
            Wq, cq = items[q]["W"], items[q]["c0"]
            sw = q % NRW
            op("pool", lambda e: e.tensor_tensor(out=wbf[sw][:, 0:Wq], in0=Pbg[sg][:, cq + 1:cq + Wq + 1],
                                                 in1=Pbg[sg][:, cq:cq + Wq], op=ALU.subtract),
               reads=[b_Pbg[sg]], writes=[b_wbf[sw]])

    def stC(g):
        it = items[g]
        sw = g % NRW
        st_ = g % NRT
        s2 = g % 2
        W = it["W"]
        for kbi in range(W // 128):
            op("pe", lambda e, kbi=kbi: e.transpose(wtp_bf[s2][:, kbi * 128:(kbi + 1) * 128],
                                                    wbf[sw][:, kbi * 128:(kbi + 1) * 128], identb[:, :]),
               reads=[b_wbf[sw]] + CONST, writes=[wtp[s2][1]])
        op("act", lambda e: e.activation(out=wT[st_][:, 0:W], in_=wtp_bf[s2][:, 0:W], func=AF.Copy),
           reads=[wtp[s2][1]], writes=[b_wT[st_]])

    def stD(g):
        it = items[g]
        s3 = g % NRT
        tb, lo, W = it["tb"], it["lo"], it["W"]
        ab = acc[(tb // 4) % 2]
        ac = slice((tb % 4) * 128, (tb % 4 + 1) * 128)
        nk = W // 128
        for kbi in range(nk):
            kb = lo // 128 + kbi
            op("pe", lambda e, kbi=kbi, kb=kb: e.matmul(
                ab[0][64:128, ac], lhsT=vsb[:, kb, :], rhs=wT[s3][:, kbi * 128:(kbi + 1) * 128],
                start=(it["first"] and kbi == 0), stop=(it["last"] and kbi == nk - 1)),
               reads=[b_vsb[kb], b_wT[s3]], writes=[ab[1]])
        if it["last"] and tb % 4 == 3:
            q4 = tb // 4
            i2 = q4 % 2
            cols = slice(q4 * 512, (q4 + 1) * 512)
            op("dve", lambda e: e.tensor_tensor(out=ya[i2][64:128, :], in0=ab[0][64:128, :],
                                                in1=S0res[64:128, cols], op=ALU.mult),
               reads=[ab[1], b_S0g[q4]], writes=[b_ya[i2]])
            ysink["a"](q4, ya[i2], b_ya[i2])

    for i in range(n + 1 + KC + KD):
        if i < n:
            stA(i)
        if 0 <= i - 1 < n:
            stB(i - 1)
        if 0 <= i - 1 - KC < n:
            stC(i - 1 - KC)
        if 0 <= i - 1 - KC - KD < n:
            stD(i - 1 - KC - KD)

    tr.barrier()
    es_cw.close()
    es_c.close()
    if "after_a" in ysink:
        ysink["after_a"]()


def emit_T(nc, tr, es0, psb, xt, xt_deps, yload, wg, wup, wout, lngb, cst, osink, pfx, use_pool=True):
    PE2 = "pool" if use_pool else "dve"
    op, dma = tr.op, tr.dma
    es_t = contextlib.ExitStack()
    es0.enter_context(es_t)
    ident = sb(nc, es_t, pfx + "ident", [128, 128], F32)
    lng = sb(nc, es_t, pfx + "lng", [128, D], F32)
    lnb = sb(nc, es_t, pfx + "lnb", [128, D], F32)
    b_c = tr.buf("tconst")
    dma("sp", ident[:], cst["ident"], b_c, writes=[b_c])
    dma("sp", lng[:], lngb[0:1, :].to_broadcast([128, D]), b_c, writes=[b_c])
    dma("sp", lnb[:], lngb[1:2, :].to_broadcast([128, D]), b_c, writes=[b_c])
    CONST = [b_c]
    wgb = sb(nc, es_t, pfx + "wgb", [128, 8, 3072], BF16)
    wupb = sb(nc, es_t, pfx + "wupb", [128, 6, D], BF16)
    woutb = sb(nc, es_t, pfx + "woutb", [128, 8, D], BF16)
    b_w = tr.buf("tw")
    stg = [sb(nc, es_t, pfx + f"stg{i}", [128, D], F32) for i in range(3)]
    b_stg = tr.bufs(3, "stg")
    cp_rr = [0]

    def cast_copy(out, in_, reads, writes):
        k = cp_rr[0] % 2
        cp_rr[0] += 1
        if k == 0:
            op("dve", lambda e: e.tensor_copy(out=out, in_=in_), reads=reads, writes=writes)
        elif k == 1:
            op("act", lambda e: e.activation(out=out, in_=in_, func=AF.Copy), reads=reads, writes=writes)
        else:
            op("pool", lambda e: e.tensor_copy(out=out, in_=in_), reads=reads, writes=writes)

    si = 0
    wg_v = wg.rearrange("(j p) c -> j p c", p=128)
    for j in range(8):
        for q in range(3):
            s = si % 3
            si += 1
            dma("sp", stg[s][:], wg_v[j][:, q * 1024:(q + 1) * 1024], b_stg[s], writes=[b_stg[s]])
            cast_copy(wgb[:, j, q * 1024:(q + 1) * 1024], stg[s][:], [b_stg[s]], [b_w])
    wup_v = wup.rearrange("(j p) c -> j p c", p=128)
    for j in range(6):
        s = si % 3
        si += 1
        dma("sp", stg[s][:], wup_v[j], b_stg[s], writes=[b_stg[s]])
        cast_copy(wupb[:, j, :], stg[s][:], [b_stg[s]], [b_w])
    wout_v = wout.rearrange("(j p) c -> j p c", p=128)
    for j in range(8):
        s = si % 3
        si += 1
        dma("sp", stg[s][:], wout_v[j], b_stg[s], writes=[b_stg[s]])
        cast_copy(woutb[:, j, :], stg[s][:], [b_stg[s]], [b_w])

    xres = sb(nc, es_t, pfx + "xres", [128, 4, D], F32)
    b_xres = tr.bufs(4, "xres")
    xT = sb(nc, es_t, pfx + "xT", [128, 8, 512], BF16)
    b_xT = [tr.bufs(2, "xT") for _ in range(4)]
    ybf = sb(nc, es_t, pfx + "ybf", [128, 6, 512], BF16)
    b_ybf = tr.bufs(6, "ybf")
    gsb = [sb(nc, es_t, pfx + f"gsb{i}", [128, 512], F32) for i in range(3)]
    b_gsb = tr.bufs(3, "gsb")
    tm_ = [sb(nc, es_t, pfx + f"tm{i}", [128, 512], F32) for i in range(2)]
    b_tm = tr.bufs(2, "tm")
    macc = [sb(nc, es_t, pfx + f"macc{i}", [128, 512], F32) for i in range(2)]
    b_macc = tr.bufs(2, "macc")
    mT = sb(nc, es_t, pfx + "mT", [128, 8, 512], BF16)
    b_mT = tr.bufs(8, "mT")
    rr = [sb(nc, es_t, pfx + f"rr{i}", [128, D], F32) for i in range(2)]
    b_rr = tr.bufs(2, "rr")
    sqj = [sb(nc, es_t, pfx + f"sqj{i}", [128, D], F32) for i in range(2)]
    b_sqj = tr.bufs(2, "sqj")
    st = [sb(nc, es_t, pfx + f"st{i}", [128, 8], F32) for i in range(2)]
    b_st = tr.bufs(2, "st")
    yo_ = [sb(nc, es_t, pfx + f"yo{i}", [128, D], F32) for i in range(2)]
    b_yo = tr.bufs(2, "yo")

    tp = [psb[0], psb[1]]
    gpb = [psb[2], psb[3], psb[4]]
    upb = [psb[5], psb[6]]
    outb = [psb[6], psb[7]]
    xt_v = xt.rearrange("(n p) d -> n p d", p=128)
    evr = [0]

    def evac_copy(out, in_, reads, writes):
        evr[0] ^= 1
        if evr[0]:
            op("act", lambda e: e.activation(out=out, in_=in_, func=AF.Copy), reads=reads, writes=writes)
        else:
            op("dve", lambda e: e.tensor_copy(out=out, in_=in_), reads=reads, writes=writes)

    gi = 0
    ui = 0
    for ti in range(TT // 512):
        cols = slice(ti * 512, (ti + 1) * 512)
        for bi in range(4):
            blk = ti * 4 + bi
            dma("sp", xres[:, bi, :], xt_v[blk], b_xres[bi], reads=xt_deps, writes=[b_xres[bi]])
            for half in range(2):
                bank = tp[half]
                for jj in range(4):
                    j = half * 4 + jj
                    op("pe", lambda e: e.transpose(bank[0][:, jj * 128:(jj + 1) * 128],
                                                   xres[:, bi, j * 128:(j + 1) * 128], ident[:]),
                       reads=[b_xres[bi]] + CONST, writes=[bank[1]])
                evac_copy(xT[:, half * 4:(half + 1) * 4, bi * 128:(bi + 1) * 128],
                          bank[0][:, :].rearrange("p (j t) -> p j t", t=128),
                          [bank[1]], [b_xT[bi][half]])
        for r in range(6):
            yload(r, ti, ybf[:, r, :], b_ybf[r])
        for dc in range(8):
            mi = dc % 2
            for nb_ in range(3):
                gb = gpb[gi % 3]
                gs = gi % 3
                gi += 1
                for j in range(8):
                    op("pe", lambda e: e.matmul(
                        gb[0][:, :], lhsT=wgb[:, j, nb_ * 1024 + dc * 128:nb_ * 1024 + (dc + 1) * 128],
                        rhs=xT[:, j, :], start=(j == 0), stop=(j == 7)),
                       reads=[b_w] + [b_xT[bi][j // 4] for bi in range(4)], writes=[gb[1]])
                op("act", lambda e: e.activation(out=gsb[gs][:, :], in_=gb[0][:, :], func=AF.Sigmoid),
                   reads=[gb[1]], writes=[b_gsb[gs]])
                ub = upb[ui % 2]
                ui += 1
                for r in range(2):
                    op("pe", lambda e: e.matmul(
                        ub[0][:, :], lhsT=wupb[:, nb_ * 2 + r, dc * 128:(dc + 1) * 128],
                        rhs=ybf[:, nb_ * 2 + r, :], start=(r == 0), stop=(r == 1)),
                       reads=[b_w, b_ybf[nb_ * 2 + r]], writes=[ub[1]])
                if nb_ == 0:
                    op("dve", lambda e: e.tensor_tensor(out=macc[mi][:, :], in0=gsb[gs][:, :], in1=ub[0][:, :],
                                                        op=ALU.mult),
                       reads=[b_gsb[gs], ub[1]], writes=[b_macc[mi]])
                else:
                    t2 = nb_ % 2
                    op("dve", lambda e: e.tensor_tensor(out=tm_[t2][:, :], in0=gsb[gs][:, :], in1=ub[0][:, :],
                                                        op=ALU.mult),
                       reads=[b_gsb[gs], ub[1]], writes=[b_tm[t2]])
                    if nb_ == 1:
                        op(PE2, lambda e: e.tensor_tensor(out=macc[mi][:, :], in0=macc[mi][:, :],
                                                             in1=tm_[t2][:, :], op=ALU.add),
                           reads=[b_macc[mi], b_tm[t2]], writes=[b_macc[mi]])
                    else:
                        op(PE2, lambda e: e.tensor_tensor(out=mT[:, dc, :], in0=macc[mi][:, :],
                                                             in1=tm_[t2][:, :], op=ALU.add),
                           reads=[b_macc[mi], b_tm[t2]], writes=[b_mT[dc]])
        for bi in range(4):
            blk = ti * 4 + bi
            i2 = blk % 2
            for hf in range(2):
                ob = outb[hf]
                for dc in range(8):
                    op("pe", lambda e: e.matmul(
                        ob[0][:, :], lhsT=mT[:, dc, bi * 128:(bi + 1) * 128],
                        rhs=woutb[:, dc, hf * 512:(hf + 1) * 512], start=(dc == 0), stop=(dc == 7)),
                       reads=[b_w, b_mT[dc]], writes=[ob[1]])
                op("dve", lambda e: e.scalar_tensor_tensor(
                    out=rr[i2][:, hf * 512:(hf + 1) * 512], in0=xres[:, bi, hf * 512:(hf + 1) * 512],
                    scalar=ALPHA, in1=ob[0][:, :], op0=ALU.mult, op1=ALU.add),
                   reads=[b_xres[bi], ob[1]], writes=[b_rr[i2]])
            op("dve", lambda e: e.reduce_sum(out=st[i2][:, 0:1], in_=rr[i2][:, :], axis=mybir.AxisListType.X),
               reads=[b_rr[i2]], writes=[b_st[i2]])
            op("dve", lambda e: e.tensor_scalar(out=st[i2][:, 1:2], in0=st[i2][:, 0:1], scalar1=-1.0 / D,
                                                scalar2=None, op0=ALU.mult),
               reads=[b_st[i2]], writes=[b_st[i2]])
            op("act", lambda e: e.activation(out=rr[i2][:, :], in_=rr[i2][:, :], func=AF.Identity,
                                             bias=st[i2][:, 1:2], scale=1.0),
               reads=[b_rr[i2], b_st[i2]], writes=[b_rr[i2]])
            op(PE2, lambda e: e.tensor_tensor(out=sqj[i2][:, :], in0=rr[i2][:, :], in1=rr[i2][:, :], op=ALU.mult),
               reads=[b_rr[i2]], writes=[b_sqj[i2]])
            op("dve", lambda e: e.reduce_sum(out=st[i2][:, 2:3], in_=sqj[i2][:, :], axis=mybir.AxisListType.X),
               reads=[b_sqj[i2], b_st[i2]], writes=[b_st[i2]])
            op("dve", lambda e: e.tensor_scalar(out=st[i2][:, 3:4], in0=st[i2][:, 2:3], scalar1=1.0 / D,
                                                scalar2=EPS, op0=ALU.mult, op1=ALU.add),
               reads=[b_st[i2]], writes=[b_st[i2]])
            op("act", lambda e: e.activation(out=st[i2][:, 5:6], in_=st[i2][:, 3:4], func=AF.Sqrt),
               reads=[b_st[i2]], writes=[b_st[i2]])
            op("dve", lambda e: e.reciprocal(out=st[i2][:, 4:5], in_=st[i2][:, 5:6]),
               reads=[b_st[i2]], writes=[b_st[i2]])
            op("dve", lambda e: e.scalar_tensor_tensor(out=yo_[i2][:, :], in0=rr[i2][:, :], scalar=st[i2][:, 4:5],
                                                       in1=lng[:, :], op0=ALU.mult, op1=ALU.mult),
               reads=[b_rr[i2], b_st[i2]] + CONST, writes=[b_yo[i2]])
            op(PE2, lambda e: e.tensor_tensor(out=yo_[i2][:, :], in0=yo_[i2][:, :], in1=lnb[:, :], op=ALU.add),
               reads=[b_yo[i2]] + CONST, writes=[b_yo[i2]])
            osink(blk, ti, bi, yo_[i2], b_yo[i2])
    tr.barrier()
    es_t.close()


def _consts():
    ident = np.eye(128, dtype=np.float32)
    p = np.arange(128)[:, None]
    cmask = ((p % 64) <= np.arange(64)[None, :]).astype(np.float32)
    m01 = (np.arange(128)[None, :] >= p).astype(np.float32)
    rmask = np.ones((128, 512), np.float32)
    rmask[:, ::64] = 0.0
    onesblk = np.zeros((128, 128), np.float32)
    onesblk[0:64, 0:64] = 1.0 / 64
    onesblk[64:128, 64:128] = 1.0 / 64
    return {"ident": ident, "cmask": cmask, "m01": m01, "rmask": rmask, "onesblk": onesblk}


_OFF = dict(a_q=0, a_k=256, a_v=512, a_g=768, h_f=1024, h_i=1280, h_q=1536, h_g=1792,
            c_q=2048, c_k=2176, c_v=2304, c_g=2560, c_r=2816, gates=2832)


def _h_weights(w_in_l, h):
    def col(name, width):
        o = _OFF[name] + h * width
        return w_in_l[:, o:o + width]
    z = lambda n: np.zeros((D, n), np.float32)
    wfm = np.concatenate([
        col("a_q", 64), col("a_g", 64),
        col("a_k", 64), w_in_l[:, _OFF["c_r"]:_OFF["c_r"] + 16], z(48),
        col("h_q", 64), col("c_q", 32), z(32),
        col("h_f", 64), col("c_k", 32), z(32),
        col("h_g", 64), col("c_g", 64),
    ], axis=1)
    wtm = np.concatenate([col("a_v", 64), col("h_i", 64), col("c_v", 64)], axis=1)
    return np.ascontiguousarray(wfm), np.ascontiguousarray(wtm)


def _h_params(inp, layer, h):
    pp = np.zeros((128, 8), np.float32)
    pp[0:64, 0] = inp["hgrn_lb_logits"][0, h * 64:(h + 1) * 64]
    pp[0:64, 1] = inp["hgrn_lb_logits"][1, h * 64:(h + 1) * 64]
    pp[64:96, 2] = inp["gla_gate_b"][layer, h * 32:(h + 1) * 32]
    pp[0:64, 3] = inp["hgrn_norm_g"][layer, h * 64:(h + 1) * 64]
    pp[64:128, 3] = inp["gla_norm_g"][layer, h * 64:(h + 1) * 64]
    pp[:, 4] = 1.0
    pp[64:96, 4] = 32.0 ** -0.5
    w2 = np.zeros((128, 32), np.float32)
    w2[64:80, :] = inp["gla_gate_w2"][layer][:, h * 32:(h + 1) * 32]
    return pp, w2


def _psum_banks(nc, tr, es):
    banks = []
    for i in range(8):
        t = es.enter_context(nc.psum_tensor(f"ps{i}", [128, 512], F32))
        banks.append((t, tr.buf("ps")))
    return banks


_CACHE = {}
HC = ("ident", "cmask", "m01", "rmask", "onesblk")
I32 = mybir.dt.int32
RG = [[0, 1, 2, 3], [4, 5, 6, 7]]


def build_fused():
    if "F" in _CACHE:
        return _CACHE["F"]
    nc = bass.Bass("TRN2", target_bir_lowering=False)
    ein = lambda name, shape: nc.dram_tensor(name, shape, F32, kind="ExternalInput").ap()
    xs = ein("xs", [T, D])
    xt = ein("xt", [TT, D])
    oh_in = ein("oh", [128, 8])
    cst = {k: ein("c_" + k, list(v.shape)) for k, v in _consts().items()}
    LW = []
    for l in range(2):
        LW.append(dict(wfm=ein(f"wfm{l}", [D, 640]), wtm=ein(f"wtm{l}", [D, 192]), pp=ein(f"pp{l}", [128, 8]),
                       w2=ein(f"w2{l}", [128, 32]), wg=ein(f"wg{l}", [D, 3072]), wup=ein(f"wup{l}", [768, D]),
                       wout=ein(f"wout{l}", [D, D]), lngb=ein(f"lngb{l}", [2, D])))
    xo = nc.dram_tensor("xo", [TT, D], F32, kind="ExternalOutput").ap()

    ybc_in = [nc.dram_tensor(f"ybc_in{q}", [512, 1024], F32) for q in range(4)]
    ya_in = [nc.dram_tensor(f"ya_in{q}", [256, 1024], F32) for q in range(4)]
    ybc_out = [nc.dram_tensor(f"ybc_out{q}", [512, 1024], F32) for q in range(4)]
    ya_out = [nc.dram_tensor(f"ya_out{q}", [256, 1024], F32) for q in range(4)]
    x1_in = [[nc.dram_tensor(f"x1_in{t}_{hf}", [2048, 256], F32) for hf in range(2)] for t in range(4)]
    x1_g = [[nc.dram_tensor(f"x1_g{t}_{hf}", [2048, 256], F32) for hf in range(2)] for t in range(4)]
    x1loc = nc.dram_tensor("x1loc", [TT, D], F32).ap()
    bf = lambda t: t.ap().bitcast(BF16)
    bf3 = lambda t: t.ap().bitcast(BF16).rearrange("(r p) c -> r p c", r=4)
    dyn = lambda v: v.rearrange("o p c -> (o p) c")

    with contextlib.ExitStack() as es:
        tr = Tr(nc, es)
        op, dma = tr.op, tr.dma
        psb = _psum_banks(nc, tr, es)

        b_ybc_in = tr.bufs(4, "ybc_in")
        b_ya_in = tr.bufs(4, "ya_in")
        b_ybc_out = tr.bufs(4, "ybc_out")
        b_ya_out = tr.bufs(4, "ya_out")
        b_x1_in = tr.bufs(4, "x1_in")
        b_x1_g = tr.bufs(4, "x1_g")
        b_x1loc = tr.buf("x1loc")

        oh = sb(nc, es, "oh_sb", [128, 8], F32)
        b_oh = tr.buf("oh")
        dma("sp", oh[:], oh_in, b_oh, writes=[b_oh])
        slot = [sb(nc, es, f"slot{i}", [128, 512], BF16) for i in range(4)]
        b_slot = tr.bufs(4, "slot")
        sl = [0]

        def scatter(tile_ap, nfree, dsts, reads, rows=slice(0, 128)):
            for j in range(4):
                k = sl[0] % 4
                sl[0] += 1
                op("act", lambda e: e.activation(out=slot[k][rows, 0:nfree], in_=tile_ap, func=AF.Identity,
                                                 scale=oh[rows, j:j + 1], bias=oh[rows, 4:5]),
                   reads=list(reads) + [b_oh], writes=[b_slot[k]])
                for (dram_ap, srows, dbuf) in dsts[j]:
                    dma("sp", dram_ap, slot[k][srows, 0:nfree], b_slot[k], reads=[b_slot[k]], writes=[dbuf])

        def run_layer(l):
            W = LW[l]

            def sink_bc(ti, tile, buf):
                q, c0 = ti // 4, (ti % 4) * 512
                v = bf(ybc_in[q])
                scatter(tile[:, :], 512,
                        [[(v[j * 64:(j + 1) * 64, c0:c0 + 512], slice(0, 64), b_ybc_in[q]),
                          (v[256 + j * 64:256 + (j + 1) * 64, c0:c0 + 512], slice(64, 128), b_ybc_in[q])]
                         for j in range(4)], [buf])

            def sink_a(q4, tile, buf):
                q, c0 = q4 // 4, (q4 % 4) * 512
                v = bf(ya_in[q])
                scatter(tile[64:128, :], 512,
                        [[(v[j * 64:(j + 1) * 64, c0:c0 + 512], slice(64, 128), b_ya_in[q])] for j in range(4)], [buf],
                        rows=slice(64, 128))

            def after_bc():
                for q in range(4):
                    tr.coll(lambda e: e.collective_compute(
                        "AllReduce", ALU.add, replica_groups=RG,
                        ins=[ybc_in[q].ap().opt()], outs=[ybc_out[q].ap().opt()]),
                        reads=[b_ybc_in[q]], writes=[b_ybc_out[q]])

            def after_a():
                for q in range(4):
                    tr.coll(lambda e: e.collective_compute(
                        "AllReduce", ALU.add, replica_groups=RG,
                        ins=[ya_in[q].ap().opt()], outs=[ya_out[q].ap().opt()]),
                        reads=[b_ya_in[q]], writes=[b_ya_out[q]])

            if l == 0:
                xsrc = ("f32", xs)
            else:
                def xblk(blk):
                    rk, t, bi = blk // 16, (blk // 4) % 4, blk % 4
                    return [bf(x1_g[t][hf])[rk * 512 + bi * 128:rk * 512 + (bi + 1) * 128, :] for hf in range(2)]
                xsrc = ("bf16", xblk, lambda blk: [b_x1_g[(blk // 4) % 4]])
            emit_H(nc, tr, es, psb, l, xsrc, W["wfm"], W["wtm"], W["pp"], W["w2"], cst,
                   dict(bc=sink_bc, a=sink_a, after_bc=after_bc, after_a=after_a), f"h{l}_")

            def yload(r6, ti, dst, buf):
                c0 = ti * 512
                for j in range(4):
                    if r6 < 2:
                        src = bf(ya_out[j])[r6 * 128:(r6 + 1) * 128, c0:c0 + 512]
                        sbuf_ = b_ya_out[j]
                    else:
                        src = bf(ybc_out[j])[(r6 - 2) * 128:(r6 - 1) * 128, c0:c0 + 512]
                        sbuf_ = b_ybc_out[j]
                    k = sl[0] % 4
                    sl[0] += 1
                    dma("sp", slot[k][:, 0:512], src, b_slot[k], reads=[sbuf_], writes=[b_slot[k]])
                    eng = "dve"
                    if j == 0:
                        op(eng, lambda e: e.tensor_scalar(out=dst, in0=slot[k][:, 0:512], scalar1=oh[:, 0:1],
                                                          scalar2=None, op0=ALU.mult),
                           reads=[b_slot[k], b_oh], writes=[buf])
                    else:
                        op("dve", lambda e: e.scalar_tensor_tensor(out=dst, in0=slot[k][:, 0:512], scalar=oh[:, j:j + 1],
                                                                 in1=dst, op0=ALU.mult, op1=ALU.add),
                           reads=[b_slot[k], b_oh, buf], writes=[buf])

            if l == 0:
                def osink(blk, ti, bi, tile, buf):
                    dma("sp", x1loc[blk * 128:(blk + 1) * 128, :], tile[:, :], buf, reads=[buf], writes=[b_x1loc])
                    for hf in range(2):
                        v = bf(x1_in[ti][hf])
                        scatter(tile[:, hf * 512:(hf + 1) * 512], 512,
                                [[(v[j * 512 + bi * 128:j * 512 + (bi + 1) * 128, :],
                                   slice(0, 128), b_x1_in[ti])] for j in range(4)], [buf])
                    if bi == 3:
                        for hf in range(2):
                            tr.coll(lambda e: e.collective_compute(
                                "AllReduce", ALU.add, replica_groups=RG,
                                ins=[x1_in[ti][hf].ap().opt()], outs=[x1_g[ti][hf].ap().opt()]),
                                reads=[b_x1_in[ti]], writes=[b_x1_g[ti]])
                emit_T(nc, tr, es, psb, xt, [], yload, W["wg"], W["wup"], W["wout"], W["lngb"], cst,
                       osink, f"t{l}_", use_pool=False)
            else:
                def osink(blk, ti, bi, tile, buf):
                    dma("sp", xo[blk * 128:(blk + 1) * 128, :], tile[:, :], buf, reads=[buf])
                emit_T(nc, tr, es, psb, x1loc, [b_x1loc], yload, W["wg"], W["wup"], W["wout"], W["lngb"], cst,
                       osink, f"t{l}_")

        for l in range(2):
            run_layer(l)
        tr.finish()
    _CACHE["F"] = nc
    return nc


def kernel(**inputs):
    inp = {k: np.asarray(v, dtype=np.float32) for k, v in inputs.items()}
    nc = build_fused()
    c = _consts()
    x = inp["x"]
    xf = x.reshape(2 * T, D)
    shared = {"c_" + k: v for k, v in c.items()}
    for l in range(2):
        shared[f"wg{l}"] = np.ascontiguousarray(inp["w_in"][l][:, _OFF["gates"]:_OFF["gates"] + 3072])
        shared[f"wup{l}"] = np.ascontiguousarray(inp["w_up"][l].reshape(768, D))
        shared[f"wout{l}"] = np.ascontiguousarray(inp["w_out"][l])
        shared[f"lngb{l}"] = np.stack([inp["ln_g"][l], inp["ln_b"][l]]).astype(np.float32)
    maps = []
    for core in range(NCORE):
        b, h = core // 4, core % 4
        m = dict(shared)
        m["xs"] = np.ascontiguousarray(x[b])
        m["xt"] = np.ascontiguousarray(xf[core * TT:(core + 1) * TT])
        ohm = np.zeros((128, 8), np.float32)
        ohm[:, h] = 1.0
        m["oh"] = ohm
        for l in range(2):
            wfm, wtm = _h_weights(inp["w_in"][l], h)
            pp, w2 = _h_params(inp, l, h)
            m[f"wfm{l}"], m[f"wtm{l}"], m[f"pp{l}"], m[f"w2{l}"] = wfm, wtm, pp, w2
        maps.append(m)
    res = run_bass_kernel_spmd(nc, maps, core_ids=list(range(NCORE)))
    out = np.concatenate([res.results[core]["xo"] for core in range(NCORE)], axis=0)
    return out.reshape(2, T, D).astype(np.float32)
```
